# Optimizing a Trainium2 kernel written in Bass

```python
import jax, jax.numpy as jnp
from jax import lax
import numpy as np

D_MODEL = 2048
BATCH = 1
SEQ = 8192
DEPTH = 4

D_FF = 5632
N_EVEN = (DEPTH + 1) // 2
N_ODD = DEPTH // 2
GLA_HEADS = 4
GLA_DK = 128
GLA_DV = 256
GLA_QK = GLA_HEADS * GLA_DK
GLA_V = GLA_HEADS * GLA_DV
GLA_RANK = 16
GLA_GATE_TAU = 16.0
GLA_CHUNK = 64
CONV_DIM = D_MODEL // 2
CONV_WIDTH = 31
AB_SPLITS = (GLA_QK, 2 * GLA_QK, 2 * GLA_QK + GLA_V, 2 * GLA_QK + 2 * GLA_V,
             2 * GLA_QK + 2 * GLA_V + GLA_RANK)
AB_IN = 2 * GLA_QK + 2 * GLA_V + GLA_RANK + 2 * CONV_DIM
AB_MIX = GLA_V + CONV_DIM
FOX_HEADS = 16
FOX_DH = 128
FOX_D = FOX_HEADS * FOX_DH
FOX_IN = 4 * FOX_D + FOX_HEADS
FOX_BLOCK = 128
EPS = 1e-6

kernel_name = "hybrid_gla_conv_fox_macaron"


def rms_norm(x, g):
    xf = x.astype(jnp.float32)
    y = xf * lax.rsqrt(jnp.mean(xf * xf, axis=-1, keepdims=True) + EPS)
    return (y * g.astype(jnp.float32)).astype(x.dtype)


def layer_norm(x, g, b):
    xf = x.astype(jnp.float32)
    mu = jnp.mean(xf, axis=-1, keepdims=True)
    xc = xf - mu
    y = xc * lax.rsqrt(jnp.mean(xc * xc, axis=-1, keepdims=True) + EPS)
    return (y * g.astype(jnp.float32) + b.astype(jnp.float32)).astype(x.dtype)


def swiglu(x, w_gate, w_up, w_down):
    return (jax.nn.silu(x @ w_gate) * (x @ w_up)) @ w_down


def gla_chunked(q, k, v, log_a):
    b_, t_, h_, dk = q.shape
    dv = v.shape[-1]
    nc = t_ // GLA_CHUNK
    f32 = jnp.float32
    q = q.astype(f32).reshape(b_, nc, GLA_CHUNK, h_, dk) * (dk ** -0.5)
    k = k.astype(f32).reshape(b_, nc, GLA_CHUNK, h_, dk)
    v = v.astype(f32).reshape(b_, nc, GLA_CHUNK, h_, dv)
    cum = jnp.cumsum(log_a.astype(f32).reshape(b_, nc, GLA_CHUNK, h_, dk), axis=2)
    ref = cum[:, :, GLA_CHUNK // 2:GLA_CHUNK // 2 + 1]
    last = cum[:, :, -1:]
    a = jnp.einsum('bnchd,bnshd->bnhcs', q * jnp.exp(cum - ref), k * jnp.exp(ref - cum))
    causal = jnp.tril(jnp.ones((GLA_CHUNK, GLA_CHUNK), dtype=bool))
    a = jnp.where(causal, a, 0.0)
    o_intra = jnp.einsum('bnhcs,bnshv->bnchv', a, v)
    q_in = q * jnp.exp(cum)
    k_out = k * jnp.exp(last - cum)
    decay = jnp.exp(last[:, :, 0])

    def step(state, inp):
        qn, kn, vn, dn = inp
        o = jnp.einsum('bchd,bhdv->bchv', qn, state)
        state = state * dn[..., None] + jnp.einsum('bchd,bchv->bhdv', kn, vn)
        return state, o

    s0 = jnp.zeros((b_, h_, dk, dv), f32)
    xs = (jnp.moveaxis(q_in, 1, 0), jnp.moveaxis(k_out, 1, 0),
          jnp.moveaxis(v, 1, 0), jnp.moveaxis(decay, 1, 0))
    _, o_inter = lax.scan(step, s0, xs)
    o = o_intra + jnp.moveaxis(o_inter, 0, 1)
    return o.reshape(b_, t_, h_, dv)


def mixer_gla_conv(u, w_in, w_gk2, b_gk, gla_g, conv_w, conv_b, ln_g, ln_b, w_out):
    b_, t_, _ = u.shape
    z = u @ w_in
    q, k, v, g, gk_lr, conv_in = jnp.split(z, AB_SPLITS, axis=-1)
    log_a = jax.nn.log_sigmoid((gk_lr @ w_gk2 + b_gk).astype(jnp.float32)) / GLA_GATE_TAU
    hs = lambda a_, d_: a_.reshape(b_, t_, GLA_HEADS, d_)
    o = gla_chunked(hs(q, GLA_DK), hs(k, GLA_DK), hs(v, GLA_DV), hs(log_a, GLA_DK))
    o = rms_norm(o, gla_g).astype(u.dtype)
    o_gla = o.reshape(b_, t_, GLA_V) * jax.nn.silu(g)
    val, gate = jnp.split(conv_in, 2, axis=-1)
    c = val * jax.nn.sigmoid(gate)
    c = lax.conv_general_dilated(c, conv_w[:, None, :], window_strides=(1,),
                                 padding=[(CONV_WIDTH - 1, 0)],
                                 dimension_numbers=('NWC', 'WIO', 'NWC'),
                                 feature_group_count=CONV_DIM) + conv_b
    c = jax.nn.silu(layer_norm(c, ln_g, ln_b))
    return jnp.concatenate([o_gla, c], axis=-1) @ w_out


def mixer_fox(u, w_in, b_f, q_g, k_g, w_out):
    b_, t_, _ = u.shape
    f32 = jnp.float32
    z = u @ w_in
    q, k, v, og, f_logit = jnp.split(z, (FOX_D, 2 * FOX_D, 3 * FOX_D, 4 * FOX_D), axis=-1)
    heads = lambda a_: a_.reshape(b_, t_, FOX_HEADS, FOX_DH).transpose(0, 2, 1, 3)
    q = rms_norm(heads(q), q_g)
    k = rms_norm(heads(k), k_g)
    v = heads(v)
    log_f = jax.nn.log_sigmoid((f_logit + b_f).astype(f32))
    cum = jnp.cumsum(log_f, axis=1).transpose(0, 2, 1)
    nb = t_ // FOX_BLOCK
    qb = q.reshape(b_, FOX_HEADS, nb, FOX_BLOCK, FOX_DH).transpose(2, 0, 1, 3, 4)
    cb = cum.reshape(b_, FOX_HEADS, nb, FOX_BLOCK).transpose(2, 0, 1, 3)
    kpos = jnp.arange(t_)
    scale = FOX_DH ** -0.5

    def block(args):
        qi, ci, i = args
        s = jnp.einsum('bhqd,bhkd->bhqk', qi, k).astype(f32) * scale
        s = s + ci[..., None] - cum[:, :, None, :]
        qpos = i * FOX_BLOCK + jnp.arange(FOX_BLOCK)
        s = jnp.where(kpos[None, :] <= qpos[:, None], s, -jnp.inf)
        p = jax.nn.softmax(s, axis=-1).astype(v.dtype)
        return jnp.einsum('bhqk,bhkd->bhqd', p, v)

    o = lax.map(block, (qb, cb, jnp.arange(nb)))
    o = o.transpose(1, 0, 3, 2, 4).reshape(b_, t_, FOX_D)
    return (o * jax.nn.sigmoid(og)) @ w_out


def setup_inputs(seed: int = 0) -> dict:
    key = jax.random.key(seed)
    ks = iter(jax.random.split(key, 32))
    f32 = jnp.float32

    def w(shape, fan_in):
        return jax.random.normal(next(ks), shape, f32) * (fan_in ** -0.5)

    def gain(shape):
        return 1.0 + 0.02 * jax.random.normal(next(ks), shape, f32)

    def bias(shape, scale=0.02):
        return scale * jax.random.normal(next(ks), shape, f32)

    L, E, O = DEPTH, N_EVEN, N_ODD
    return {
        "x": jax.random.normal(next(ks), (BATCH, SEQ, D_MODEL), f32),
        "ffn1_norm": gain((L, D_MODEL)),
        "ffn1_gate": w((L, D_MODEL, D_FF), D_MODEL),
        "ffn1_up": w((L, D_MODEL, D_FF), D_MODEL),
        "ffn1_down": w((L, D_FF, D_MODEL), D_FF),
        "mix_norm": gain((L, D_MODEL)),
        "ffn2_norm": gain((L, D_MODEL)),
        "ffn2_gate": w((L, D_MODEL, D_FF), D_MODEL),
        "ffn2_up": w((L, D_MODEL, D_FF), D_MODEL),
        "ffn2_down": w((L, D_FF, D_MODEL), D_FF),
        "ab_w_in": w((E, D_MODEL, AB_IN), D_MODEL),
        "gla_w_gk2": w((E, GLA_RANK, GLA_QK), GLA_RANK),
        "gla_b_gk": bias((E, GLA_QK), 0.1),
        "gla_out_norm": gain((E, GLA_DV)),
        "conv_w": w((E, CONV_WIDTH, CONV_DIM), CONV_WIDTH),
        "conv_b": bias((E, CONV_DIM)),
        "conv_ln_g": gain((E, CONV_DIM)),
        "conv_ln_b": bias((E, CONV_DIM)),
        "ab_w_out": w((E, AB_MIX, D_MODEL), AB_MIX),
        "fox_w_in": w((O, D_MODEL, FOX_IN), D_MODEL),
        "fox_b_f": 3.0 + 0.5 * jax.random.normal(next(ks), (O, FOX_HEADS), f32),
        "fox_q_norm": gain((O, FOX_DH)),
        "fox_k_norm": gain((O, FOX_DH)),
        "fox_w_out": w((O, FOX_D, D_MODEL), FOX_D),
    }


def reference(x, ffn1_norm, ffn1_gate, ffn1_up, ffn1_down, mix_norm, ffn2_norm, ffn2_gate, ffn2_up,
              ffn2_down, ab_w_in, gla_w_gk2, gla_b_gk, gla_out_norm, conv_w, conv_b, conv_ln_g,
              conv_ln_b, ab_w_out, fox_w_in, fox_b_f, fox_q_norm, fox_k_norm, fox_w_out):
    h = x
    for layer in range(DEPTH):
        h = h + 0.5 * swiglu(rms_norm(h, ffn1_norm[layer]), ffn1_gate[layer], ffn1_up[layer], ffn1_down[layer])
        u = rms_norm(h, mix_norm[layer])
        if layer % 2 == 0:
            e = layer // 2
            h = h + mixer_gla_conv(u, ab_w_in[e], gla_w_gk2[e], gla_b_gk[e], gla_out_norm[e], conv_w[e],
                                   conv_b[e], conv_ln_g[e], conv_ln_b[e], ab_w_out[e])
        else:
            o = layer // 2
            h = h + mixer_fox(u, fox_w_in[o], fox_b_f[o], fox_q_norm[o], fox_k_norm[o], fox_w_out[o])
        h = h + 0.5 * swiglu(rms_norm(h, ffn2_norm[layer]), ffn2_gate[layer], ffn2_up[layer], ffn2_down[layer])
    return h
```

```python
import contextlib
import numpy as np
import concourse.bass as bass
import concourse.mybir as mybir
from concourse.bass_utils import run_bass_kernel_spmd

F32 = mybir.dt.float32
BF16 = mybir.dt.bfloat16
AF = mybir.ActivationFunctionType
ALU = mybir.AluOpType
AX = mybir.AxisListType

NCORES = 8
D = 2048
KC = D // 128
T = 1024
TB = 512
NTB = T // TB
DFF = 5632
NF = DFF // 128
FR = 4
NR = NF // FR
EPS = 1e-6


class Sched:
    ENGS = ("tensor", "vector", "scalar", "gpsimd", "sync")

    def __init__(self, nc, stack):
        self.nc = nc
        self.stack = stack
        self.ops = {e: [] for e in self.ENGS}
        self.semh = {}
        self.cnt = {}
        for e in self.ENGS:
            self.semh[e] = stack.enter_context(nc.semaphore(f"s_{e}"))
            self.cnt[e] = 0
        self.waited = {e: {} for e in self.ENGS}
        self.res = {}
        self.nwaits = 0

    def _deps(self, eng, reads, writes):
        need = {}

        def add(tok):
            k, v = tok
            if k == eng and v > self.cnt[eng]:
                return
            if k not in self.ENGS:
                v = self.cnt[k]
            if need.get(k, 0) < v:
                need[k] = v

        for r in reads:
            st = self.res.get(r)
            if st is not None and st[0] is not None:
                add(st[0])
        for w in writes:
            st = self.res.get(w)
            if st is not None:
                if st[0] is not None:
                    add(st[0])
                for k, v in st[1].items():
                    add((k, v))
        wd = self.waited[eng]
        for k, v in need.items():
            if wd.get(k, 0) < v:
                wd[k] = v
                h = self.semh[k]
                self.nwaits += 1
                self.ops[eng].append(lambda e, h=h, v=v: e.wait_ge(h, v))

    def _mark(self, tok, reads, writes):
        k, v = tok
        for r in reads:
            st = self.res.get(r)
            if st is None:
                st = self.res[r] = [None, {}]
            if st[1].get(k, 0) < v:
                st[1][k] = v
        for w in writes:
            self.res[w] = [tok, {}]

    def op(self, eng, fn, reads=(), writes=(), inc=True):
        self._deps(eng, reads, writes)
        if inc:
            self.cnt[eng] += 1
            tok = (eng, self.cnt[eng])
            h = self.semh[eng]
            self.ops[eng].append(lambda e, fn=fn, h=h: fn(e).then_inc(h, 1))
        else:
            assert eng == "tensor"
            tok = (eng, self.cnt[eng] + 1)
            self.ops[eng].append(lambda e, fn=fn: fn(e))
        self._mark(tok, reads, writes)
        return tok

    def dma(self, eng, out, in_, semkey, reads=(), writes=(), **kw):
        self._deps(eng, reads, writes)
        if semkey not in self.semh:
            self.semh[semkey] = self.stack.enter_context(self.nc.semaphore(f"d_{semkey}"))
            self.cnt[semkey] = 0
        self.cnt[semkey] += 16
        tok = (semkey, self.cnt[semkey])
        h = self.semh[semkey]
        self.ops[eng].append(
            lambda e, out=out, in_=in_, h=h, kw=kw: e.dma_start(out=out, in_=in_, **kw).then_inc(h, 16))
        self._mark(tok, reads, writes)
        return tok

    def wait_all(self, eng, keys):
        self._deps(eng, list(keys), list(keys))

    def emit(self):
        with self.nc.Block() as block:
            @block.tensor
            def _(e):
                for f in self.ops["tensor"]:
                    f(e)

            @block.vector
            def _(e):
                for f in self.ops["vector"]:
                    f(e)

            @block.scalar
            def _(e):
                for f in self.ops["scalar"]:
                    f(e)

            @block.gpsimd
            def _(e):
                for f in self.ops["gpsimd"]:
                    f(e)

            @block.sync
            def _(e):
                for f in self.ops["sync"]:
                    f(e)


class Ctx:
    def __init__(self, nc, stack):
        self.nc = nc
        self.stack = stack
        self.S = Sched(nc, stack)
        self.ntmp = 0

    def sb(self, name, shape, dt):
        return self.stack.enter_context(self.nc.sbuf_tensor(name, list(shape), dt))

    def ps(self, name, shape, dt=F32):
        return self.stack.enter_context(self.nc.psum_tensor(name, list(shape), dt))


def emit_rmsnorm(C, hT, xnT, gcol, ones_bf, sq, pss, rstd, tag):
    S = C.S
    n = 0
    for tb in range(NTB):
        ts = slice(tb * TB, (tb + 1) * TB)
        for kc in range(KC):
            s = sq[n % len(sq)]
            rs = ("sq", n % len(sq))
            n += 1
            S.op("scalar", lambda e, s=s, kc=kc, ts=ts: e.activation(
                out=s[:], in_=hT[:, kc, ts], func=AF.Square),
                reads=[("hT", kc, tb)], writes=[rs])
            S.op("tensor", lambda e, s=s, kc=kc, tb=tb: e.matmul(
                pss[tb][:], ones_bf[:], s[:], start=(kc == 0), stop=(kc == KC - 1)),
                reads=[rs], writes=[("pss", tb)], inc=(kc == KC - 1))
        S.op("vector", lambda e, tb=tb, ts=ts: e.tensor_scalar(
            out=rstd[:, ts], in0=pss[tb][:], scalar1=1.0 / D, scalar2=EPS,
            op0=ALU.mult, op1=ALU.add),
            reads=[("pss", tb)], writes=[("rstd", tb)])
        S.op("vector", lambda e, ts=ts: e.tensor_scalar(
            out=rstd[:, ts], in0=rstd[:, ts], scalar1=-0.5, scalar2=None,
            op0=ALU.pow),
            reads=[("rstd", tb)], writes=[("rstd", tb)])
        for kc in range(KC):
            S.op("vector", lambda e, kc=kc, ts=ts: e.scalar_tensor_tensor(
                out=xnT[:, kc, ts], in0=hT[:, kc, ts], scalar=gcol(kc),
                in1=rstd[:, ts], op0=ALU.mult, op1=ALU.mult),
                reads=[("hT", kc, tb), ("rstd", tb), "consts"], writes=[("xn", kc, tb)])


def emit_ffn(C, hT, xnT, wg_d, wu_d, wd_d, bufs, tag):
    S = C.S
    wgb, wub, wdb, h1, sg, pg, pu, py = (bufs[k] for k in
                                         ("wg", "wu", "wd", "h1", "sg", "pg", "pu", "py"))
    ny = 0
    for r in range(NR):
        rb = r % 2
        for fi in range(FR):
            f = r * FR + fi
            wb = f % 2
            S.dma("gpsimd", wgb[wb][:], wg_d[f], f"wg{wb}", writes=[("wg", wb)])
            S.dma("gpsimd", wub[wb][:], wu_d[f], f"wu{wb}", writes=[("wu", wb)])
            S.dma("gpsimd", wdb[rb][:, fi, :], wd_d[f], f"wd{rb}", writes=[("wd", rb, fi)])
            for tb in range(NTB):
                ts = slice(tb * TB, (tb + 1) * TB)
                for kc in range(KC):
                    S.op("tensor", lambda e, wb=wb, kc=kc, ts=ts, tb=tb: e.matmul(
                        pg[tb][:], wgb[wb][:, kc * 128:(kc + 1) * 128], xnT[:, kc, ts],
                        start=(kc == 0), stop=(kc == KC - 1)),
                        reads=[("wg", wb), ("xn", kc, tb)], writes=[("pg", tb)],
                        inc=(kc == KC - 1))
                for kc in range(KC):
                    S.op("tensor", lambda e, wb=wb, kc=kc, ts=ts, tb=tb: e.matmul(
                        pu[tb][:], wub[wb][:, kc * 128:(kc + 1) * 128], xnT[:, kc, ts],
                        start=(kc == 0), stop=(kc == KC - 1)),
                        reads=[("wu", wb), ("xn", kc, tb)], writes=[("pu", tb)],
                        inc=(kc == KC - 1))
                S.op("scalar", lambda e, tb=tb: e.activation(
                    out=sg[tb][:], in_=pg[tb][:], func=AF.Silu),
                    reads=[("pg", tb)], writes=[("sg", tb)])
                S.op("vector", lambda e, tb=tb, rb=rb, fi=fi, ts=ts: e.tensor_tensor(
                    out=h1[rb][:, fi, ts], in0=sg[tb][:], in1=pu[tb][:], op=ALU.mult),
                    reads=[("sg", tb), ("pu", tb)], writes=[("h1", rb, fi, tb)])
        for d in range(KC):
            for tb in range(NTB):
                ts = slice(tb * TB, (tb + 1) * TB)
                yb = ny % len(py)
                ny += 1
                for fi in range(FR):
                    S.op("tensor", lambda e, yb=yb, rb=rb, fi=fi, d=d, ts=ts: e.matmul(
                        py[yb][:], wdb[rb][:, fi, d * 128:(d + 1) * 128], h1[rb][:, fi, ts],
                        start=(fi == 0), stop=(fi == FR - 1)),
                        reads=[("wd", rb, fi), ("h1", rb, fi, tb)], writes=[("py", yb)],
                        inc=(fi == FR - 1))
                S.op("vector", lambda e, yb=yb, d=d, ts=ts: e.scalar_tensor_tensor(
                    out=hT[:, d, ts], in0=py[yb][:], scalar=0.5, in1=hT[:, d, ts],
                    op0=ALU.mult, op1=ALU.add),
                    reads=[("py", yb), ("hT", d, tb)], writes=[("hT", d, tb)])


def new_prog():
    nc = bass.Bass("TRN2", target_bir_lowering=False)
    return nc


def emit_norm(C, hT, xnT, gcol, ones_bf, epsc, sq, pss, rstd, psres, dim=D, nkc=KC, inkey="hT"):
    S = C.S
    n = 0
    for tb in range(NTB):
        ts = slice(tb * TB, (tb + 1) * TB)
        for kc in range(nkc):
            s = sq[n % len(sq)]
            rs = ("sq", n % len(sq))
            n += 1
            S.op("scalar", lambda e, s=s, kc=kc, ts=ts: e.activation(
                out=s[:], in_=hT[:, kc, ts], func=AF.Square),
                reads=[(inkey, kc, tb)], writes=[rs])
            S.op("tensor", lambda e, s=s, kc=kc, tb=tb: e.matmul(
                pss[tb][:], ones_bf[:], s[:], start=(kc == 0), stop=(kc == nkc - 1)),
                reads=[rs, "ones"], writes=[psres(tb)], inc=True)
        S.op("scalar", lambda e, tb=tb, ts=ts: e.activation(
            out=rstd[:, ts], in_=pss[tb][:], func=AF.Sqrt, scale=1.0 / dim, bias=epsc[:]),
            reads=[psres(tb), "ones"], writes=[("rstd", tb)])
        S.op("vector", lambda e, ts=ts: e.reciprocal(out=rstd[:, ts], in_=rstd[:, ts]),
             reads=[("rstd", tb)], writes=[("rstd", tb)])
        for kc in range(nkc):
            S.op("vector", lambda e, kc=kc, ts=ts: e.scalar_tensor_tensor(
                out=xnT[:, kc, ts], in0=hT[:, kc, ts], scalar=gcol(kc),
                in1=rstd[:, ts], op0=ALU.mult, op1=ALU.mult),
                reads=[(inkey, kc, tb), ("rstd", tb), "consts"], writes=[("xn", kc, tb)])


def emit_proj_resid(C, hT, xT, w_d, nm, wbuf, py, xkey, nk=KC, wtag="wo"):
    S = C.S
    ny = 0
    for m in range(nm):
        wb = m % 2
        S.dma("gpsimd", wbuf[wb][:, 0:nk * 128], w_d[m], f"{wtag}{wb}", writes=[(wtag, wb)])
        for tb in range(NTB):
            ts = slice(tb * TB, (tb + 1) * TB)
            yb = ny % len(py)
            ny += 1
            for k in range(nk):
                S.op("tensor", lambda e, yb=yb, wb=wb, k=k, ts=ts: e.matmul(
                    py[yb][:], wbuf[wb][:, k * 128:(k + 1) * 128], xT[:, k, ts],
                    start=(k == 0), stop=(k == nk - 1)),
                    reads=[(wtag, wb), (xkey, k, tb)], writes=[("py", yb)], inc=(k == nk - 1))
            S.op("vector", lambda e, yb=yb, m=m, ts=ts: e.scalar_tensor_tensor(
                out=hT[:, m, ts], in0=py[yb][:], scalar=1.0, in1=hT[:, m, ts],
                op0=ALU.mult, op1=ALU.add),
                reads=[("py", yb), ("hT", m, tb)], writes=[("hT", m, tb)])


def load_hT(C, hT, h_in):
    hv = h_in.rearrange("(kc p) t -> p kc t", p=128)
    for kc in range(KC):
        C.S.dma("sync", hT[:, kc, :], hv[:, kc, :], "hin",
                writes=[("hT", kc, tb) for tb in range(NTB)])


def store_hT(C, hT, h_out, key="hT", nkc=KC):
    ov = h_out.rearrange("(kc p) t -> p kc t", p=128)
    for kc in range(nkc):
        C.S.dma("sync", ov[:, kc, :], hT[:, kc, :], "hout",
                reads=[(key, kc, tb) for tb in range(NTB)], writes=[("hout", kc)])
    C.S.wait_all("sync", [("hout", kc) for kc in range(nkc)])


def build_F():
    nc = new_prog()
    h_in = nc.dram_tensor("h_in", [D, T], F32, kind="ExternalInput").ap()
    h_out = nc.dram_tensor("h_out", [D, T], F32, kind="ExternalOutput").ap()
    wg_d = nc.dram_tensor("wg", [NF, 128, KC * 128], F32, kind="ExternalInput").ap()
    wu_d = nc.dram_tensor("wu", [NF, 128, KC * 128], F32, kind="ExternalInput").ap()
    wd_d = nc.dram_tensor("wd", [NF, 128, D], F32, kind="ExternalInput").ap()
    gains_d = nc.dram_tensor("gains", [128, KC], F32, kind="ExternalInput").ap()
    with contextlib.ExitStack() as stack:
        C = Ctx(nc, stack)
        S = C.S
        hT = C.sb("hT", [128, KC, T], F32)
        xnT = C.sb("xnT", [128, KC, T], BF16)
        gains = C.sb("gains_sb", [128, KC], F32)
        ones_bf = C.sb("ones", [128, 128], BF16)
        epsc = C.sb("epsc", [128, 1], F32)
        rstd = C.sb("rstd", [128, T], F32)
        sq = [C.sb(f"sq{i}", [128, TB], BF16) for i in range(3)]
        bufs = dict(
            wg=[C.sb(f"wg{i}", [128, KC * 128], BF16) for i in range(2)],
            wu=[C.sb(f"wu{i}", [128, KC * 128], BF16) for i in range(2)],
            wd=[C.sb(f"wd{i}", [128, FR, D], BF16) for i in range(2)],
            h1=[C.sb(f"h1{i}", [128, FR, T], BF16) for i in range(2)],
            sg=[C.sb(f"sg{i}", [128, TB], F32) for i in range(2)],
            pg=[C.ps(f"pg{i}", [128, TB]) for i in range(2)],
            pu=[C.ps(f"pu{i}", [128, TB]) for i in range(2)],
            py=[C.ps(f"py{i}", [128, TB]) for i in range(4)],
        )
        S.dma("sync", gains[:], gains_d, "c0", writes=["consts"])
        S.op("vector", lambda e: e.memset(epsc[:], EPS), writes=["epsc"])
        S.op("vector", lambda e: e.memset(ones_bf[:], 1.0), writes=["ones"])
        load_hT(C, hT, h_in)
        emit_norm(C, hT, xnT, lambda kc: gains[:, kc:kc + 1], ones_bf, epsc, sq,
                  bufs["py"], rstd, lambda tb: ("py", tb))
        emit_ffn(C, hT, xnT, wg_d, wu_d, wd_d, bufs, "f")
        store_hT(C, hT, h_out)
        S.emit()
    return nc


def build_OA():
    nc = new_prog()
    h_in = nc.dram_tensor("h_in", [D, T], F32, kind="ExternalInput").ap()
    u_out = nc.dram_tensor("u_out", [D, T], BF16, kind="ExternalOutput").ap()
    gains_d = nc.dram_tensor("gains", [128, KC], F32, kind="ExternalInput").ap()
    with contextlib.ExitStack() as stack:
        C = Ctx(nc, stack)
        S = C.S
        hT = C.sb("hT", [128, KC, T], F32)
        xnT = C.sb("xnT", [128, KC, T], BF16)
        gains = C.sb("gains_sb", [128, KC], F32)
        ones_bf = C.sb("ones", [128, 128], BF16)
        epsc = C.sb("epsc", [128, 1], F32)
        rstd = C.sb("rstd", [128, T], F32)
        sq = [C.sb(f"sq{i}", [128, TB], BF16) for i in range(3)]
        py = [C.ps(f"py{i}", [128, TB]) for i in range(2)]
        S.dma("sync", gains[:], gains_d, "c0", writes=["consts"])
        S.op("vector", lambda e: e.memset(epsc[:], EPS), writes=["epsc"])
        S.op("vector", lambda e: e.memset(ones_bf[:], 1.0), writes=["ones"])
        load_hT(C, hT, h_in)
        emit_norm(C, hT, xnT, lambda kc: gains[:, kc:kc + 1], ones_bf, epsc, sq,
                  py, rstd, lambda tb: ("py", tb))
        store_hT(C, xnT, u_out, key="xn")
        S.emit()
    return nc


def build_OD():
    nc = new_prog()
    h_in = nc.dram_tensor("h_in", [D, T], F32, kind="ExternalInput").ap()
    m_in = nc.dram_tensor("m_in", [D, T], F32, kind="ExternalInput").ap()
    h_out = nc.dram_tensor("h_out", [D, T], F32, kind="ExternalOutput").ap()
    wo_d = nc.dram_tensor("wo", [KC, 128, KC * 128], F32, kind="ExternalInput").ap()
    with contextlib.ExitStack() as stack:
        C = Ctx(nc, stack)
        S = C.S
        hT = C.sb("hT", [128, KC, T], F32)
        mT = C.sb("mT", [128, KC, T], BF16)
        wbuf = [C.sb(f"wo{i}", [128, KC * 128], BF16) for i in range(2)]
        py = [C.ps(f"py{i}", [128, TB]) for i in range(4)]
        load_hT(C, hT, h_in)
        mv = m_in.rearrange("(kc p) t -> p kc t", p=128)
        for kc in range(KC):
            S.dma("gpsimd", mT[:, kc, :], mv[:, kc, :], "min",
                  writes=[("mx", kc, tb) for tb in range(NTB)])
        emit_proj_resid(C, hT, mT, wo_d, KC, wbuf, py, "mx")
        store_hT(C, hT, h_out)
        S.emit()
    return nc


TA = 8192
NB = TA // 128
NTBA = TA // TB
DH = 128


def build_OC(nheads=2, nblk=NB):
    nc = new_prog()
    u_in = nc.dram_tensor("u_all", [D, TA], BF16, kind="ExternalInput").ap()
    wqk_d = nc.dram_tensor("wqk", [2, 2, 128, KC * 128], F32, kind="ExternalInput").ap()
    wvo_d = nc.dram_tensor("wvo", [2, 2, 128, KC * 128], F32, kind="ExternalInput").ap()
    wf_d = nc.dram_tensor("wf", [2, 128, KC], F32, kind="ExternalInput").ap()
    sc_d = nc.dram_tensor("sc", [2, 128, 4], F32, kind="ExternalInput").ap()
    mask_d = nc.dram_tensor("mask", [128, 128], F32, kind="ExternalInput").ap()
    o_out = nc.dram_tensor("o_out", [TA, 2 * DH], F32, kind="ExternalOutput").ap()
    uv = u_in.rearrange("(kc p) t -> p kc t", p=128)
    ntb = (nblk * 128) // TB
    with contextlib.ExitStack() as stack:
        C = Ctx(nc, stack)
        S = C.S
        ub = [C.sb(f"ub{i}", [128, KC, TB], BF16) for i in range(2)]
        wq = C.sb("wq", [128, KC * 128], BF16)
        wk = C.sb("wk", [128, KC * 128], BF16)
        wv = C.sb("wv", [128, KC * 128], BF16)
        wo = C.sb("wog", [128, KC * 128], BF16)
        wf = C.sb("wf_sb", [128, KC], BF16)
        sc = C.sb("sc_sb", [128, 4], F32)
        qT = C.sb("qT", [128, TA], BF16)
        kT = C.sb("kT", [128, TA], BF16)
        Va = C.sb("Va", [128, NB, DH + 1], BF16)
        sog = C.sb("sog", [128, NB, DH], BF16)
        lf = C.sb("lf", [1, TA], F32)
        cum = C.sb("cum", [1, NB, 128], F32)
        ones_bf = C.sb("ones", [128, 128], BF16)
        onesf = C.sb("onesf", [1, 128], F32)
        epsc = C.sb("epsc", [128, 1], F32)
        onec = C.sb("onec", [128, 1], F32)
        mask = C.sb("mask_sb", [128, 128], BF16)
        sq = [C.sb(f"sq{i}", [128, TB], BF16) for i in range(2)]
        rr = [C.sb(f"rr{i}", [128, TB], F32) for i in range(2)]
        ef = C.sb("ef", [1, TB], F32)
        cK = C.sb("cK", [128, 128], F32)
        bias = [C.sb(f"bias{i}", [128, NB], F32) for i in range(2)]
        PT = [C.sb(f"PT{i}", [128, 128], BF16) for i in range(4)]
        rinv = [C.sb(f"rinv{i}", [128, 1], F32) for i in range(2)]
        ot = [C.sb(f"ot{i}", [128, DH], F32) for i in range(2)]
        pb = [C.ps(f"pb{i}", [128, TB]) for i in range(8)]

        S.op("vector", lambda e: e.memset(epsc[:], EPS), writes=["epsc"])
        S.op("vector", lambda e: e.memset(onec[:], 1.0), writes=["onec"])
        S.op("vector", lambda e: e.memset(ones_bf[:], 1.0), writes=["ones"])
        S.op("vector", lambda e: e.memset(onesf[:], 1.0), writes=["onesf"])
        S.op("vector", lambda e: e.memset(Va[:, :, DH:DH + 1], 1.0), writes=["Va1"])
        S.dma("gpsimd", mask[:], mask_d, "cmask", writes=["mask"])
        scale = float(DH) ** -0.5
        npt = 0
        nst = 0
        for h in range(nheads):
            S.dma("gpsimd", wq[:], wqk_d[h, 0], "wq", writes=["wq"])
            S.dma("gpsimd", wk[:], wqk_d[h, 1], "wk", writes=["wk"])
            S.dma("gpsimd", wv[:], wvo_d[h, 0], "wv", writes=["wv"])
            S.dma("gpsimd", wo[:], wvo_d[h, 1], "wog", writes=["wog"])
            S.dma("gpsimd", wf[:], wf_d[h], "wf", writes=["wf"])
            S.dma("sync", sc[:], sc_d[h], "sc", writes=["sc"])
            for tb in range(ntb):
                b = tb % 2
                ts = slice(tb * TB, (tb + 1) * TB)
                S.dma("sync", ub[b][:], uv[:, :, ts], f"ub{b}", writes=[("ub", b)])
                for which, (wt, wkey, dst, dkey, gcol) in enumerate(
                        ((wq, "wq", qT, "qT", 0), (wk, "wk", kT, "kT", 1))):
                    pbi = which
                    for kc in range(KC):
                        S.op("tensor", lambda e, pbi=pbi, wt=wt, kc=kc, b=b: e.matmul(
                            pb[pbi][:], wt[:, kc * 128:(kc + 1) * 128], ub[b][:, kc, :],
                            start=(kc == 0), stop=(kc == KC - 1)),
                            reads=[wkey, ("ub", b)], writes=[("pb", pbi)], inc=(kc == KC - 1))
                    S.op("scalar", lambda e, pbi=pbi: e.activation(
                        out=sq[pbi][:], in_=pb[pbi][:], func=AF.Square),
                        reads=[("pb", pbi)], writes=[("sq", pbi)])
                    S.op("tensor", lambda e, pbi=pbi: e.matmul(
                        pb[2][:], ones_bf[:], sq[pbi][:], start=True, stop=True),
                        reads=[("sq", pbi), "ones"], writes=[("pb", 2)])
                    S.op("scalar", lambda e, pbi=pbi: e.activation(
                        out=rr[pbi][:], in_=pb[2][:], func=AF.Sqrt, scale=1.0 / DH, bias=epsc[:]),
                        reads=[("pb", 2), "epsc"], writes=[("rr", pbi)])
                    S.op("vector", lambda e, pbi=pbi: e.reciprocal(out=rr[pbi][:], in_=rr[pbi][:]),
                         reads=[("rr", pbi)], writes=[("rr", pbi)])
                    S.op("vector", lambda e, pbi=pbi, dst=dst, ts=ts, gcol=gcol: e.scalar_tensor_tensor(
                        out=dst[:, ts], in0=pb[pbi][:], scalar=sc[:, gcol:gcol + 1], in1=rr[pbi][:],
                        op0=ALU.mult, op1=ALU.mult),
                        reads=[("pb", pbi), ("rr", pbi), "sc"], writes=[(dkey, tb)])
                for pbi, wt, wkey in ((3, wv, "wv"), (4, wo, "wog")):
                    for sbk in range(4):
                        cs = slice(sbk * 128, (sbk + 1) * 128)
                        for kc in range(KC):
                            S.op("tensor", lambda e, pbi=pbi, wt=wt, kc=kc, b=b, cs=cs: e.matmul(
                                pb[pbi][:, cs], ub[b][:, kc, cs], wt[:, kc * 128:(kc + 1) * 128],
                                start=(kc == 0), stop=(kc == KC - 1)),
                                reads=[wkey, ("ub", b)], writes=[("pb", pbi)],
                                inc=(kc == KC - 1))
                for sbk in range(4):
                    cs = slice(sbk * 128, (sbk + 1) * 128)
                    blk = tb * 4 + sbk
                    S.op("vector", lambda e, cs=cs, blk=blk: e.tensor_copy(
                        out=Va[:, blk, 0:DH], in_=pb[3][:, cs]),
                        reads=[("pb", 3)], writes=[("Va", blk)])
                    S.op("scalar", lambda e, cs=cs, blk=blk: e.activation(
                        out=sog[:, blk, :], in_=pb[4][:, cs], func=AF.Sigmoid),
                        reads=[("pb", 4)], writes=[("sog", blk)])
                for kc in range(KC):
                    S.op("tensor", lambda e, kc=kc, b=b: e.matmul(
                        pb[5][0:1, :], wf[:, kc:kc + 1], ub[b][:, kc, :],
                        start=(kc == 0), stop=(kc == KC - 1)),
                        reads=["wf", ("ub", b)], writes=[("pb", 5)], inc=(kc == KC - 1))
                S.op("scalar", lambda e: e.activation(
                    out=ef[:], in_=pb[5][0:1, :], func=AF.Exp, scale=-1.0, bias=sc[0:1, 2:3]),
                    reads=[("pb", 5), "sc"], writes=["ef"])
                S.op("scalar", lambda e: e.activation(
                    out=ef[:], in_=ef[:], func=AF.Ln, bias=onec[0:1, :]),
                    reads=["ef", "onec"], writes=["ef"])
                S.op("vector", lambda e, ts=ts: e.tensor_scalar(
                    out=lf[0:1, ts], in0=ef[:], scalar1=-1.0, scalar2=None, op0=ALU.mult),
                    reads=["ef"], writes=[("lf", tb)])
            for blk in range(nblk):
                init = 0.0 if blk == 0 else cum[0:1, blk - 1, 127:128]
                S.op("vector", lambda e, blk=blk, init=init: e.tensor_tensor_scan(
                    out=cum[0:1, blk, :], data0=onesf[0:1, :], data1=lf[0:1, blk * 128:(blk + 1) * 128],
                    initial=init, op0=ALU.mult, op1=ALU.add),
                    reads=[("lf", blk // 4), "onesf", "cum"], writes=["cum"])
            for j in range(nblk):
                S.op("tensor", lambda e, j=j: e.matmul(
                    pb[5][:, j:j + 1], cum[0:1, j, :], onesf[0:1, 0:1], start=True, stop=True),
                    reads=["cum", "onesf"], writes=[("pb", 5)], inc=(j == nblk - 1))
            S.op("tensor", lambda e: e.matmul(
                pb[5][:, 64:64 + nblk], onesf[0:1, :], cum[0:1, 0:nblk, 127], start=True, stop=True),
                reads=["cum", "onesf"], writes=[("pb", 5)])
            S.op("vector", lambda e: e.tensor_copy(out=cK[:, 0:nblk], in_=pb[5][:, 0:nblk]),
                 reads=[("pb", 5)], writes=["cK"])
            S.op("vector", lambda e: e.tensor_copy(out=cK[:, 64:64 + nblk], in_=pb[5][:, 64:64 + nblk]),
                 reads=[("pb", 5), "cK"], writes=["cK"])
            for i in range(nblk):
                bi = i % 2
                qs = slice(i * 128, (i + 1) * 128)
                S.op("vector", lambda e, bi=bi, i=i: e.tensor_scalar(
                    out=bias[bi][:, 0:i + 1], in0=cK[:, 0:i + 1], scalar1=-1.0,
                    scalar2=cK[:, 64 + i:65 + i], op0=ALU.mult, op1=ALU.add),
                    reads=["cK"], writes=[("bias", bi)])
                po = 6 + (i % 2)
                for j in range(i + 1):
                    ks = slice(j * 128, (j + 1) * 128)
                    sb_ = nst % 2
                    nst += 1
                    S.op("tensor", lambda e, sb_=sb_, ks=ks, qs=qs: e.matmul(
                        pb[sb_][:, 0:128], kT[:, ks], qT[:, qs], start=True, stop=True),
                        reads=[("kT", j // 4), ("qT", i // 4)], writes=[("pb", sb_)])
                    p = npt % len(PT)
                    npt += 1
                    S.op("scalar", lambda e, sb_=sb_, p=p, bi=bi, j=j: e.activation(
                        out=PT[p][:], in_=pb[sb_][:, 0:128], func=AF.Exp, scale=scale,
                        bias=bias[bi][:, j:j + 1]),
                        reads=[("pb", sb_), ("bias", bi)], writes=[("PT", p)])
                    if j == i:
                        S.op("vector", lambda e, p=p: e.tensor_tensor(
                            out=PT[p][:], in0=PT[p][:], in1=mask[:], op=ALU.mult),
                            reads=[("PT", p), "mask"], writes=[("PT", p)])
                    S.op("tensor", lambda e, po=po, p=p, j=j, i=i: e.matmul(
                        pb[po][:, 0:DH + 1], PT[p][:], Va[:, j, :], start=(j == 0), stop=(j == i)),
                        reads=[("PT", p), ("Va", j), "Va1"], writes=[("pb", po)])
                S.op("vector", lambda e, po=po, bi=bi: e.reciprocal(
                    out=rinv[bi][:], in_=pb[po][:, DH:DH + 1]),
                    reads=[("pb", po)], writes=[("rinv", bi)])
                S.op("vector", lambda e, po=po, bi=bi, i=i: e.scalar_tensor_tensor(
                    out=ot[bi][:], in0=pb[po][:, 0:DH], scalar=rinv[bi][:, 0:1], in1=sog[:, i, :],
                    op0=ALU.mult, op1=ALU.mult),
                    reads=[("pb", po), ("rinv", bi), ("sog", i)], writes=[("ot", bi)])
                S.dma("sync", o_out[qs, h * DH:(h + 1) * DH], ot[bi][:], f"oo{bi}",
                      reads=[("ot", bi)], writes=[("oout", bi)])
        S.wait_all("sync", [("oout", 0), ("oout", 1)])
        S.emit()
    return nc


NZ = 41


def build_EA():
    nc = new_prog()
    h_in = nc.dram_tensor("h_in", [D, T], F32, kind="ExternalInput").ap()
    z_out = nc.dram_tensor("z_out", [NZ * 128, T], F32, kind="ExternalOutput").ap()
    w_d = nc.dram_tensor("w", [NZ, 128, KC * 128], F32, kind="ExternalInput").ap()
    gains_d = nc.dram_tensor("gains", [128, KC], F32, kind="ExternalInput").ap()
    with contextlib.ExitStack() as stack:
        C = Ctx(nc, stack)
        S = C.S
        hT = C.sb("hT", [128, KC, T], F32)
        xnT = C.sb("xnT", [128, KC, T], BF16)
        gains = C.sb("gains_sb", [128, KC], F32)
        ones_bf = C.sb("ones", [128, 128], BF16)
        epsc = C.sb("epsc", [128, 1], F32)
        rstd = C.sb("rstd", [128, T], F32)
        sq = [C.sb(f"sq{i}", [128, TB], BF16) for i in range(3)]
        wbuf = [C.sb(f"w{i}", [128, KC * 128], BF16) for i in range(2)]
        stg = [C.sb(f"stg{i}", [128, TB], F32) for i in range(4)]
        py = [C.ps(f"py{i}", [128, TB]) for i in range(4)]
        S.dma("sync", gains[:], gains_d, "c0", writes=["consts"])
        S.op("vector", lambda e: e.memset(epsc[:], EPS), writes=["epsc"])
        S.op("vector", lambda e: e.memset(ones_bf[:], 1.0), writes=["ones"])
        load_hT(C, hT, h_in)
        emit_norm(C, hT, xnT, lambda kc: gains[:, kc:kc + 1], ones_bf, epsc, sq,
                  py, rstd, lambda tb: ("py", tb))
        n = 0
        for m in range(NZ):
            wb = m % 2
            S.dma("gpsimd", wbuf[wb][:], w_d[m], f"w{wb}", writes=[("w", wb)])
            for tb in range(NTB):
                ts = slice(tb * TB, (tb + 1) * TB)
                yb = n % 4
                n += 1
                for k in range(KC):
                    S.op("tensor", lambda e, yb=yb, wb=wb, k=k, ts=ts: e.matmul(
                        py[yb][:], wbuf[wb][:, k * 128:(k + 1) * 128], xnT[:, k, ts],
                        start=(k == 0), stop=(k == KC - 1)),
                        reads=[("w", wb), ("xn", k, tb)], writes=[("py", yb)], inc=(k == KC - 1))
                eng = "vector" if yb % 2 == 0 else "scalar"
                if eng == "vector":
                    S.op("vector", lambda e, yb=yb: e.tensor_copy(out=stg[yb][:], in_=py[yb][:]),
                         reads=[("py", yb)], writes=[("stg", yb)])
                else:
                    S.op("scalar", lambda e, yb=yb: e.copy(out=stg[yb][:], in_=py[yb][:]),
                         reads=[("py", yb)], writes=[("stg", yb)])
                S.dma("sync", z_out[m * 128:(m + 1) * 128, ts], stg[yb][:], f"zo{yb}",
                      reads=[("stg", yb)], writes=[("zo", yb)])
        S.wait_all("sync", [("zo", i) for i in range(4)])
        S.emit()
    return nc


GC = 64
NCH = T // GC
DK = 128
DV = 256


def build_EG(nsuper=TA // T):
    nc = new_prog()
    q_in = nc.dram_tensor("qT", [DK, TA], F32, kind="ExternalInput").ap()
    k_in = nc.dram_tensor("kT", [DK, TA], F32, kind="ExternalInput").ap()
    v_in = nc.dram_tensor("v", [TA, DV], F32, kind="ExternalInput").ap()
    gk_in = nc.dram_tensor("gkT", [16, TA], F32, kind="ExternalInput").ap()
    w2_d = nc.dram_tensor("w2", [16, DK], F32, kind="ExternalInput").ap()
    sc_d = nc.dram_tensor("sc", [128, 4], F32, kind="ExternalInput").ap()
    m64_d = nc.dram_tensor("m64", [GC, GC], F32, kind="ExternalInput").ap()
    id_d = nc.dram_tensor("ident", [128, 128], F32, kind="ExternalInput").ap()
    o_out = nc.dram_tensor("o_out", [DV, TA], F32, kind="ExternalOutput").ap()
    vv = v_in.rearrange("(n p) c -> p n c", p=GC)
    with contextlib.ExitStack() as stack:
        C = Ctx(nc, stack)
        S = C.S
        qf = C.sb("qf", [128, T], F32)
        kf = C.sb("kf", [128, T], F32)
        gk = C.sb("gk", [16, T], BF16)
        vs = C.sb("vs", [GC, NCH, DV], BF16)
        w2 = C.sb("w2_sb", [16, DK], BF16)
        sc = C.sb("sc_sb", [128, 4], F32)
        m64 = C.sb("m64_sb", [GC, GC], F32)
        ident = C.sb("ident_sb", [128, 128], BF16)
        ones_bf = C.sb("ones", [128, 128], BF16)
        epsc = C.sb("epsc", [128, 1], F32)
        onec = C.sb("onec", [128, 1], F32)
        la = C.sb("la", [128, T], F32)
        cum = C.sb("cum", [128, T], F32)
        nref = C.sb("nref", [128, NCH], F32)
        ee = [C.sb(f"ee{i}", [128, GC], F32) for i in range(4)]
        qt = C.sb("qt", [128, T], BF16)
        kt = C.sb("kt", [128, T], BF16)
        qi = C.sb("qi", [128, T], BF16)
        ko = C.sb("ko", [128, T], BF16)
        dec = C.sb("dec", [128, NCH], F32)
        kot = [C.sb(f"kot{i}", [GC, DK], BF16) for i in range(2)]
        ats = [C.sb(f"ats{i}", [GC, GC], BF16) for i in range(2)]
        St = C.sb("St", [128, DV], F32)
        Sb = [C.sb(f"Sb{i}", [128, DV], BF16) for i in range(2)]
        oT = C.sb("oT", [128, 2, T], F32)
        onT = C.sb("onT", [128, 2, T], F32)
        rstd = C.sb("rstd", [128, T], F32)
        sq = [C.sb(f"sq{i}", [128, TB], BF16) for i in range(3)]
        pb = [C.ps(f"pb{i}", [128, TB]) for i in range(8)]
        scale = float(DK) ** -0.5
        onesrow = C.sb("onesrow", [128, GC], F32)
        S.op("vector", lambda e: e.memset(onesrow[:], 1.0), writes=["onesrow"])

        S.op("vector", lambda e: e.memset(epsc[:], EPS), writes=["epsc"])
        S.op("vector", lambda e: e.memset(onec[:], 1.0), writes=["onec"])
        S.op("vector", lambda e: e.memset(ones_bf[:], 1.0), writes=["ones"])
        S.op("vector", lambda e: e.memset(St[:], 0.0), writes=["St"])
        S.op("vector", lambda e: e.memset(Sb[0][:], 0.0), writes=[("Sb", 0)])
        S.dma("gpsimd", w2[:], w2_d, "c1", writes=["w2"])
        S.dma("gpsimd", ident[:], id_d, "c2", writes=["ident"])
        S.dma("sync", sc[:], sc_d, "c3", writes=["sc", "consts"])
        S.dma("sync", m64[:], m64_d, "c4", writes=["m64"])
        nsb = 0
        for sp in range(nsuper):
            t0 = sp * T
            S.dma("sync", qf[:], q_in[:, t0:t0 + T], "qf", writes=["qf"])
            S.dma("sync", kf[:], k_in[:, t0:t0 + T], "kf", writes=["kf"])
            S.dma("gpsimd", gk[:], gk_in[:, t0:t0 + T], "gk", writes=["gk"])
            S.dma("gpsimd", vs[:], vv[:, sp * NCH:(sp + 1) * NCH, :], "vs", writes=["vs"])
            for tb in range(NTB):
                ts = slice(tb * TB, (tb + 1) * TB)
                S.op("tensor", lambda e, ts=ts: e.matmul(
                    pb[0][:], w2[:], gk[:, ts], start=True, stop=True),
                    reads=["w2", "gk"], writes=[("pb", 0)])
                S.op("scalar", lambda e, ts=ts: e.activation(
                    out=la[:, ts], in_=pb[0][:], func=AF.Exp, scale=-1.0, bias=sc[:, 0:1]),
                    reads=[("pb", 0), "sc"], writes=[("la", tb)])
                S.op("scalar", lambda e, ts=ts: e.activation(
                    out=la[:, ts], in_=la[:, ts], func=AF.Ln, bias=onec[:]),
                    reads=[("la", tb), "onec"], writes=[("la", tb)])
                S.op("vector", lambda e, ts=ts: e.tensor_scalar(
                    out=la[:, ts], in0=la[:, ts], scalar1=-1.0 / 16.0, scalar2=None, op0=ALU.mult),
                    reads=[("la", tb)], writes=[("la", tb)])
            for n in range(NCH):
                cs = slice(n * GC, (n + 1) * GC)
                S.op("vector", lambda e, cs=cs: e.tensor_tensor_scan(
                    out=cum[:, cs], data0=onesrow[:, 0:GC], data1=la[:, cs],
                    initial=0.0, op0=ALU.mult, op1=ALU.add),
                    reads=[("la", (n * GC) // TB), "onesrow"], writes=[("cum", n)])
            S.op("vector", lambda e: e.tensor_scalar(
                out=nref[:], in0=cum[:, GC // 2::GC], scalar1=-1.0, scalar2=None, op0=ALU.mult),
                reads=[("cum", n) for n in range(NCH)], writes=["nref"])
            S.op("scalar", lambda e: e.activation(
                out=dec[:], in_=cum[:, GC - 1::GC], func=AF.Exp),
                reads=[("cum", n) for n in range(NCH)], writes=["dec"])
            for n in range(NCH):
                cs = slice(n * GC, (n + 1) * GC)
                rcol = n * GC + GC // 2
                lcol = n * GC + GC - 1
                S.op("scalar", lambda e, cs=cs, n=n: e.activation(
                    out=ee[0][:], in_=cum[:, cs], func=AF.Exp, bias=nref[:, n:n + 1]),
                    reads=[("cum", n), "nref"], writes=[("ee", 0)])
                S.op("scalar", lambda e, cs=cs, rcol=rcol: e.activation(
                    out=ee[1][:], in_=cum[:, cs], func=AF.Exp, scale=-1.0, bias=cum[:, rcol:rcol + 1]),
                    reads=[("cum", n)], writes=[("ee", 1)])
                S.op("scalar", lambda e, cs=cs: e.activation(
                    out=ee[2][:], in_=cum[:, cs], func=AF.Exp),
                    reads=[("cum", n)], writes=[("ee", 2)])
                S.op("scalar", lambda e, cs=cs, lcol=lcol: e.activation(
                    out=ee[3][:], in_=cum[:, cs], func=AF.Exp, scale=-1.0, bias=cum[:, lcol:lcol + 1]),
                    reads=[("cum", n)], writes=[("ee", 3)])
                S.op("vector", lambda e, cs=cs: e.scalar_tensor_tensor(
                    out=qt[:, cs], in0=qf[:, cs], scalar=scale, in1=ee[0][:], op0=ALU.mult, op1=ALU.mult),
                    reads=["qf", ("ee", 0)], writes=[("qt", n)])
                S.op("vector", lambda e, cs=cs: e.tensor_tensor(
                    out=kt[:, cs], in0=kf[:, cs], in1=ee[1][:], op=ALU.mult),
                    reads=["kf", ("ee", 1)], writes=[("kt", n)])
                S.op("vector", lambda e, cs=cs: e.scalar_tensor_tensor(
                    out=qi[:, cs], in0=qf[:, cs], scalar=scale, in1=ee[2][:], op0=ALU.mult, op1=ALU.mult),
                    reads=["qf", ("ee", 2)], writes=[("qi", n)])
                S.op("vector", lambda e, cs=cs: e.tensor_tensor(
                    out=ko[:, cs], in0=kf[:, cs], in1=ee[3][:], op=ALU.mult),
                    reads=["kf", ("ee", 3)], writes=[("ko", n)])
            for n in range(NCH):
                cs = slice(n * GC, (n + 1) * GC)
                b2 = n % 2
                sbi = nsb % 2
                S.op("tensor", lambda e, cs=cs: e.matmul(
                    pb[1][0:GC, 0:GC], kt[:, cs], qt[:, cs], start=True, stop=True),
                    reads=[("kt", n), ("qt", n)], writes=[("pb", 1)])
                S.op("vector", lambda e, b2=b2: e.tensor_tensor(
                    out=ats[b2][:], in0=pb[1][0:GC, 0:GC], in1=m64[:], op=ALU.mult),
                    reads=[("pb", 1), "m64"], writes=[("ats", b2)])
                S.op("tensor", lambda e, cs=cs: e.matmul(
                    pb[2][0:GC, 0:DK], ko[:, cs], ident[:], start=True, stop=True),
                    reads=[("ko", n), "ident"], writes=[("pb", 2)])
                S.op("scalar", lambda e, b2=b2: e.copy(out=kot[b2][:], in_=pb[2][0:GC, 0:DK]),
                     reads=[("pb", 2)], writes=[("kot", b2)])
                for c in range(2):
                    ob = 3 if c == 0 else 7
                    S.op("tensor", lambda e, c=c, b2=b2, n=n, ob=ob: e.matmul(
                        pb[ob][:, 0:GC], vs[:, n, c * 128:(c + 1) * 128], ats[b2][:],
                        start=True, stop=False),
                        reads=["vs", ("ats", b2)], writes=[("pb", ob)], inc=False)
                    S.op("tensor", lambda e, c=c, sbi=sbi, cs=cs, ob=ob: e.matmul(
                        pb[ob][:, 0:GC], Sb[sbi][:, c * 128:(c + 1) * 128], qi[:, cs],
                        start=False, stop=True),
                        reads=[("Sb", sbi), ("qi", n)], writes=[("pb", ob)])
                    S.op("vector", lambda e, c=c, cs=cs, ob=ob: e.tensor_copy(
                        out=oT[:, c, cs], in_=pb[ob][:, 0:GC]),
                        reads=[("pb", ob)], writes=[("oT", c, (n * GC) // TB)])
                S.op("tensor", lambda e, b2=b2, n=n: e.matmul(
                    pb[4][:, 0:DV], kot[b2][:], vs[:, n, :], start=True, stop=True),
                    reads=[("kot", b2), "vs"], writes=[("pb", 4)])
                S.op("vector", lambda e, n=n: e.scalar_tensor_tensor(
                    out=St[:], in0=St[:], scalar=dec[:, n:n + 1], in1=pb[4][:, 0:DV],
                    op0=ALU.mult, op1=ALU.add),
                    reads=["St", "dec", ("pb", 4)], writes=["St"])
                nsb += 1
                S.op("scalar", lambda e, nb=nsb % 2: e.copy(out=Sb[nb][:], in_=St[:]),
                     reads=["St"], writes=[("Sb", nsb % 2)])
            emit_norm(C, oT, onT, lambda c: sc[:, 1 + c:2 + c], ones_bf, epsc, sq,
                      [pb[5], pb[6]], rstd, lambda tb: ("pb", 5 + tb), dim=DV, nkc=2, inkey="oT")
            ov = o_out.rearrange("(c p) t -> p c t", p=128)
            for c in range(2):
                S.dma("sync", ov[:, c, t0:t0 + T], onT[:, c, :], "oo",
                      reads=[("xn", c, tb) for tb in range(NTB)], writes=[("oo", c)])
        S.wait_all("sync", [("oo", 0), ("oo", 1)])
        S.emit()
    return nc


def ones_f_row(C):
    if not hasattr(C, "_onesrow"):
        C._onesrow = C.sb("onesrow", [128, GC], F32)
        C.S.op("vector", lambda e: e.memset(C._onesrow[:], 1.0), writes=["onesrow"])
    return C._onesrow


CW = 31
HALO = CW - 1
NCC = 8


def build_EB():
    nc = new_prog()
    h_in = nc.dram_tensor("h_in", [D, T], F32, kind="ExternalInput").ap()
    on_in = nc.dram_tensor("on_in", [1024, T], F32, kind="ExternalInput").ap()
    zg_in = nc.dram_tensor("zg_in", [1024, T], F32, kind="ExternalInput").ap()
    zv_in = nc.dram_tensor("zv_in", [1024, HALO + T], F32, kind="ExternalInput").ap()
    zs_in = nc.dram_tensor("zs_in", [1024, HALO + T], F32, kind="ExternalInput").ap()
    cs_d = nc.dram_tensor("cs", [128, NCC, 34], F32, kind="ExternalInput").ap()
    wo_d = nc.dram_tensor("wo", [KC, 128, KC * 128], F32, kind="ExternalInput").ap()
    h_out = nc.dram_tensor("h_out", [D, T], F32, kind="ExternalOutput").ap()
    with contextlib.ExitStack() as stack:
        C = Ctx(nc, stack)
        S = C.S
        hT = C.sb("hT", [128, KC, T], F32)
        mixT = C.sb("mixT", [128, KC, T], BF16)
        yT = C.sb("yT", [128, NCC, T], F32)
        cs = C.sb("cs_sb", [128, NCC, 34], F32)
        sa = [C.sb(f"sa{i}", [128, HALO + T], F32) for i in range(2)]
        sb_ = [C.sb(f"sb{i}", [128, HALO + T], F32) for i in range(2)]
        cc = [C.sb(f"cc{i}", [128, HALO + T], F32) for i in range(2)]
        ones_bf = C.sb("ones", [128, 128], BF16)
        epsc = C.sb("epsc", [128, 1], F32)
        ybf = [C.sb(f"ybf{i}", [128, TB], BF16) for i in range(3)]
        mean = C.sb("mean", [128, T], F32)
        var = C.sb("var", [128, T], F32)
        rstd = C.sb("rstd", [128, T], F32)
        tmp = [C.sb(f"tmp{i}", [128, T], F32) for i in range(2)]
        wbuf = [C.sb(f"wo{i}", [128, KC * 128], BF16) for i in range(2)]
        py = [C.ps(f"py{i}", [128, TB]) for i in range(4)]
        ps1 = [C.ps(f"ps1{i}", [128, TB]) for i in range(2)]
        ps2 = [C.ps(f"ps2{i}", [128, TB]) for i in range(2)]
        S.op("vector", lambda e: e.memset(epsc[:], EPS), writes=["epsc"])
        S.op("vector", lambda e: e.memset(ones_bf[:], 1.0), writes=["ones"])
        S.dma("sync", cs[:], cs_d, "c0", writes=["cs"])
        load_hT(C, hT, h_in)
        for c in range(NCC):
            b = c % 2
            rs = slice(c * 128, (c + 1) * 128)
            S.dma("sync", sa[b][:, 0:T], on_in[rs, :], f"sa{b}", writes=[("sa", b)])
            S.dma("sync", sb_[b][:, 0:T], zg_in[rs, :], f"sb{b}", writes=[("sb", b)])
            S.op("scalar", lambda e, b=b: e.activation(out=sb_[b][:, 0:T], in_=sb_[b][:, 0:T], func=AF.Silu),
                 reads=[("sb", b)], writes=[("sb", b)])
            S.op("vector", lambda e, b=b, c=c: e.tensor_tensor(
                out=mixT[:, c, :], in0=sa[b][:, 0:T], in1=sb_[b][:, 0:T], op=ALU.mult),
                reads=[("sa", b), ("sb", b)], writes=[("mx", c, 0), ("mx", c, 1)])
        for c in range(NCC):
            b = c % 2
            rs = slice(c * 128, (c + 1) * 128)
            S.dma("sync", sa[b][:], zv_in[rs, :], f"sa{b}", writes=[("sa", b)])
            S.dma("sync", sb_[b][:], zs_in[rs, :], f"sb{b}", writes=[("sb", b)])
            S.op("scalar", lambda e, b=b: e.activation(out=sb_[b][:], in_=sb_[b][:], func=AF.Sigmoid),
                 reads=[("sb", b)], writes=[("sb", b)])
            S.op("vector", lambda e, b=b: e.tensor_tensor(
                out=cc[b][:], in0=sa[b][:], in1=sb_[b][:], op=ALU.mult),
                reads=[("sa", b), ("sb", b)], writes=[("cc", b)])
            S.op("vector", lambda e, b=b, c=c: e.tensor_scalar(
                out=yT[:, c, :], in0=cc[b][:, 0:T], scalar1=cs[:, c, 0:1], scalar2=cs[:, c, 31:32],
                op0=ALU.mult, op1=ALU.add),
                reads=[("cc", b), "cs"], writes=[("y", c)])
            for j in range(1, CW):
                S.op("vector", lambda e, b=b, c=c, j=j: e.scalar_tensor_tensor(
                    out=yT[:, c, :], in0=cc[b][:, j:j + T], scalar=cs[:, c, j:j + 1], in1=yT[:, c, :],
                    op0=ALU.mult, op1=ALU.add),
                    reads=[("cc", b), "cs", ("y", c)], writes=[("y", c)])
        nb = 0
        for tb in range(NTB):
            ts = slice(tb * TB, (tb + 1) * TB)
            for c in range(NCC):
                i1 = nb % 3
                nb += 1
                S.op("scalar", lambda e, i1=i1, c=c, ts=ts: e.copy(out=ybf[i1][:], in_=yT[:, c, ts]),
                     reads=[("y", c)], writes=[("ybf", i1)])
                S.op("tensor", lambda e, i1=i1, tb=tb, c=c: e.matmul(
                    ps1[tb][:], ones_bf[:], ybf[i1][:], start=(c == 0), stop=(c == NCC - 1)),
                    reads=[("ybf", i1), "ones"], writes=[("ps1", tb)])
                i2 = nb % 3
                nb += 1
                S.op("scalar", lambda e, i2=i2, c=c, ts=ts: e.activation(
                    out=ybf[i2][:], in_=yT[:, c, ts], func=AF.Square),
                    reads=[("y", c)], writes=[("ybf", i2)])
                S.op("tensor", lambda e, i2=i2, tb=tb, c=c: e.matmul(
                    ps2[tb][:], ones_bf[:], ybf[i2][:], start=(c == 0), stop=(c == NCC - 1)),
                    reads=[("ybf", i2), "ones"], writes=[("ps2", tb)])
            S.op("vector", lambda e, tb=tb, ts=ts: e.tensor_scalar(
                out=mean[:, ts], in0=ps1[tb][:], scalar1=1.0 / 1024, scalar2=None, op0=ALU.mult),
                reads=[("ps1", tb)], writes=[("mean", tb)])
            S.op("vector", lambda e, ts=ts: e.tensor_tensor(
                out=var[:, ts], in0=mean[:, ts], in1=mean[:, ts], op=ALU.mult),
                reads=[("mean", tb)], writes=[("var", tb)])
            S.op("vector", lambda e, tb=tb, ts=ts: e.scalar_tensor_tensor(
                out=var[:, ts], in0=ps2[tb][:], scalar=1.0 / 1024, in1=var[:, ts],
                op0=ALU.mult, op1=ALU.subtract),
                reads=[("ps2", tb), ("var", tb)], writes=[("var", tb)])
            S.op("scalar", lambda e, ts=ts: e.activation(
                out=rstd[:, ts], in_=var[:, ts], func=AF.Sqrt, bias=epsc[:]),
                reads=[("var", tb), "epsc"], writes=[("rstd", tb)])
            S.op("vector", lambda e, ts=ts: e.reciprocal(out=rstd[:, ts], in_=rstd[:, ts]),
                 reads=[("rstd", tb)], writes=[("rstd", tb)])
        for c in range(NCC):
            b = c % 2
            S.op("vector", lambda e, b=b, c=c: e.tensor_tensor(
                out=tmp[b][:], in0=yT[:, c, :], in1=mean[:], op=ALU.subtract),
                reads=[("y", c), ("mean", 0), ("mean", 1)], writes=[("tmp", b)])
            S.op("vector", lambda e, b=b: e.tensor_tensor(
                out=tmp[b][:], in0=tmp[b][:], in1=rstd[:], op=ALU.mult),
                reads=[("tmp", b), ("rstd", 0), ("rstd", 1)], writes=[("tmp", b)])
            S.op("scalar", lambda e, b=b, c=c: e.activation(
                out=mixT[:, NCC + c, :], in_=tmp[b][:], func=AF.Silu,
                scale=cs[:, c, 32:33], bias=cs[:, c, 33:34]),
                reads=[("tmp", b), "cs"], writes=[("mx", NCC + c, 0), ("mx", NCC + c, 1)])
        emit_proj_resid(C, hT, mixT, wo_d, KC, wbuf, py, "mx")
        store_hT(C, hT, h_out)
        S.emit()
    return nc


import ml_dtypes

_PROGS = {}


def _prog(name, fn):
    if name not in _PROGS:
        _PROGS[name] = fn()
    return _PROGS[name]


def _run(nc, in_maps):
    res = run_bass_kernel_spmd(nc, in_maps, core_ids=list(range(len(in_maps))))
    return res.results


def _c(a):
    return np.ascontiguousarray(a)


def lay_w(w, nm):
    K = w.shape[0]
    return _c(w.reshape(K // 128, 128, nm, 128).transpose(2, 1, 0, 3)).reshape(nm, 128, K)


def lay_g(g):
    return _c(g.reshape(-1, 128).T)


def _shards(aT):
    return [_c(aT[:, c * T:(c + 1) * T]) for c in range(NCORES)]


def run_ffn(hT, norm_g, wg, wu, wd):
    nc = _prog("F", build_F)
    wg_l, wu_l = lay_w(wg, NF), lay_w(wu, NF)
    wd_l = _c(wd.reshape(NF, 128, D))
    gl = lay_g(norm_g)
    hs = _shards(hT)
    res = _run(nc, [dict(h_in=hs[c], wg=wg_l, wu=wu_l, wd=wd_l, gains=gl) for c in range(NCORES)])
    return np.concatenate([r["h_out"] for r in res], axis=1)


def prep_cs(conv_w, conv_b, ln_g, ln_b):
    cs = np.zeros((128, NCC, 34), np.float32)
    cs[:, :, 0:31] = conv_w.T.reshape(NCC, 128, 31).transpose(1, 0, 2)
    cs[:, :, 31] = conv_b.reshape(NCC, 128).T
    cs[:, :, 32] = ln_g.reshape(NCC, 128).T
    cs[:, :, 33] = ln_b.reshape(NCC, 128).T
    return cs


_M64 = (np.arange(GC)[None, :] >= np.arange(GC)[:, None]).astype(np.float32)
_M128 = (np.arange(128)[None, :] >= np.arange(128)[:, None]).astype(np.float32)
_ID = np.eye(128, dtype=np.float32)


def run_even(hT, mix_g, w_in, w_gk2, b_gk, out_g, conv_w, conv_b, ln_g, ln_b, w_out):
    nc = _prog("EA", build_EA)
    wpad = np.zeros((D, NZ * 128), np.float32)
    wpad[:, :w_in.shape[1]] = w_in
    w_l = lay_w(wpad, NZ)
    gl = lay_g(mix_g)
    hs = _shards(hT)
    res = _run(nc, [dict(h_in=hs[c], w=w_l, gains=gl) for c in range(NCORES)])
    zT = np.concatenate([r["z_out"] for r in res], axis=1)
    nc = _prog("EG", build_EG)
    ims = []
    for h in range(4):
        sc = np.zeros((128, 4), np.float32)
        sc[:, 0] = -b_gk[h * 128:(h + 1) * 128]
        sc[:, 1] = out_g[:128]
        sc[:, 2] = out_g[128:]
        ims.append(dict(qT=_c(zT[h * 128:(h + 1) * 128]), kT=_c(zT[512 + h * 128:512 + (h + 1) * 128]),
                        v=_c(zT[1024 + h * 256:1024 + (h + 1) * 256].T), gkT=_c(zT[3072:3088]),
                        w2=_c(w_gk2[:, h * 128:(h + 1) * 128]), sc=sc, m64=_M64, ident=_ID))
    res = _run(nc, ims)
    onT = np.concatenate([r["o_out"] for r in res], axis=0)
    nc = _prog("EB", build_EB)
    zv = np.zeros((1024, HALO + TA), np.float32)
    zv[:, HALO:] = zT[3088:4112]
    zs = np.zeros((1024, HALO + TA), np.float32)
    zs[:, HALO:] = zT[4112:5136]
    cs = prep_cs(conv_w, conv_b, ln_g, ln_b)
    wo = lay_w(w_out, KC)
    ims = []
    for c in range(NCORES):
        sl = slice(c * T, (c + 1) * T)
        ims.append(dict(h_in=hs[c], on_in=_c(onT[:, sl]), zg_in=_c(zT[2048:3072, sl]),
                        zv_in=_c(zv[:, c * T:c * T + HALO + T]), zs_in=_c(zs[:, c * T:c * T + HALO + T]),
                        cs=cs, wo=wo))
    res = _run(nc, ims)
    return np.concatenate([r["h_out"] for r in res], axis=1)


def prep_oc_weights(w_in, b_f, q_g, k_g, heads):
    FD = 2048

    def lay(cols):
        return _c(cols.reshape(KC, 128, 128).transpose(1, 0, 2)).reshape(128, KC * 128)
    wqk = np.stack([np.stack([lay(w_in[:, 0 * FD + h * 128:0 * FD + (h + 1) * 128]),
                              lay(w_in[:, 1 * FD + h * 128:1 * FD + (h + 1) * 128])]) for h in heads])
    wvo = np.stack([np.stack([lay(w_in[:, 2 * FD + h * 128:2 * FD + (h + 1) * 128]),
                              lay(w_in[:, 3 * FD + h * 128:3 * FD + (h + 1) * 128])]) for h in heads])
    wf = np.stack([_c(w_in[:, 4 * FD + h].reshape(KC, 128).T) for h in heads])
    sc = np.zeros((2, 128, 4), np.float32)
    for i, h in enumerate(heads):
        sc[i, :, 0] = q_g
        sc[i, :, 1] = k_g
        sc[i, :, 2] = -b_f[h]
    return dict(wqk=_c(wqk.astype(np.float32)), wvo=_c(wvo.astype(np.float32)),
                wf=_c(wf.astype(np.float32)), sc=sc)


def run_odd(hT, mix_g, w_in, b_f, q_g, k_g, w_out):
    nc = _prog("OA", build_OA)
    gl = lay_g(mix_g)
    hs = _shards(hT)
    res = _run(nc, [dict(h_in=hs[c], gains=gl) for c in range(NCORES)])
    u_all = _c(np.concatenate([r["u_out"] for r in res], axis=1))
    nc = _prog("OC", build_OC)
    ims = [dict(u_all=u_all, mask=_M128, **prep_oc_weights(w_in, b_f, q_g, k_g, [2 * c, 2 * c + 1]))
           for c in range(NCORES)]
    res = _run(nc, ims)
    oT = _c(np.concatenate([r["o_out"] for r in res], axis=1).T)
    nc = _prog("OD", build_OD)
    wo = lay_w(w_out, KC)
    ms = _shards(oT)
    res = _run(nc, [dict(h_in=hs[c], m_in=ms[c], wo=wo) for c in range(NCORES)])
    return np.concatenate([r["h_out"] for r in res], axis=1)


def kernel(x, ffn1_norm, ffn1_gate, ffn1_up, ffn1_down, mix_norm, ffn2_norm, ffn2_gate, ffn2_up,
           ffn2_down, ab_w_in, gla_w_gk2, gla_b_gk, gla_out_norm, conv_w, conv_b, conv_ln_g,
           conv_ln_b, ab_w_out, fox_w_in, fox_b_f, fox_q_norm, fox_k_norm, fox_w_out):
    f = lambda a: np.asarray(a, dtype=np.float32)
    hT = _c(f(x)[0].T)
    for l in range(4):
        hT = run_ffn(hT, f(ffn1_norm)[l], f(ffn1_gate)[l], f(ffn1_up)[l], f(ffn1_down)[l])
        if l % 2 == 0:
            e = l // 2
            hT = run_even(hT, f(mix_norm)[l], f(ab_w_in)[e], f(gla_w_gk2)[e], f(gla_b_gk)[e],
                          f(gla_out_norm)[e], f(conv_w)[e], f(conv_b)[e], f(conv_ln_g)[e],
                          f(conv_ln_b)[e], f(ab_w_out)[e])
        else:
            o = l // 2
            hT = run_odd(hT, f(mix_norm)[l], f(fox_w_in)[o], f(fox_b_f)[o], f(fox_q_norm)[o],
                         f(fox_k_norm)[o], f(fox_w_out)[o])
        hT = run_ffn(hT, f(ffn2_norm)[l], f(ffn2_gate)[l], f(ffn2_up)[l], f(ffn2_down)[l])
    return _c(hT.T)[None].astype(np.float32)
```

```python
import contextlib
import numpy as np
import concourse.bass as bass
import concourse.mybir as mybir
from concourse.bass_utils import run_bass_kernel_spmd

F32 = mybir.dt.float32
BF16 = mybir.dt.bfloat16
AF = mybir.ActivationFunctionType
ALU = mybir.AluOpType
AX = mybir.AxisListType

NCORES = 8
D = 2048
KC = D // 128
T = 1024
TB = 512
NTB = T // TB
DFF = 5632
NF = DFF // 128
FR = 4
NR = NF // FR
EPS = 1e-6


class Sched:
    ENGS = ("tensor", "vector", "scalar", "gpsimd", "sync")

    def __init__(self, nc, stack):
        self.nc = nc
        self.stack = stack
        self.ops = {e: [] for e in self.ENGS}
        self.semh = {}
        self.cnt = {}
        for e in self.ENGS:
            self.semh[e] = stack.enter_context(nc.semaphore(f"s_{e}"))
            self.cnt[e] = 0
        self.waited = {e: {} for e in self.ENGS}
        self.res = {}
        self.nwaits = 0

    def _deps(self, eng, reads, writes):
        need = {}

        def add(tok):
            k, v = tok
            if k == eng and v > self.cnt[eng]:
                return
            if k not in self.ENGS:
                v = self.cnt[k]
            if need.get(k, 0) < v:
                need[k] = v

        for r in reads:
            st = self.res.get(r)
            if st is not None and st[0] is not None:
                add(st[0])
        for w in writes:
            st = self.res.get(w)
            if st is not None:
                if st[0] is not None:
                    add(st[0])
                for k, v in st[1].items():
                    add((k, v))
        wd = self.waited[eng]
        for k, v in need.items():
            if wd.get(k, 0) < v:
                wd[k] = v
                h = self.semh[k]
                self.nwaits += 1
                self.ops[eng].append(lambda e, h=h, v=v: e.wait_ge(h, v))

    def _mark(self, tok, reads, writes):
        k, v = tok
        for r in reads:
            st = self.res.get(r)
            if st is None:
                st = self.res[r] = [None, {}]
            if st[1].get(k, 0) < v:
                st[1][k] = v
        for w in writes:
            self.res[w] = [tok, {}]

    def op(self, eng, fn, reads=(), writes=(), inc=True):
        self._deps(eng, reads, writes)
        if inc:
            self.cnt[eng] += 1
            tok = (eng, self.cnt[eng])
            h = self.semh[eng]
            self.ops[eng].append(lambda e, fn=fn, h=h: fn(e).then_inc(h, 1))
        else:
            assert eng == "tensor"
            tok = (eng, self.cnt[eng] + 1)
            self.ops[eng].append(lambda e, fn=fn: fn(e))
        self._mark(tok, reads, writes)
        return tok

    def dma(self, eng, out, in_, semkey, reads=(), writes=(), **kw):
        self._deps(eng, reads, writes)
        if semkey not in self.semh:
            self.semh[semkey] = self.stack.enter_context(self.nc.semaphore(f"d_{semkey}"))
            self.cnt[semkey] = 0
        self.cnt[semkey] += 16
        tok = (semkey, self.cnt[semkey])
        h = self.semh[semkey]
        self.ops[eng].append(
            lambda e, out=out, in_=in_, h=h, kw=kw: e.dma_start(out=out, in_=in_, **kw).then_inc(h, 16))
        self._mark(tok, reads, writes)
        return tok

    def wait_all(self, eng, keys):
        self._deps(eng, list(keys), list(keys))

    def emit(self):
        with self.nc.Block() as block:
            @block.tensor
            def _(e):
                for f in self.ops["tensor"]:
                    f(e)

            @block.vector
            def _(e):
                for f in self.ops["vector"]:
                    f(e)

            @block.scalar
            def _(e):
                for f in self.ops["scalar"]:
                    f(e)

            @block.gpsimd
            def _(e):
                for f in self.ops["gpsimd"]:
                    f(e)

            @block.sync
            def _(e):
                for f in self.ops["sync"]:
                    f(e)


class Ctx:
    def __init__(self, nc, stack):
        self.nc = nc
        self.stack = stack
        self.S = Sched(nc, stack)
        self.ntmp = 0

    def sb(self, name, shape, dt):
        return self.stack.enter_context(self.nc.sbuf_tensor(name, list(shape), dt))

    def ps(self, name, shape, dt=F32):
        return self.stack.enter_context(self.nc.psum_tensor(name, list(shape), dt))


def emit_rmsnorm(C, hT, xnT, gcol, ones_bf, sq, pss, rstd, tag):
    S = C.S
    n = 0
    for tb in range(NTB):
        ts = slice(tb * TB, (tb + 1) * TB)
        for kc in range(KC):
            s = sq[n % len(sq)]
            rs = ("sq", n % len(sq))
            n += 1
            S.op("scalar", lambda e, s=s, kc=kc, ts=ts: e.activation(
                out=s[:], in_=hT[:, kc, ts], func=AF.Square),
                reads=[("hT", kc, tb)], writes=[rs])
            S.op("tensor", lambda e, s=s, kc=kc, tb=tb: e.matmul(
                pss[tb][:], ones_bf[:], s[:], start=(kc == 0), stop=(kc == KC - 1)),
                reads=[rs], writes=[("pss", tb)], inc=(kc == KC - 1))
        S.op("vector", lambda e, tb=tb, ts=ts: e.tensor_scalar(
            out=rstd[:, ts], in0=pss[tb][:], scalar1=1.0 / D, scalar2=EPS,
            op0=ALU.mult, op1=ALU.add),
            reads=[("pss", tb)], writes=[("rstd", tb)])
        S.op("vector", lambda e, ts=ts: e.tensor_scalar(
            out=rstd[:, ts], in0=rstd[:, ts], scalar1=-0.5, scalar2=None,
            op0=ALU.pow),
            reads=[("rstd", tb)], writes=[("rstd", tb)])
        for kc in range(KC):
            S.op("vector", lambda e, kc=kc, ts=ts: e.scalar_tensor_tensor(
                out=xnT[:, kc, ts], in0=hT[:, kc, ts], scalar=gcol(kc),
                in1=rstd[:, ts], op0=ALU.mult, op1=ALU.mult),
                reads=[("hT", kc, tb), ("rstd", tb), "consts"], writes=[("xn", kc, tb)])


def emit_ffn(C, hT, xnT, wg_d, wu_d, wd_d, bufs, tag):
    S = C.S
    wgb, wub, wdb, h1, sg, pg, pu, py = (bufs[k] for k in
                                         ("wg", "wu", "wd", "h1", "sg", "pg", "pu", "py"))
    ny = 0
    for r in range(NR):
        rb = r % 2
        for fi in range(FR):
            f = r * FR + fi
            wb = f % 2
            S.dma("gpsimd", wgb[wb][:], wg_d[f], f"wg{wb}", writes=[("wg", wb)])
            S.dma("gpsimd", wub[wb][:], wu_d[f], f"wu{wb}", writes=[("wu", wb)])
            S.dma("gpsimd", wdb[rb][:, fi, :], wd_d[f], f"wd{rb}", writes=[("wd", rb, fi)])
            for tb in range(NTB):
                ts = slice(tb * TB, (tb + 1) * TB)
                for kc in range(KC):
                    S.op("tensor", lambda e, wb=wb, kc=kc, ts=ts, tb=tb: e.matmul(
                        pg[tb][:], wgb[wb][:, kc * 128:(kc + 1) * 128], xnT[:, kc, ts],
                        start=(kc == 0), stop=(kc == KC - 1)),
                        reads=[("wg", wb), ("xn", kc, tb)], writes=[("pg", tb)],
                        inc=(kc == KC - 1))
                for kc in range(KC):
                    S.op("tensor", lambda e, wb=wb, kc=kc, ts=ts, tb=tb: e.matmul(
                        pu[tb][:], wub[wb][:, kc * 128:(kc + 1) * 128], xnT[:, kc, ts],
                        start=(kc == 0), stop=(kc == KC - 1)),
                        reads=[("wu", wb), ("xn", kc, tb)], writes=[("pu", tb)],
                        inc=(kc == KC - 1))
                S.op("scalar", lambda e, tb=tb: e.activation(
                    out=sg[tb][:], in_=pg[tb][:], func=AF.Silu),
                    reads=[("pg", tb)], writes=[("sg", tb)])
                S.op("vector", lambda e, tb=tb, rb=rb, fi=fi, ts=ts: e.tensor_tensor(
                    out=h1[rb][:, fi, ts], in0=sg[tb][:], in1=pu[tb][:], op=ALU.mult),
                    reads=[("sg", tb), ("pu", tb)], writes=[("h1", rb, fi, tb)])
        for d in range(KC):
            for tb in range(NTB):
                ts = slice(tb * TB, (tb + 1) * TB)
                yb = ny % len(py)
                ny += 1
                for fi in range(FR):
                    S.op("tensor", lambda e, yb=yb, rb=rb, fi=fi, d=d, ts=ts: e.matmul(
                        py[yb][:], wdb[rb][:, fi, d * 128:(d + 1) * 128], h1[rb][:, fi, ts],
                        start=(fi == 0), stop=(fi == FR - 1)),
                        reads=[("wd", rb, fi), ("h1", rb, fi, tb)], writes=[("py", yb)],
                        inc=(fi == FR - 1))
                S.op("vector", lambda e, yb=yb, d=d, ts=ts: e.scalar_tensor_tensor(
                    out=hT[:, d, ts], in0=py[yb][:], scalar=0.5, in1=hT[:, d, ts],
                    op0=ALU.mult, op1=ALU.add),
                    reads=[("py", yb), ("hT", d, tb)], writes=[("hT", d, tb)])


def new_prog():
    nc = bass.Bass("TRN2", target_bir_lowering=False)
    return nc


def emit_norm(C, hT, xnT, gcol, ones_bf, epsc, sq, pss, rstd, psres, dim=D, nkc=KC, inkey="hT"):
    S = C.S
    n = 0
    for tb in range(NTB):
        ts = slice(tb * TB, (tb + 1) * TB)
        for kc in range(nkc):
            s = sq[n % len(sq)]
            rs = ("sq", n % len(sq))
            n += 1
            S.op("scalar", lambda e, s=s, kc=kc, ts=ts: e.activation(
                out=s[:], in_=hT[:, kc, ts], func=AF.Square),
                reads=[(inkey, kc, tb)], writes=[rs])
            S.op("tensor", lambda e, s=s, kc=kc, tb=tb: e.matmul(
                pss[tb][:], ones_bf[:], s[:], start=(kc == 0), stop=(kc == nkc - 1)),
                reads=[rs, "ones"], writes=[psres(tb)], inc=True)
        S.op("scalar", lambda e, tb=tb, ts=ts: e.activation(
            out=rstd[:, ts], in_=pss[tb][:], func=AF.Sqrt, scale=1.0 / dim, bias=epsc[:]),
            reads=[psres(tb), "ones"], writes=[("rstd", tb)])
        S.op("vector", lambda e, ts=ts: e.reciprocal(out=rstd[:, ts], in_=rstd[:, ts]),
             reads=[("rstd", tb)], writes=[("rstd", tb)])
        for kc in range(nkc):
            S.op("vector", lambda e, kc=kc, ts=ts: e.scalar_tensor_tensor(
                out=xnT[:, kc, ts], in0=hT[:, kc, ts], scalar=gcol(kc),
                in1=rstd[:, ts], op0=ALU.mult, op1=ALU.mult),
                reads=[(inkey, kc, tb), ("rstd", tb), "consts"], writes=[("xn", kc, tb)])


def emit_proj_resid(C, hT, xT, w_d, nm, wbuf, py, xkey, nk=KC, wtag="wo"):
    S = C.S
    ny = 0
    for m in range(nm):
        wb = m % 2
        S.dma("gpsimd", wbuf[wb][:, 0:nk * 128], w_d[m], f"{wtag}{wb}", writes=[(wtag, wb)])
        for tb in range(NTB):
            ts = slice(tb * TB, (tb + 1) * TB)
            yb = ny % len(py)
            ny += 1
            for k in range(nk):
                S.op("tensor", lambda e, yb=yb, wb=wb, k=k, ts=ts: e.matmul(
                    py[yb][:], wbuf[wb][:, k * 128:(k + 1) * 128], xT[:, k, ts],
                    start=(k == 0), stop=(k == nk - 1)),
                    reads=[(wtag, wb), (xkey, k, tb)], writes=[("py", yb)], inc=(k == nk - 1))
            S.op("vector", lambda e, yb=yb, m=m, ts=ts: e.scalar_tensor_tensor(
                out=hT[:, m, ts], in0=py[yb][:], scalar=1.0, in1=hT[:, m, ts],
                op0=ALU.mult, op1=ALU.add),
                reads=[("py", yb), ("hT", m, tb)], writes=[("hT", m, tb)])


def load_hT(C, hT, h_in):
    hv = h_in.rearrange("(kc p) t -> p kc t", p=128)
    for kc in range(KC):
        C.S.dma("sync", hT[:, kc, :], hv[:, kc, :], "hin",
                writes=[("hT", kc, tb) for tb in range(NTB)])


def store_hT(C, hT, h_out, key="hT", nkc=KC):
    ov = h_out.rearrange("(kc p) t -> p kc t", p=128)
    for kc in range(nkc):
        C.S.dma("sync", ov[:, kc, :], hT[:, kc, :], "hout",
                reads=[(key, kc, tb) for tb in range(NTB)], writes=[("hout", kc)])
    C.S.wait_all("sync", [("hout", kc) for kc in range(nkc)])


def build_F():
    nc = new_prog()
    h_in = nc.dram_tensor("h_in", [D, T], F32, kind="ExternalInput").ap()
    h_out = nc.dram_tensor("h_out", [D, T], F32, kind="ExternalOutput").ap()
    wg_d = nc.dram_tensor("wg", [NF, 128, KC * 128], F32, kind="ExternalInput").ap()
    wu_d = nc.dram_tensor("wu", [NF, 128, KC * 128], F32, kind="ExternalInput").ap()
    wd_d = nc.dram_tensor("wd", [NF, 128, D], F32, kind="ExternalInput").ap()
    gains_d = nc.dram_tensor("gains", [128, KC], F32, kind="ExternalInput").ap()
    with contextlib.ExitStack() as stack:
        C = Ctx(nc, stack)
        S = C.S
        hT = C.sb("hT", [128, KC, T], F32)
        xnT = C.sb("xnT", [128, KC, T], BF16)
        gains = C.sb("gains_sb", [128, KC], F32)
        ones_bf = C.sb("ones", [128, 128], BF16)
        epsc = C.sb("epsc", [128, 1], F32)
        rstd = C.sb("rstd", [128, T], F32)
        sq = [C.sb(f"sq{i}", [128, TB], BF16) for i in range(3)]
        bufs = dict(
            wg=[C.sb(f"wg{i}", [128, KC * 128], BF16) for i in range(2)],
            wu=[C.sb(f"wu{i}", [128, KC * 128], BF16) for i in range(2)],
            wd=[C.sb(f"wd{i}", [128, FR, D], BF16) for i in range(2)],
            h1=[C.sb(f"h1{i}", [128, FR, T], BF16) for i in range(2)],
            sg=[C.sb(f"sg{i}", [128, TB], F32) for i in range(2)],
            pg=[C.ps(f"pg{i}", [128, TB]) for i in range(2)],
            pu=[C.ps(f"pu{i}", [128, TB]) for i in range(2)],
            py=[C.ps(f"py{i}", [128, TB]) for i in range(4)],
        )
        S.dma("sync", gains[:], gains_d, "c0", writes=["consts"])
        S.op("vector", lambda e: e.memset(epsc[:], EPS), writes=["epsc"])
        S.op("vector", lambda e: e.memset(ones_bf[:], 1.0), writes=["ones"])
        load_hT(C, hT, h_in)
        emit_norm(C, hT, xnT, lambda kc: gains[:, kc:kc + 1], ones_bf, epsc, sq,
                  bufs["py"], rstd, lambda tb: ("py", tb))
        emit_ffn(C, hT, xnT, wg_d, wu_d, wd_d, bufs, "f")
        store_hT(C, hT, h_out)
        S.emit()
    return nc


def build_OA():
    nc = new_prog()
    h_in = nc.dram_tensor("h_in", [D, T], F32, kind="ExternalInput").ap()
    u_out = nc.dram_tensor("u_out", [D, T], BF16, kind="ExternalOutput").ap()
    gains_d = nc.dram_tensor("gains", [128, KC], F32, kind="ExternalInput").ap()
    with contextlib.ExitStack() as stack:
        C = Ctx(nc, stack)
        S = C.S
        hT = C.sb("hT", [128, KC, T], F32)
        xnT = C.sb("xnT", [128, KC, T], BF16)
        gains = C.sb("gains_sb", [128, KC], F32)
        ones_bf = C.sb("ones", [128, 128], BF16)
        epsc = C.sb("epsc", [128, 1], F32)
        rstd = C.sb("rstd", [128, T], F32)
        sq = [C.sb(f"sq{i}", [128, TB], BF16) for i in range(3)]
        py = [C.ps(f"py{i}", [128, TB]) for i in range(2)]
        S.dma("sync", gains[:], gains_d, "c0", writes=["consts"])
        S.op("vector", lambda e: e.memset(epsc[:], EPS), writes=["epsc"])
        S.op("vector", lambda e: e.memset(ones_bf[:], 1.0), writes=["ones"])
        load_hT(C, hT, h_in)
        emit_norm(C, hT, xnT, lambda kc: gains[:, kc:kc + 1], ones_bf, epsc, sq,
                  py, rstd, lambda tb: ("py", tb))
        store_hT(C, xnT, u_out, key="xn")
        S.emit()
    return nc


def build_OD():
    nc = new_prog()
    h_in = nc.dram_tensor("h_in", [D, T], F32, kind="ExternalInput").ap()
    m_in = nc.dram_tensor("m_in", [D, T], F32, kind="ExternalInput").ap()
    h_out = nc.dram_tensor("h_out", [D, T], F32, kind="ExternalOutput").ap()
    wo_d = nc.dram_tensor("wo", [KC, 128, KC * 128], F32, kind="ExternalInput").ap()
    with contextlib.ExitStack() as stack:
        C = Ctx(nc, stack)
        S = C.S
        hT = C.sb("hT", [128, KC, T], F32)
        mT = C.sb("mT", [128, KC, T], BF16)
        wbuf = [C.sb(f"wo{i}", [128, KC * 128], BF16) for i in range(2)]
        py = [C.ps(f"py{i}", [128, TB]) for i in range(4)]
        load_hT(C, hT, h_in)
        mv = m_in.rearrange("(kc p) t -> p kc t", p=128)
        for kc in range(KC):
            S.dma("gpsimd", mT[:, kc, :], mv[:, kc, :], "min",
                  writes=[("mx", kc, tb) for tb in range(NTB)])
        emit_proj_resid(C, hT, mT, wo_d, KC, wbuf, py, "mx")
        store_hT(C, hT, h_out)
        S.emit()
    return nc


TA = 8192
NB = TA // 128
NTBA = TA // TB
DH = 128


def build_OC(nheads=2, nblk=NB):
    nc = new_prog()
    u_in = nc.dram_tensor("u_all", [D, TA], BF16, kind="ExternalInput").ap()
    wqk_d = nc.dram_tensor("wqk", [2, 2, 128, KC * 128], F32, kind="ExternalInput").ap()
    wvo_d = nc.dram_tensor("wvo", [2, 2, 128, KC * 128], F32, kind="ExternalInput").ap()
    wf_d = nc.dram_tensor("wf", [2, 128, KC], F32, kind="ExternalInput").ap()
    sc_d = nc.dram_tensor("sc", [2, 128, 4], F32, kind="ExternalInput").ap()
    mask_d = nc.dram_tensor("mask", [128, 128], F32, kind="ExternalInput").ap()
    o_out = nc.dram_tensor("o_out", [TA, 2 * DH], F32, kind="ExternalOutput").ap()
    uv = u_in.rearrange("(kc p) t -> p kc t", p=128)
    ntb = (nblk * 128) // TB
    with contextlib.ExitStack() as stack:
        C = Ctx(nc, stack)
        S = C.S
        ub = [C.sb(f"ub{i}", [128, KC, TB], BF16) for i in range(2)]
        wq = C.sb("wq", [128, KC * 128], BF16)
        wk = C.sb("wk", [128, KC * 128], BF16)
        wv = C.sb("wv", [128, KC * 128], BF16)
        wo = C.sb("wog", [128, KC * 128], BF16)
        wf = C.sb("wf_sb", [128, KC], BF16)
        sc = C.sb("sc_sb", [128, 4], F32)
        qT = C.sb("qT", [128, TA], BF16)
        kT = C.sb("kT", [128, TA], BF16)
        Va = C.sb("Va", [128, NB, DH + 1], BF16)
        sog = C.sb("sog", [128, NB, DH], BF16)
        lf = C.sb("lf", [1, TA], F32)
        cum = C.sb("cum", [1, NB, 128], F32)
        ones_bf = C.sb("ones", [128, 128], BF16)
        onesf = C.sb("onesf", [1, 128], F32)
        epsc = C.sb("epsc", [128, 1], F32)
        onec = C.sb("onec", [128, 1], F32)
        mask = C.sb("mask_sb", [128, 128], BF16)
        sq = [C.sb(f"sq{i}", [128, TB], BF16) for i in range(2)]
        rr = [C.sb(f"rr{i}", [128, TB], F32) for i in range(2)]
        ef = C.sb("ef", [1, TB], F32)
        cK = C.sb("cK", [128, 128], F32)
        bias = [C.sb(f"bias{i}", [128, NB], F32) for i in range(2)]
        PT = [C.sb(f"PT{i}", [128, 256], BF16) for i in range(6)]
        rinv = [C.sb(f"rinv{i}", [128, 1], F32) for i in range(2)]
        ot = [C.sb(f"ot{i}", [128, DH], F32) for i in range(2)]
        pb = [C.ps(f"pb{i}", [128, TB]) for i in range(8)]

        S.op("vector", lambda e: e.memset(epsc[:], EPS), writes=["epsc"])
        S.op("vector", lambda e: e.memset(onec[:], 1.0), writes=["onec"])
        S.op("vector", lambda e: e.memset(ones_bf[:], 1.0), writes=["ones"])
        S.op("vector", lambda e: e.memset(onesf[:], 1.0), writes=["onesf"])
        S.op("vector", lambda e: e.memset(Va[:, :, DH:DH + 1], 1.0), writes=["Va1"])
        S.dma("gpsimd", mask[:], mask_d, "cmask", writes=["mask"])
        scale = float(DH) ** -0.5
        npt = 0
        nst = 0
        for h in range(nheads):
            S.dma("gpsimd", wq[:], wqk_d[h, 0], "wq", writes=["wq"])
            S.dma("gpsimd", wk[:], wqk_d[h, 1], "wk", writes=["wk"])
            S.dma("gpsimd", wv[:], wvo_d[h, 0], "wv", writes=["wv"])
            S.dma("gpsimd", wo[:], wvo_d[h, 1], "wog", writes=["wog"])
            S.dma("gpsimd", wf[:], wf_d[h], "wf", writes=["wf"])
            S.dma("sync", sc[:], sc_d[h], "sc", writes=["sc"])
            for tb in range(ntb):
                b = tb % 2
                ts = slice(tb * TB, (tb + 1) * TB)
                S.dma("sync", ub[b][:], uv[:, :, ts], f"ub{b}", writes=[("ub", b)])
                for which, (wt, wkey, dst, dkey, gcol) in enumerate(
                        ((wq, "wq", qT, "qT", 0), (wk, "wk", kT, "kT", 1))):
                    pbi = which
                    for kc in range(KC):
                        S.op("tensor", lambda e, pbi=pbi, wt=wt, kc=kc, b=b: e.matmul(
                            pb[pbi][:], wt[:, kc * 128:(kc + 1) * 128], ub[b][:, kc, :],
                            start=(kc == 0), stop=(kc == KC - 1)),
                            reads=[wkey, ("ub", b)], writes=[("pb", pbi)], inc=(kc == KC - 1))
                    S.op("scalar", lambda e, pbi=pbi: e.activation(
                        out=sq[pbi][:], in_=pb[pbi][:], func=AF.Square),
                        reads=[("pb", pbi)], writes=[("sq", pbi)])
                    S.op("tensor", lambda e, pbi=pbi: e.matmul(
                        pb[2][:], ones_bf[:], sq[pbi][:], start=True, stop=True),
                        reads=[("sq", pbi), "ones"], writes=[("pb", 2)])
                    S.op("scalar", lambda e, pbi=pbi: e.activation(
                        out=rr[pbi][:], in_=pb[2][:], func=AF.Sqrt, scale=1.0 / DH, bias=epsc[:]),
                        reads=[("pb", 2), "epsc"], writes=[("rr", pbi)])
                    S.op("vector", lambda e, pbi=pbi: e.reciprocal(out=rr[pbi][:], in_=rr[pbi][:]),
                         reads=[("rr", pbi)], writes=[("rr", pbi)])
                    S.op("vector", lambda e, pbi=pbi, dst=dst, ts=ts, gcol=gcol: e.scalar_tensor_tensor(
                        out=dst[:, ts], in0=pb[pbi][:], scalar=sc[:, gcol:gcol + 1], in1=rr[pbi][:],
                        op0=ALU.mult, op1=ALU.mult),
                        reads=[("pb", pbi), ("rr", pbi), "sc"], writes=[(dkey, tb)])
                for pbi, wt, wkey in ((3, wv, "wv"), (4, wo, "wog")):
                    for sbk in range(4):
                        cs = slice(sbk * 128, (sbk + 1) * 128)
                        for kc in range(KC):
                            S.op("tensor", lambda e, pbi=pbi, wt=wt, kc=kc, b=b, cs=cs: e.matmul(
                                pb[pbi][:, cs], ub[b][:, kc, cs], wt[:, kc * 128:(kc + 1) * 128],
                                start=(kc == 0), stop=(kc == KC - 1)),
                                reads=[wkey, ("ub", b)], writes=[("pb", pbi)],
                                inc=(kc == KC - 1))
                for sbk in range(4):
                    cs = slice(sbk * 128, (sbk + 1) * 128)
                    blk = tb * 4 + sbk
                    S.op("vector", lambda e, cs=cs, blk=blk: e.tensor_copy(
                        out=Va[:, blk, 0:DH], in_=pb[3][:, cs]),
                        reads=[("pb", 3)], writes=[("Va", blk)])
                    S.op("scalar", lambda e, cs=cs, blk=blk: e.activation(
                        out=sog[:, blk, :], in_=pb[4][:, cs], func=AF.Sigmoid),
                        reads=[("pb", 4)], writes=[("sog", blk)])
                for kc in range(KC):
                    S.op("tensor", lambda e, kc=kc, b=b: e.matmul(
                        pb[5][0:1, :], wf[:, kc:kc + 1], ub[b][:, kc, :],
                        start=(kc == 0), stop=(kc == KC - 1)),
                        reads=["wf", ("ub", b)], writes=[("pb", 5)], inc=(kc == KC - 1))
                S.op("scalar", lambda e: e.activation(
                    out=ef[:], in_=pb[5][0:1, :], func=AF.Exp, scale=-1.0, bias=sc[0:1, 2:3]),
                    reads=[("pb", 5), "sc"], writes=["ef"])
                S.op("scalar", lambda e: e.activation(
                    out=ef[:], in_=ef[:], func=AF.Ln, bias=onec[0:1, :]),
                    reads=["ef", "onec"], writes=["ef"])
                S.op("vector", lambda e, ts=ts: e.tensor_scalar(
                    out=lf[0:1, ts], in0=ef[:], scalar1=-1.0, scalar2=None, op0=ALU.mult),
                    reads=["ef"], writes=[("lf", tb)])
            for blk in range(nblk):
                init = 0.0 if blk == 0 else cum[0:1, blk - 1, 127:128]
                S.op("vector", lambda e, blk=blk, init=init: e.tensor_tensor_scan(
                    out=cum[0:1, blk, :], data0=onesf[0:1, :], data1=lf[0:1, blk * 128:(blk + 1) * 128],
                    initial=init, op0=ALU.mult, op1=ALU.add),
                    reads=[("lf", blk // 4), "onesf", "cum"], writes=["cum"])
            for j in range(nblk):
                S.op("tensor", lambda e, j=j: e.matmul(
                    pb[5][:, j:j + 1], cum[0:1, j, :], onesf[0:1, 0:1], start=True, stop=True),
                    reads=["cum", "onesf"], writes=[("pb", 5)], inc=(j == nblk - 1))
            S.op("tensor", lambda e: e.matmul(
                pb[5][:, 64:64 + nblk], onesf[0:1, :], cum[0:1, 0:nblk, 127], start=True, stop=True),
                reads=["cum", "onesf"], writes=[("pb", 5)])
            S.op("vector", lambda e: e.tensor_copy(out=cK[:, 0:nblk], in_=pb[5][:, 0:nblk]),
                 reads=[("pb", 5)], writes=["cK"])
            S.op("vector", lambda e: e.tensor_copy(out=cK[:, 64:64 + nblk], in_=pb[5][:, 64:64 + nblk]),
                 reads=[("pb", 5), "cK"], writes=["cK"])
            items = []
            for I in range(nblk // 2):
                i0, i1 = 2 * I, 2 * I + 1
                pos = (4 + 2 * (I % 2), 5 + 2 * (I % 2))
                for j in range(2 * I):
                    items.append(dict(I=I, wide=True, k=j, q0=i0, pvs=[(pos[0], 0, j == 0, False), (pos[1], 1, j == 0, False)],
                                      mask=False, first=(j == 0), lastq=[]))
                items.append(dict(I=I, wide=False, k=i0, q0=i0, pvs=[(pos[0], 0, I == 0, True)], mask=True,
                                  first=(I == 0), lastq=[(i0, pos[0])]))
                items.append(dict(I=I, wide=False, k=i0, q0=i1, pvs=[(pos[1], 0, I == 0, False)], mask=False,
                                  first=False, lastq=[]))
                items.append(dict(I=I, wide=False, k=i1, q0=i1, pvs=[(pos[1], 0, False, True)], mask=True,
                                  first=False, lastq=[(i1, pos[1])]))
            NPT = len(PT)
            DEPTH = 3

            def stage_a(n, it):
                I = it["I"]
                bi = I % 2
                if it["first"]:
                    i1 = 2 * I + 1
                    S.op("vector", lambda e, bi=bi, i1=i1: e.tensor_scalar(
                        out=bias[bi][:, 0:i1 + 1], in0=cK[:, 0:i1 + 1], scalar1=-1.0,
                        scalar2=cK[:, 64 + i1:65 + i1], op0=ALU.mult, op1=ALU.add),
                        reads=["cK"], writes=[("bias", bi)])
                w = 256 if it["wide"] else 128
                ks = slice(it["k"] * 128, (it["k"] + 1) * 128)
                qs_ = slice(it["q0"] * 128, it["q0"] * 128 + w)
                sb_ = n % 4
                p = n % NPT
                S.op("tensor", lambda e, sb_=sb_, ks=ks, qs_=qs_, w=w: e.matmul(
                    pb[sb_][:, 0:w], kT[:, ks], qT[:, qs_], start=True, stop=True),
                    reads=[("kT", it["k"] // 4), ("qT", it["q0"] // 4), ("qT", (it["q0"] + w // 128 - 1) // 4)],
                    writes=[("pb", sb_)])
                S.op("scalar", lambda e, sb_=sb_, p=p, bi=bi, k=it["k"], w=w: e.activation(
                    out=PT[p][:, 0:w], in_=pb[sb_][:, 0:w], func=AF.Exp, scale=scale,
                    bias=bias[bi][:, k:k + 1]),
                    reads=[("pb", sb_), ("bias", bi)], writes=[("PT", p)])
                if it["mask"]:
                    S.op("vector", lambda e, p=p: e.tensor_tensor(
                        out=PT[p][:, 0:128], in0=PT[p][:, 0:128], in1=mask[:], op=ALU.mult),
                        reads=[("PT", p), "mask"], writes=[("PT", p)])

            def stage_b(n, it):
                p = n % NPT
                for (po, qq, st, sp) in it["pvs"]:
                    S.op("tensor", lambda e, po=po, p=p, k=it["k"], qq=qq, st=st, sp=sp: e.matmul(
                        pb[po][:, 0:DH + 1], PT[p][:, qq * 128:(qq + 1) * 128], Va[:, k, :], start=st, stop=sp),
                        reads=[("PT", p), ("Va", it["k"]), "Va1"], writes=[("pb", po)])
                for (i, po) in it["lastq"]:
                    ob = i % 2
                    qs = slice(i * 128, (i + 1) * 128)
                    S.op("vector", lambda e, po=po, ob=ob: e.reciprocal(
                        out=rinv[ob][:], in_=pb[po][:, DH:DH + 1]),
                        reads=[("pb", po)], writes=[("rinv", ob)])
                    S.op("vector", lambda e, po=po, ob=ob, i=i: e.scalar_tensor_tensor(
                        out=ot[ob][:], in0=pb[po][:, 0:DH], scalar=rinv[ob][:, 0:1], in1=sog[:, i, :],
                        op0=ALU.mult, op1=ALU.mult),
                        reads=[("pb", po), ("rinv", ob), ("sog", i)], writes=[("ot", ob)])
                    S.dma("sync", o_out[qs, h * DH:(h + 1) * DH], ot[ob][:], f"oo{ob}",
                          reads=[("ot", ob)], writes=[("oout", ob)])

            for n in range(len(items) + DEPTH):
                if n < len(items):
                    stage_a(n, items[n])
                if n - DEPTH >= 0:
                    stage_b(n - DEPTH, items[n - DEPTH])
        S.wait_all("sync", [("oout", 0), ("oout", 1)])
        S.emit()
    return nc


NZ = 41


def build_EA():
    nc = new_prog()
    h_in = nc.dram_tensor("h_in", [D, T], F32, kind="ExternalInput").ap()
    z_out = nc.dram_tensor("z_out", [NZ * 128, T], F32, kind="ExternalOutput").ap()
    w_d = nc.dram_tensor("w", [NZ, 128, KC * 128], F32, kind="ExternalInput").ap()
    gains_d = nc.dram_tensor("gains", [128, KC], F32, kind="ExternalInput").ap()
    with contextlib.ExitStack() as stack:
        C = Ctx(nc, stack)
        S = C.S
        hT = C.sb("hT", [128, KC, T], F32)
        xnT = C.sb("xnT", [128, KC, T], BF16)
        gains = C.sb("gains_sb", [128, KC], F32)
        ones_bf = C.sb("ones", [128, 128], BF16)
        epsc = C.sb("epsc", [128, 1], F32)
        rstd = C.sb("rstd", [128, T], F32)
        sq = [C.sb(f"sq{i}", [128, TB], BF16) for i in range(3)]
        wbuf = [C.sb(f"w{i}", [128, KC * 128], BF16) for i in range(2)]
        stg = [C.sb(f"stg{i}", [128, TB], F32) for i in range(4)]
        py = [C.ps(f"py{i}", [128, TB]) for i in range(4)]
        S.dma("sync", gains[:], gains_d, "c0", writes=["consts"])
        S.op("vector", lambda e: e.memset(epsc[:], EPS), writes=["epsc"])
        S.op("vector", lambda e: e.memset(ones_bf[:], 1.0), writes=["ones"])
        load_hT(C, hT, h_in)
        emit_norm(C, hT, xnT, lambda kc: gains[:, kc:kc + 1], ones_bf, epsc, sq,
                  py, rstd, lambda tb: ("py", tb))
        n = 0
        for m in range(NZ):
            wb = m % 2
            S.dma("gpsimd", wbuf[wb][:], w_d[m], f"w{wb}", writes=[("w", wb)])
            for tb in range(NTB):
                ts = slice(tb * TB, (tb + 1) * TB)
                yb = n % 4
                n += 1
                for k in range(KC):
                    S.op("tensor", lambda e, yb=yb, wb=wb, k=k, ts=ts: e.matmul(
                        py[yb][:], wbuf[wb][:, k * 128:(k + 1) * 128], xnT[:, k, ts],
                        start=(k == 0), stop=(k == KC - 1)),
                        reads=[("w", wb), ("xn", k, tb)], writes=[("py", yb)], inc=(k == KC - 1))
                eng = "vector" if yb % 2 == 0 else "scalar"
                if eng == "vector":
                    S.op("vector", lambda e, yb=yb: e.tensor_copy(out=stg[yb][:], in_=py[yb][:]),
                         reads=[("py", yb)], writes=[("stg", yb)])
                else:
                    S.op("scalar", lambda e, yb=yb: e.copy(out=stg[yb][:], in_=py[yb][:]),
                         reads=[("py", yb)], writes=[("stg", yb)])
                S.dma("sync", z_out[m * 128:(m + 1) * 128, ts], stg[yb][:], f"zo{yb}",
                      reads=[("stg", yb)], writes=[("zo", yb)])
        S.wait_all("sync", [("zo", i) for i in range(4)])
        S.emit()
    return nc


GC = 64
NCH = T // GC
DK = 128
DV = 256


def build_EG(nsuper=TA // T):
    nc = new_prog()
    q_in = nc.dram_tensor("qT", [DK, TA], F32, kind="ExternalInput").ap()
    k_in = nc.dram_tensor("kT", [DK, TA], F32, kind="ExternalInput").ap()
    v_in = nc.dram_tensor("v", [TA, DV], F32, kind="ExternalInput").ap()
    gk_in = nc.dram_tensor("gkT", [16, TA], F32, kind="ExternalInput").ap()
    w2_d = nc.dram_tensor("w2", [16, DK], F32, kind="ExternalInput").ap()
    sc_d = nc.dram_tensor("sc", [128, 4], F32, kind="ExternalInput").ap()
    m64_d = nc.dram_tensor("m64", [GC, GC], F32, kind="ExternalInput").ap()
    id_d = nc.dram_tensor("ident", [128, 128], F32, kind="ExternalInput").ap()
    o_out = nc.dram_tensor("o_out", [DV, TA], F32, kind="ExternalOutput").ap()
    vv = v_in.rearrange("(n p) c -> p n c", p=GC)
    with contextlib.ExitStack() as stack:
        C = Ctx(nc, stack)
        S = C.S
        qf = C.sb("qf", [128, T], F32)
        kf = C.sb("kf", [128, T], F32)
        gk = C.sb("gk", [16, T], BF16)
        vs = C.sb("vs", [GC, NCH, DV], BF16)
        w2 = C.sb("w2_sb", [16, DK], BF16)
        sc = C.sb("sc_sb", [128, 4], F32)
        m64 = C.sb("m64_sb", [GC, GC], F32)
        ident = C.sb("ident_sb", [128, 128], BF16)
        ones_bf = C.sb("ones", [128, 128], BF16)
        epsc = C.sb("epsc", [128, 1], F32)
        onec = C.sb("onec", [128, 1], F32)
        la = C.sb("la", [128, T], F32)
        cum = C.sb("cum", [128, T], F32)
        nref = C.sb("nref", [128, NCH], F32)
        ee = [C.sb(f"ee{i}", [128, GC], F32) for i in range(4)]
        qt = C.sb("qt", [128, T], BF16)
        kt = C.sb("kt", [128, T], BF16)
        qi = C.sb("qi", [128, T], BF16)
        ko = C.sb("ko", [128, T], BF16)
        dec = C.sb("dec", [128, NCH], F32)
        kot = [C.sb(f"kot{i}", [GC, DK], BF16) for i in range(2)]
        ats = [C.sb(f"ats{i}", [GC, GC], BF16) for i in range(2)]
        St = C.sb("St", [128, DV], F32)
        Sb = [C.sb(f"Sb{i}", [128, DV], BF16) for i in range(2)]
        oT = C.sb("oT", [128, 2, T], F32)
        onT = C.sb("onT", [128, 2, T], F32)
        rstd = C.sb("rstd", [128, T], F32)
        sq = [C.sb(f"sq{i}", [128, TB], BF16) for i in range(3)]
        pb = [C.ps(f"pb{i}", [128, TB]) for i in range(8)]
        scale = float(DK) ** -0.5
        onesrow = C.sb("onesrow", [128, GC], F32)
        S.op("vector", lambda e: e.memset(onesrow[:], 1.0), writes=["onesrow"])

        S.op("vector", lambda e: e.memset(epsc[:], EPS), writes=["epsc"])
        S.op("vector", lambda e: e.memset(onec[:], 1.0), writes=["onec"])
        S.op("vector", lambda e: e.memset(ones_bf[:], 1.0), writes=["ones"])
        S.op("vector", lambda e: e.memset(St[:], 0.0), writes=["St"])
        S.op("vector", lambda e: e.memset(Sb[0][:], 0.0), writes=[("Sb", 0)])
        S.dma("gpsimd", w2[:], w2_d, "c1", writes=["w2"])
        S.dma("gpsimd", ident[:], id_d, "c2", writes=["ident"])
        S.dma("sync", sc[:], sc_d, "c3", writes=["sc", "consts"])
        S.dma("sync", m64[:], m64_d, "c4", writes=["m64"])
        nsb = 0
        for sp in range(nsuper):
            t0 = sp * T
            S.dma("sync", qf[:], q_in[:, t0:t0 + T], "qf", writes=["qf"])
            S.dma("sync", kf[:], k_in[:, t0:t0 + T], "kf", writes=["kf"])
            S.dma("gpsimd", gk[:], gk_in[:, t0:t0 + T], "gk", writes=["gk"])
            S.dma("gpsimd", vs[:], vv[:, sp * NCH:(sp + 1) * NCH, :], "vs", writes=["vs"])
            for tb in range(NTB):
                ts = slice(tb * TB, (tb + 1) * TB)
                S.op("tensor", lambda e, ts=ts: e.matmul(
                    pb[0][:], w2[:], gk[:, ts], start=True, stop=True),
                    reads=["w2", "gk"], writes=[("pb", 0)])
                S.op("scalar", lambda e, ts=ts: e.activation(
                    out=la[:, ts], in_=pb[0][:], func=AF.Exp, scale=-1.0, bias=sc[:, 0:1]),
                    reads=[("pb", 0), "sc"], writes=[("la", tb)])
                S.op("scalar", lambda e, ts=ts: e.activation(
                    out=la[:, ts], in_=la[:, ts], func=AF.Ln, bias=onec[:]),
                    reads=[("la", tb), "onec"], writes=[("la", tb)])
                S.op("vector", lambda e, ts=ts: e.tensor_scalar(
                    out=la[:, ts], in0=la[:, ts], scalar1=-1.0 / 16.0, scalar2=None, op0=ALU.mult),
                    reads=[("la", tb)], writes=[("la", tb)])
            for n in range(NCH):
                cs = slice(n * GC, (n + 1) * GC)
                S.op("vector", lambda e, cs=cs: e.tensor_tensor_scan(
                    out=cum[:, cs], data0=onesrow[:, 0:GC], data1=la[:, cs],
                    initial=0.0, op0=ALU.mult, op1=ALU.add),
                    reads=[("la", (n * GC) // TB), "onesrow"], writes=[("cum", n)])
            S.op("vector", lambda e: e.tensor_scalar(
                out=nref[:], in0=cum[:, GC // 2::GC], scalar1=-1.0, scalar2=None, op0=ALU.mult),
                reads=[("cum", n) for n in range(NCH)], writes=["nref"])
            S.op("scalar", lambda e: e.activation(
                out=dec[:], in_=cum[:, GC - 1::GC], func=AF.Exp),
                reads=[("cum", n) for n in range(NCH)], writes=["dec"])
            for n in range(NCH):
                cs = slice(n * GC, (n + 1) * GC)
                rcol = n * GC + GC // 2
                lcol = n * GC + GC - 1
                S.op("scalar", lambda e, cs=cs, n=n: e.activation(
                    out=ee[0][:], in_=cum[:, cs], func=AF.Exp, bias=nref[:, n:n + 1]),
                    reads=[("cum", n), "nref"], writes=[("ee", 0)])
                S.op("scalar", lambda e, cs=cs, rcol=rcol: e.activation(
                    out=ee[1][:], in_=cum[:, cs], func=AF.Exp, scale=-1.0, bias=cum[:, rcol:rcol + 1]),
                    reads=[("cum", n)], writes=[("ee", 1)])
                S.op("scalar", lambda e, cs=cs: e.activation(
                    out=ee[2][:], in_=cum[:, cs], func=AF.Exp),
                    reads=[("cum", n)], writes=[("ee", 2)])
                S.op("scalar", lambda e, cs=cs, lcol=lcol: e.activation(
                    out=ee[3][:], in_=cum[:, cs], func=AF.Exp, scale=-1.0, bias=cum[:, lcol:lcol + 1]),
                    reads=[("cum", n)], writes=[("ee", 3)])
                S.op("vector", lambda e, cs=cs: e.scalar_tensor_tensor(
                    out=qt[:, cs], in0=qf[:, cs], scalar=scale, in1=ee[0][:], op0=ALU.mult, op1=ALU.mult),
                    reads=["qf", ("ee", 0)], writes=[("qt", n)])
                S.op("vector", lambda e, cs=cs: e.tensor_tensor(
                    out=kt[:, cs], in0=kf[:, cs], in1=ee[1][:], op=ALU.mult),
                    reads=["kf", ("ee", 1)], writes=[("kt", n)])
                S.op("vector", lambda e, cs=cs: e.scalar_tensor_tensor(
                    out=qi[:, cs], in0=qf[:, cs], scalar=scale, in1=ee[2][:], op0=ALU.mult, op1=ALU.mult),
                    reads=["qf", ("ee", 2)], writes=[("qi", n)])
                S.op("vector", lambda e, cs=cs: e.tensor_tensor(
                    out=ko[:, cs], in0=kf[:, cs], in1=ee[3][:], op=ALU.mult),
                    reads=["kf", ("ee", 3)], writes=[("ko", n)])
            for n in range(NCH):
                cs = slice(n * GC, (n + 1) * GC)
                b2 = n % 2
                sbi = nsb % 2
                S.op("tensor", lambda e, cs=cs: e.matmul(
                    pb[1][0:GC, 0:GC], kt[:, cs], qt[:, cs], start=True, stop=True),
                    reads=[("kt", n), ("qt", n)], writes=[("pb", 1)])
                S.op("vector", lambda e, b2=b2: e.tensor_tensor(
                    out=ats[b2][:], in0=pb[1][0:GC, 0:GC], in1=m64[:], op=ALU.mult),
                    reads=[("pb", 1), "m64"], writes=[("ats", b2)])
                S.op("tensor", lambda e, cs=cs: e.matmul(
                    pb[2][0:GC, 0:DK], ko[:, cs], ident[:], start=True, stop=True),
                    reads=[("ko", n), "ident"], writes=[("pb", 2)])
                S.op("scalar", lambda e, b2=b2: e.copy(out=kot[b2][:], in_=pb[2][0:GC, 0:DK]),
                     reads=[("pb", 2)], writes=[("kot", b2)])
                for c in range(2):
                    ob = 3 if c == 0 else 7
                    S.op("tensor", lambda e, c=c, b2=b2, n=n, ob=ob: e.matmul(
                        pb[ob][:, 0:GC], vs[:, n, c * 128:(c + 1) * 128], ats[b2][:],
                        start=True, stop=False),
                        reads=["vs", ("ats", b2)], writes=[("pb", ob)], inc=False)
                    S.op("tensor", lambda e, c=c, sbi=sbi, cs=cs, ob=ob: e.matmul(
                        pb[ob][:, 0:GC], Sb[sbi][:, c * 128:(c + 1) * 128], qi[:, cs],
                        start=False, stop=True),
                        reads=[("Sb", sbi), ("qi", n)], writes=[("pb", ob)])
                    S.op("vector", lambda e, c=c, cs=cs, ob=ob: e.tensor_copy(
                        out=oT[:, c, cs], in_=pb[ob][:, 0:GC]),
                        reads=[("pb", ob)], writes=[("oT", c, (n * GC) // TB)])
                S.op("tensor", lambda e, b2=b2, n=n: e.matmul(
                    pb[4][:, 0:DV], kot[b2][:], vs[:, n, :], start=True, stop=True),
                    reads=[("kot", b2), "vs"], writes=[("pb", 4)])
                S.op("vector", lambda e, n=n: e.scalar_tensor_tensor(
                    out=St[:], in0=St[:], scalar=dec[:, n:n + 1], in1=pb[4][:, 0:DV],
                    op0=ALU.mult, op1=ALU.add),
                    reads=["St", "dec", ("pb", 4)], writes=["St"])
                nsb += 1
                S.op("scalar", lambda e, nb=nsb % 2: e.copy(out=Sb[nb][:], in_=St[:]),
                     reads=["St"], writes=[("Sb", nsb % 2)])
            emit_norm(C, oT, onT, lambda c: sc[:, 1 + c:2 + c], ones_bf, epsc, sq,
                      [pb[5], pb[6]], rstd, lambda tb: ("pb", 5 + tb), dim=DV, nkc=2, inkey="oT")
            ov = o_out.rearrange("(c p) t -> p c t", p=128)
            for c in range(2):
                S.dma("sync", ov[:, c, t0:t0 + T], onT[:, c, :], "oo",
                      reads=[("xn", c, tb) for tb in range(NTB)], writes=[("oo", c)])
        S.wait_all("sync", [("oo", 0), ("oo", 1)])
        S.emit()
    return nc


def ones_f_row(C):
    if not hasattr(C, "_onesrow"):
        C._onesrow = C.sb("onesrow", [128, GC], F32)
        C.S.op("vector", lambda e: e.memset(C._onesrow[:], 1.0), writes=["onesrow"])
    return C._onesrow


CW = 31
HALO = CW - 1
NCC = 8


def build_EB():
    nc = new_prog()
    h_in = nc.dram_tensor("h_in", [D, T], F32, kind="ExternalInput").ap()
    on_in = nc.dram_tensor("on_in", [1024, T], F32, kind="ExternalInput").ap()
    zg_in = nc.dram_tensor("zg_in", [1024, T], F32, kind="ExternalInput").ap()
    zv_in = nc.dram_tensor("zv_in", [1024, HALO + T], F32, kind="ExternalInput").ap()
    zs_in = nc.dram_tensor("zs_in", [1024, HALO + T], F32, kind="ExternalInput").ap()
    cs_d = nc.dram_tensor("cs", [128, NCC, 34], F32, kind="ExternalInput").ap()
    wo_d = nc.dram_tensor("wo", [KC, 128, KC * 128], F32, kind="ExternalInput").ap()
    h_out = nc.dram_tensor("h_out", [D, T], F32, kind="ExternalOutput").ap()
    with contextlib.ExitStack() as stack:
        C = Ctx(nc, stack)
        S = C.S
        hT = C.sb("hT", [128, KC, T], F32)
        mixT = C.sb("mixT", [128, KC, T], BF16)
        yT = C.sb("yT", [128, NCC, T], F32)
        cs = C.sb("cs_sb", [128, NCC, 34], F32)
        sa = [C.sb(f"sa{i}", [128, HALO + T], F32) for i in range(2)]
        sb_ = [C.sb(f"sb{i}", [128, HALO + T], F32) for i in range(2)]
        cc = [C.sb(f"cc{i}", [128, HALO + T], F32) for i in range(2)]
        ones_bf = C.sb("ones", [128, 128], BF16)
        epsc = C.sb("epsc", [128, 1], F32)
        ybf = [C.sb(f"ybf{i}", [128, TB], BF16) for i in range(3)]
        mean = C.sb("mean", [128, T], F32)
        var = C.sb("var", [128, T], F32)
        rstd = C.sb("rstd", [128, T], F32)
        tmp = [C.sb(f"tmp{i}", [128, T], F32) for i in range(2)]
        wbuf = [C.sb(f"wo{i}", [128, KC * 128], BF16) for i in range(2)]
        py = [C.ps(f"py{i}", [128, TB]) for i in range(4)]
        ps1 = [C.ps(f"ps1{i}", [128, TB]) for i in range(2)]
        ps2 = [C.ps(f"ps2{i}", [128, TB]) for i in range(2)]
        S.op("vector", lambda e: e.memset(epsc[:], EPS), writes=["epsc"])
        S.op("vector", lambda e: e.memset(ones_bf[:], 1.0), writes=["ones"])
        S.dma("sync", cs[:], cs_d, "c0", writes=["cs"])
        load_hT(C, hT, h_in)
        for c in range(NCC):
            b = c % 2
            rs = slice(c * 128, (c + 1) * 128)
            S.dma("sync", sa[b][:, 0:T], on_in[rs, :], f"sa{b}", writes=[("sa", b)])
            S.dma("sync", sb_[b][:, 0:T], zg_in[rs, :], f"sb{b}", writes=[("sb", b)])
            S.op("scalar", lambda e, b=b: e.activation(out=sb_[b][:, 0:T], in_=sb_[b][:, 0:T], func=AF.Silu),
                 reads=[("sb", b)], writes=[("sb", b)])
            S.op("vector", lambda e, b=b, c=c: e.tensor_tensor(
                out=mixT[:, c, :], in0=sa[b][:, 0:T], in1=sb_[b][:, 0:T], op=ALU.mult),
                reads=[("sa", b), ("sb", b)], writes=[("mx", c, 0), ("mx", c, 1)])
        for c in range(NCC):
            b = c % 2
            rs = slice(c * 128, (c + 1) * 128)
            S.dma("sync", sa[b][:], zv_in[rs, :], f"sa{b}", writes=[("sa", b)])
            S.dma("sync", sb_[b][:], zs_in[rs, :], f"sb{b}", writes=[("sb", b)])
            S.op("scalar", lambda e, b=b: e.activation(out=sb_[b][:], in_=sb_[b][:], func=AF.Sigmoid),
                 reads=[("sb", b)], writes=[("sb", b)])
            S.op("vector", lambda e, b=b: e.tensor_tensor(
                out=cc[b][:], in0=sa[b][:], in1=sb_[b][:], op=ALU.mult),
                reads=[("sa", b), ("sb", b)], writes=[("cc", b)])
            S.op("vector", lambda e, b=b, c=c: e.tensor_scalar(
                out=yT[:, c, :], in0=cc[b][:, 0:T], scalar1=cs[:, c, 0:1], scalar2=cs[:, c, 31:32],
                op0=ALU.mult, op1=ALU.add),
                reads=[("cc", b), "cs"], writes=[("y", c)])
            for j in range(1, CW):
                S.op("vector", lambda e, b=b, c=c, j=j: e.scalar_tensor_tensor(
                    out=yT[:, c, :], in0=cc[b][:, j:j + T], scalar=cs[:, c, j:j + 1], in1=yT[:, c, :],
                    op0=ALU.mult, op1=ALU.add),
                    reads=[("cc", b), "cs", ("y", c)], writes=[("y", c)])
        nb = 0
        for tb in range(NTB):
            ts = slice(tb * TB, (tb + 1) * TB)
            for c in range(NCC):
                i1 = nb % 3
                nb += 1
                S.op("scalar", lambda e, i1=i1, c=c, ts=ts: e.copy(out=ybf[i1][:], in_=yT[:, c, ts]),
                     reads=[("y", c)], writes=[("ybf", i1)])
                S.op("tensor", lambda e, i1=i1, tb=tb, c=c: e.matmul(
                    ps1[tb][:], ones_bf[:], ybf[i1][:], start=(c == 0), stop=(c == NCC - 1)),
                    reads=[("ybf", i1), "ones"], writes=[("ps1", tb)])
                i2 = nb % 3
                nb += 1
                S.op("scalar", lambda e, i2=i2, c=c, ts=ts: e.activation(
                    out=ybf[i2][:], in_=yT[:, c, ts], func=AF.Square),
                    reads=[("y", c)], writes=[("ybf", i2)])
                S.op("tensor", lambda e, i2=i2, tb=tb, c=c: e.matmul(
                    ps2[tb][:], ones_bf[:], ybf[i2][:], start=(c == 0), stop=(c == NCC - 1)),
                    reads=[("ybf", i2), "ones"], writes=[("ps2", tb)])
            S.op("vector", lambda e, tb=tb, ts=ts: e.tensor_scalar(
                out=mean[:, ts], in0=ps1[tb][:], scalar1=1.0 / 1024, scalar2=None, op0=ALU.mult),
                reads=[("ps1", tb)], writes=[("mean", tb)])
            S.op("vector", lambda e, ts=ts: e.tensor_tensor(
                out=var[:, ts], in0=mean[:, ts], in1=mean[:, ts], op=ALU.mult),
                reads=[("mean", tb)], writes=[("var", tb)])
            S.op("vector", lambda e, tb=tb, ts=ts: e.scalar_tensor_tensor(
                out=var[:, ts], in0=ps2[tb][:], scalar=1.0 / 1024, in1=var[:, ts],
                op0=ALU.mult, op1=ALU.subtract),
                reads=[("ps2", tb), ("var", tb)], writes=[("var", tb)])
            S.op("scalar", lambda e, ts=ts: e.activation(
                out=rstd[:, ts], in_=var[:, ts], func=AF.Sqrt, bias=epsc[:]),
                reads=[("var", tb), "epsc"], writes=[("rstd", tb)])
            S.op("vector", lambda e, ts=ts: e.reciprocal(out=rstd[:, ts], in_=rstd[:, ts]),
                 reads=[("rstd", tb)], writes=[("rstd", tb)])
        for c in range(NCC):
            b = c % 2
            S.op("vector", lambda e, b=b, c=c: e.tensor_tensor(
                out=tmp[b][:], in0=yT[:, c, :], in1=mean[:], op=ALU.subtract),
                reads=[("y", c), ("mean", 0), ("mean", 1)], writes=[("tmp", b)])
            S.op("vector", lambda e, b=b: e.tensor_tensor(
                out=tmp[b][:], in0=tmp[b][:], in1=rstd[:], op=ALU.mult),
                reads=[("tmp", b), ("rstd", 0), ("rstd", 1)], writes=[("tmp", b)])
            S.op("scalar", lambda e, b=b, c=c: e.activation(
                out=mixT[:, NCC + c, :], in_=tmp[b][:], func=AF.Silu,
                scale=cs[:, c, 32:33], bias=cs[:, c, 33:34]),
                reads=[("tmp", b), "cs"], writes=[("mx", NCC + c, 0), ("mx", NCC + c, 1)])
        emit_proj_resid(C, hT, mixT, wo_d, KC, wbuf, py, "mx")
        store_hT(C, hT, h_out)
        S.emit()
    return nc


import ml_dtypes

_PROGS = {}


def _prog(name, fn):
    if name not in _PROGS:
        _PROGS[name] = fn()
    return _PROGS[name]


def _run(nc, in_maps):
    res = run_bass_kernel_spmd(nc, in_maps, core_ids=list(range(len(in_maps))))
    return res.results


def _c(a):
    return np.ascontiguousarray(a)


def lay_w(w, nm):
    K = w.shape[0]
    return _c(w.reshape(K // 128, 128, nm, 128).transpose(2, 1, 0, 3)).reshape(nm, 128, K)


def lay_g(g):
    return _c(g.reshape(-1, 128).T)


def _shards(aT):
    return [_c(aT[:, c * T:(c + 1) * T]) for c in range(NCORES)]


def run_ffn(hT, norm_g, wg, wu, wd):
    nc = _prog("F", build_F)
    wg_l, wu_l = lay_w(wg, NF), lay_w(wu, NF)
    wd_l = _c(wd.reshape(NF, 128, D))
    gl = lay_g(norm_g)
    hs = _shards(hT)
    res = _run(nc, [dict(h_in=hs[c], wg=wg_l, wu=wu_l, wd=wd_l, gains=gl) for c in range(NCORES)])
    return np.concatenate([r["h_out"] for r in res], axis=1)


def prep_cs(conv_w, conv_b, ln_g, ln_b):
    cs = np.zeros((128, NCC, 34), np.float32)
    cs[:, :, 0:31] = conv_w.T.reshape(NCC, 128, 31).transpose(1, 0, 2)
    cs[:, :, 31] = conv_b.reshape(NCC, 128).T
    cs[:, :, 32] = ln_g.reshape(NCC, 128).T
    cs[:, :, 33] = ln_b.reshape(NCC, 128).T
    return cs


_M64 = (np.arange(GC)[None, :] >= np.arange(GC)[:, None]).astype(np.float32)
_M128 = (np.arange(128)[None, :] >= np.arange(128)[:, None]).astype(np.float32)
_ID = np.eye(128, dtype=np.float32)


def run_even(hT, mix_g, w_in, w_gk2, b_gk, out_g, conv_w, conv_b, ln_g, ln_b, w_out):
    nc = _prog("EA", build_EA)
    wpad = np.zeros((D, NZ * 128), np.float32)
    wpad[:, :w_in.shape[1]] = w_in
    w_l = lay_w(wpad, NZ)
    gl = lay_g(mix_g)
    hs = _shards(hT)
    res = _run(nc, [dict(h_in=hs[c], w=w_l, gains=gl) for c in range(NCORES)])
    zT = np.concatenate([r["z_out"] for r in res], axis=1)
    nc = _prog("EG", build_EG)
    ims = []
    for h in range(4):
        sc = np.zeros((128, 4), np.float32)
        sc[:, 0] = -b_gk[h * 128:(h + 1) * 128]
        sc[:, 1] = out_g[:128]
        sc[:, 2] = out_g[128:]
        ims.append(dict(qT=_c(zT[h * 128:(h + 1) * 128]), kT=_c(zT[512 + h * 128:512 + (h + 1) * 128]),
                        v=_c(zT[1024 + h * 256:1024 + (h + 1) * 256].T), gkT=_c(zT[3072:3088]),
                        w2=_c(w_gk2[:, h * 128:(h + 1) * 128]), sc=sc, m64=_M64, ident=_ID))
    res = _run(nc, ims)
    onT = np.concatenate([r["o_out"] for r in res], axis=0)
    nc = _prog("EB", build_EB)
    zv = np.zeros((1024, HALO + TA), np.float32)
    zv[:, HALO:] = zT[3088:4112]
    zs = np.zeros((1024, HALO + TA), np.float32)
    zs[:, HALO:] = zT[4112:5136]
    cs = prep_cs(conv_w, conv_b, ln_g, ln_b)
    wo = lay_w(w_out, KC)
    ims = []
    for c in range(NCORES):
        sl = slice(c * T, (c + 1) * T)
        ims.append(dict(h_in=hs[c], on_in=_c(onT[:, sl]), zg_in=_c(zT[2048:3072, sl]),
                        zv_in=_c(zv[:, c * T:c * T + HALO + T]), zs_in=_c(zs[:, c * T:c * T + HALO + T]),
                        cs=cs, wo=wo))
    res = _run(nc, ims)
    return np.concatenate([r["h_out"] for r in res], axis=1)


def prep_oc_weights(w_in, b_f, q_g, k_g, heads):
    FD = 2048

    def lay(cols):
        return _c(cols.reshape(KC, 128, 128).transpose(1, 0, 2)).reshape(128, KC * 128)
    wqk = np.stack([np.stack([lay(w_in[:, 0 * FD + h * 128:0 * FD + (h + 1) * 128]),
                              lay(w_in[:, 1 * FD + h * 128:1 * FD + (h + 1) * 128])]) for h in heads])
    wvo = np.stack([np.stack([lay(w_in[:, 2 * FD + h * 128:2 * FD + (h + 1) * 128]),
                              lay(w_in[:, 3 * FD + h * 128:3 * FD + (h + 1) * 128])]) for h in heads])
    wf = np.stack([_c(w_in[:, 4 * FD + h].reshape(KC, 128).T) for h in heads])
    sc = np.zeros((2, 128, 4), np.float32)
    for i, h in enumerate(heads):
        sc[i, :, 0] = q_g
        sc[i, :, 1] = k_g
        sc[i, :, 2] = -b_f[h]
    return dict(wqk=_c(wqk.astype(np.float32)), wvo=_c(wvo.astype(np.float32)),
                wf=_c(wf.astype(np.float32)), sc=sc)


def run_odd(hT, mix_g, w_in, b_f, q_g, k_g, w_out):
    nc = _prog("OA", build_OA)
    gl = lay_g(mix_g)
    hs = _shards(hT)
    res = _run(nc, [dict(h_in=hs[c], gains=gl) for c in range(NCORES)])
    u_all = _c(np.concatenate([r["u_out"] for r in res], axis=1))
    nc = _prog("OC", build_OC)
    ims = [dict(u_all=u_all, mask=_M128, **prep_oc_weights(w_in, b_f, q_g, k_g, [2 * c, 2 * c + 1]))
           for c in range(NCORES)]
    res = _run(nc, ims)
    oT = _c(np.concatenate([r["o_out"] for r in res], axis=1).T)
    nc = _prog("OD", build_OD)
    wo = lay_w(w_out, KC)
    ms = _shards(oT)
    res = _run(nc, [dict(h_in=hs[c], m_in=ms[c], wo=wo) for c in range(NCORES)])
    return np.concatenate([r["h_out"] for r in res], axis=1)


def kernel(x, ffn1_norm, ffn1_gate, ffn1_up, ffn1_down, mix_norm, ffn2_norm, ffn2_gate, ffn2_up,
           ffn2_down, ab_w_in, gla_w_gk2, gla_b_gk, gla_out_norm, conv_w, conv_b, conv_ln_g,
           conv_ln_b, ab_w_out, fox_w_in, fox_b_f, fox_q_norm, fox_k_norm, fox_w_out):
    f = lambda a: np.asarray(a, dtype=np.float32)
    hT = _c(f(x)[0].T)
    for l in range(4):
        hT = run_ffn(hT, f(ffn1_norm)[l], f(ffn1_gate)[l], f(ffn1_up)[l], f(ffn1_down)[l])
        if l % 2 == 0:
            e = l // 2
            hT = run_even(hT, f(mix_norm)[l], f(ab_w_in)[e], f(gla_w_gk2)[e], f(gla_b_gk)[e],
                          f(gla_out_norm)[e], f(conv_w)[e], f(conv_b)[e], f(conv_ln_g)[e],
                          f(conv_ln_b)[e], f(ab_w_out)[e])
        else:
            o = l // 2
            hT = run_odd(hT, f(mix_norm)[l], f(fox_w_in)[o], f(fox_b_f)[o], f(fox_q_norm)[o],
                         f(fox_k_norm)[o], f(fox_w_out)[o])
        hT = run_ffn(hT, f(ffn2_norm)[l], f(ffn2_gate)[l], f(ffn2_up)[l], f(ffn2_down)[l])
    return _c(hT.T)[None].astype(np.float32)
```

```python
import contextlib
import numpy as np
import concourse.bass as bass
import concourse.mybir as mybir
from concourse.bass_utils import run_bass_kernel_spmd

F32 = mybir.dt.float32
BF16 = mybir.dt.bfloat16
AF = mybir.ActivationFunctionType
ALU = mybir.AluOpType
AX = mybir.AxisListType

NCORES = 8
D = 2048
KC = D // 128
T = 1024
TB = 512
NTB = T // TB
DFF = 5632
NF = DFF // 128
FR = 4
NR = NF // FR
EPS = 1e-6


class Sched:
    ENGS = ("tensor", "vector", "scalar", "gpsimd", "sync")

    def __init__(self, nc, stack):
        self.nc = nc
        self.stack = stack
        self.ops = {e: [] for e in self.ENGS}
        self.semh = {}
        self.cnt = {}
        for e in self.ENGS:
            self.semh[e] = stack.enter_context(nc.semaphore(f"s_{e}"))
            self.cnt[e] = 0
        self.waited = {e: {} for e in self.ENGS}
        self.res = {}
        self.nwaits = 0

    def _deps(self, eng, reads, writes):
        need = {}

        def add(tok):
            k, v = tok
            if k == eng and v > self.cnt[eng]:
                return
            if k not in self.ENGS:
                v = self.cnt[k]
            if need.get(k, 0) < v:
                need[k] = v

        for r in reads:
            st = self.res.get(r)
            if st is not None and st[0] is not None:
                add(st[0])
        for w in writes:
            st = self.res.get(w)
            if st is not None:
                if st[0] is not None:
                    add(st[0])
                for k, v in st[1].items():
                    add((k, v))
        wd = self.waited[eng]
        for k, v in need.items():
            if wd.get(k, 0) < v:
                wd[k] = v
                h = self.semh[k]
                self.nwaits += 1
                self.ops[eng].append(lambda e, h=h, v=v: e.wait_ge(h, v))

    def _mark(self, tok, reads, writes):
        k, v = tok
        for r in reads:
            st = self.res.get(r)
            if st is None:
                st = self.res[r] = [None, {}]
            if st[1].get(k, 0) < v:
                st[1][k] = v
        for w in writes:
            self.res[w] = [tok, {}]

    def op(self, eng, fn, reads=(), writes=(), inc=True):
        self._deps(eng, reads, writes)
        if inc:
            self.cnt[eng] += 1
            tok = (eng, self.cnt[eng])
            h = self.semh[eng]
            self.ops[eng].append(lambda e, fn=fn, h=h: fn(e).then_inc(h, 1))
        else:
            assert eng == "tensor"
            tok = (eng, self.cnt[eng] + 1)
            self.ops[eng].append(lambda e, fn=fn: fn(e))
        self._mark(tok, reads, writes)
        return tok

    def dma(self, eng, out, in_, semkey, reads=(), writes=(), **kw):
        self._deps(eng, reads, writes)
        if semkey not in self.semh:
            self.semh[semkey] = self.stack.enter_context(self.nc.semaphore(f"d_{semkey}"))
            self.cnt[semkey] = 0
        self.cnt[semkey] += 16
        tok = (semkey, self.cnt[semkey])
        h = self.semh[semkey]
        self.ops[eng].append(
            lambda e, out=out, in_=in_, h=h, kw=kw: e.dma_start(out=out, in_=in_, **kw).then_inc(h, 16))
        self._mark(tok, reads, writes)
        return tok

    def wait_all(self, eng, keys):
        self._deps(eng, list(keys), list(keys))

    def emit(self):
        with self.nc.Block() as block:
            @block.tensor
            def _(e):
                for f in self.ops["tensor"]:
                    f(e)

            @block.vector
            def _(e):
                for f in self.ops["vector"]:
                    f(e)

            @block.scalar
            def _(e):
                for f in self.ops["scalar"]:
                    f(e)

            @block.gpsimd
            def _(e):
                for f in self.ops["gpsimd"]:
                    f(e)

            @block.sync
            def _(e):
                for f in self.ops["sync"]:
                    f(e)


class Ctx:
    def __init__(self, nc, stack):
        self.nc = nc
        self.stack = stack
        self.S = Sched(nc, stack)
        self.ntmp = 0

    def sb(self, name, shape, dt):
        return self.stack.enter_context(self.nc.sbuf_tensor(name, list(shape), dt))

    def ps(self, name, shape, dt=F32):
        return self.stack.enter_context(self.nc.psum_tensor(name, list(shape), dt))


def emit_rmsnorm(C, hT, xnT, gcol, ones_bf, sq, pss, rstd, tag):
    S = C.S
    n = 0
    for tb in range(NTB):
        ts = slice(tb * TB, (tb + 1) * TB)
        for kc in range(KC):
            s = sq[n % len(sq)]
            rs = ("sq", n % len(sq))
            n += 1
            S.op("scalar", lambda e, s=s, kc=kc, ts=ts: e.activation(
                out=s[:], in_=hT[:, kc, ts], func=AF.Square),
                reads=[("hT", kc, tb)], writes=[rs])
            S.op("tensor", lambda e, s=s, kc=kc, tb=tb: e.matmul(
                pss[tb][:], ones_bf[:], s[:], start=(kc == 0), stop=(kc == KC - 1)),
                reads=[rs], writes=[("pss", tb)], inc=(kc == KC - 1))
        S.op("vector", lambda e, tb=tb, ts=ts: e.tensor_scalar(
            out=rstd[:, ts], in0=pss[tb][:], scalar1=1.0 / D, scalar2=EPS,
            op0=ALU.mult, op1=ALU.add),
            reads=[("pss", tb)], writes=[("rstd", tb)])
        S.op("vector", lambda e, ts=ts: e.tensor_scalar(
            out=rstd[:, ts], in0=rstd[:, ts], scalar1=-0.5, scalar2=None,
            op0=ALU.pow),
            reads=[("rstd", tb)], writes=[("rstd", tb)])
        for kc in range(KC):
            S.op("vector", lambda e, kc=kc, ts=ts: e.scalar_tensor_tensor(
                out=xnT[:, kc, ts], in0=hT[:, kc, ts], scalar=gcol(kc),
                in1=rstd[:, ts], op0=ALU.mult, op1=ALU.mult),
                reads=[("hT", kc, tb), ("rstd", tb), "consts"], writes=[("xn", kc, tb)])


def emit_ffn(C, hT, xnT, wg_d, wu_d, wd_d, bufs, tag):
    S = C.S
    wgb, wub, wdb, h1, sg, pg, pu, py = (bufs[k] for k in
                                         ("wg", "wu", "wd", "h1", "sg", "pg", "pu", "py"))
    ny = 0
    for r in range(NR):
        rb = r % 2
        for fi in range(FR):
            f = r * FR + fi
            wb = f % 2
            S.dma("gpsimd", wgb[wb][:], wg_d[f], f"wg{wb}", writes=[("wg", wb)])
            S.dma("gpsimd", wub[wb][:], wu_d[f], f"wu{wb}", writes=[("wu", wb)])
            S.dma("gpsimd", wdb[rb][:, fi, :], wd_d[f], f"wd{rb}", writes=[("wd", rb, fi)])
            for tb in range(NTB):
                ts = slice(tb * TB, (tb + 1) * TB)
                for kc in range(KC):
                    S.op("tensor", lambda e, wb=wb, kc=kc, ts=ts, tb=tb: e.matmul(
                        pg[tb][:], wgb[wb][:, kc * 128:(kc + 1) * 128], xnT[:, kc, ts],
                        start=(kc == 0), stop=(kc == KC - 1)),
                        reads=[("wg", wb), ("xn", kc, tb)], writes=[("pg", tb)],
                        inc=(kc == KC - 1))
                for kc in range(KC):
                    S.op("tensor", lambda e, wb=wb, kc=kc, ts=ts, tb=tb: e.matmul(
                        pu[tb][:], wub[wb][:, kc * 128:(kc + 1) * 128], xnT[:, kc, ts],
                        start=(kc == 0), stop=(kc == KC - 1)),
                        reads=[("wu", wb), ("xn", kc, tb)], writes=[("pu", tb)],
                        inc=(kc == KC - 1))
                S.op("scalar", lambda e, tb=tb: e.activation(
                    out=sg[tb][:], in_=pg[tb][:], func=AF.Silu),
                    reads=[("pg", tb)], writes=[("sg", tb)])
                S.op("vector", lambda e, tb=tb, rb=rb, fi=fi, ts=ts: e.tensor_tensor(
                    out=h1[rb][:, fi, ts], in0=sg[tb][:], in1=pu[tb][:], op=ALU.mult),
                    reads=[("sg", tb), ("pu", tb)], writes=[("h1", rb, fi, tb)])
        for d in range(KC):
            for tb in range(NTB):
                ts = slice(tb * TB, (tb + 1) * TB)
                yb = ny % len(py)
                ny += 1
                for fi in range(FR):
                    S.op("tensor", lambda e, yb=yb, rb=rb, fi=fi, d=d, ts=ts: e.matmul(
                        py[yb][:], wdb[rb][:, fi, d * 128:(d + 1) * 128], h1[rb][:, fi, ts],
                        start=(fi == 0), stop=(fi == FR - 1)),
                        reads=[("wd", rb, fi), ("h1", rb, fi, tb)], writes=[("py", yb)],
                        inc=(fi == FR - 1))
                S.op("vector", lambda e, yb=yb, d=d, ts=ts: e.scalar_tensor_tensor(
                    out=hT[:, d, ts], in0=py[yb][:], scalar=0.5, in1=hT[:, d, ts],
                    op0=ALU.mult, op1=ALU.add),
                    reads=[("py", yb), ("hT", d, tb)], writes=[("hT", d, tb)])


def new_prog():
    nc = bass.Bass("TRN2", target_bir_lowering=False)
    return nc


def emit_norm(C, hT, xnT, gcol, ones_bf, epsc, sq, pss, rstd, psres, dim=D, nkc=KC, inkey="hT", outkey="xn"):
    S = C.S
    n = 0
    for tb in range(NTB):
        ts = slice(tb * TB, (tb + 1) * TB)
        for kc in range(nkc):
            s = sq[n % len(sq)]
            rs = ("sq", n % len(sq))
            n += 1
            S.op("scalar", lambda e, s=s, kc=kc, ts=ts: e.activation(
                out=s[:], in_=hT[:, kc, ts], func=AF.Square),
                reads=[(inkey, kc, tb)], writes=[rs])
            S.op("tensor", lambda e, s=s, kc=kc, tb=tb: e.matmul(
                pss[tb][:], ones_bf[:], s[:], start=(kc == 0), stop=(kc == nkc - 1)),
                reads=[rs, "ones"], writes=[psres(tb)], inc=True)
        S.op("scalar", lambda e, tb=tb, ts=ts: e.activation(
            out=rstd[:, ts], in_=pss[tb][:], func=AF.Sqrt, scale=1.0 / dim, bias=epsc[:]),
            reads=[psres(tb), "ones"], writes=[("rstd", tb)])
        S.op("vector", lambda e, ts=ts: e.reciprocal(out=rstd[:, ts], in_=rstd[:, ts]),
             reads=[("rstd", tb)], writes=[("rstd", tb)])
        for kc in range(nkc):
            S.op("vector", lambda e, kc=kc, ts=ts: e.scalar_tensor_tensor(
                out=xnT[:, kc, ts], in0=hT[:, kc, ts], scalar=gcol(kc),
                in1=rstd[:, ts], op0=ALU.mult, op1=ALU.mult),
                reads=[(inkey, kc, tb), ("rstd", tb), "consts"], writes=[(outkey, kc, tb)])


def emit_proj_resid(C, hT, xT, w_d, nm, wbuf, py, xkey, nk=KC, wtag="wo"):
    S = C.S
    ny = 0
    for m in range(nm):
        wb = m % 2
        S.dma("gpsimd", wbuf[wb][:, 0:nk * 128], w_d[m], f"{wtag}{wb}", writes=[(wtag, wb)])
        for tb in range(NTB):
            ts = slice(tb * TB, (tb + 1) * TB)
            yb = ny % len(py)
            ny += 1
            for k in range(nk):
                S.op("tensor", lambda e, yb=yb, wb=wb, k=k, ts=ts: e.matmul(
                    py[yb][:], wbuf[wb][:, k * 128:(k + 1) * 128], xT[:, k, ts],
                    start=(k == 0), stop=(k == nk - 1)),
                    reads=[(wtag, wb), (xkey, k, tb)], writes=[("py", yb)], inc=(k == nk - 1))
            S.op("vector", lambda e, yb=yb, m=m, ts=ts: e.scalar_tensor_tensor(
                out=hT[:, m, ts], in0=py[yb][:], scalar=1.0, in1=hT[:, m, ts],
                op0=ALU.mult, op1=ALU.add),
                reads=[("py", yb), ("hT", m, tb)], writes=[("hT", m, tb)])


def load_hT(C, hT, h_in):
    hv = h_in.rearrange("(kc p) t -> p kc t", p=128)
    for kc in range(KC):
        C.S.dma("sync", hT[:, kc, :], hv[:, kc, :], "hin",
                writes=[("hT", kc, tb) for tb in range(NTB)])


def store_hT(C, hT, h_out, key="hT", nkc=KC):
    ov = h_out.rearrange("(kc p) t -> p kc t", p=128)
    for kc in range(nkc):
        C.S.dma("sync", ov[:, kc, :], hT[:, kc, :], "hout",
                reads=[(key, kc, tb) for tb in range(NTB)], writes=[("hout", kc)])
    C.S.wait_all("sync", [("hout", kc) for kc in range(nkc)])


def build_F():
    nc = new_prog()
    h_in = nc.dram_tensor("h_in", [D, T], F32, kind="ExternalInput").ap()
    h_out = nc.dram_tensor("h_out", [D, T], F32, kind="ExternalOutput").ap()
    wg_d = nc.dram_tensor("wg", [NF, 128, KC * 128], F32, kind="ExternalInput").ap()
    wu_d = nc.dram_tensor("wu", [NF, 128, KC * 128], F32, kind="ExternalInput").ap()
    wd_d = nc.dram_tensor("wd", [NF, 128, D], F32, kind="ExternalInput").ap()
    gains_d = nc.dram_tensor("gains", [128, KC], F32, kind="ExternalInput").ap()
    with contextlib.ExitStack() as stack:
        C = Ctx(nc, stack)
        S = C.S
        hT = C.sb("hT", [128, KC, T], F32)
        xnT = C.sb("xnT", [128, KC, T], BF16)
        gains = C.sb("gains_sb", [128, KC], F32)
        ones_bf = C.sb("ones", [128, 128], BF16)
        epsc = C.sb("epsc", [128, 1], F32)
        rstd = C.sb("rstd", [128, T], F32)
        sq = [C.sb(f"sq{i}", [128, TB], BF16) for i in range(3)]
        bufs = dict(
            wg=[C.sb(f"wg{i}", [128, KC * 128], BF16) for i in range(2)],
            wu=[C.sb(f"wu{i}", [128, KC * 128], BF16) for i in range(2)],
            wd=[C.sb(f"wd{i}", [128, FR, D], BF16) for i in range(2)],
            h1=[C.sb(f"h1{i}", [128, FR, T], BF16) for i in range(2)],
            sg=[C.sb(f"sg{i}", [128, TB], F32) for i in range(2)],
            pg=[C.ps(f"pg{i}", [128, TB]) for i in range(2)],
            pu=[C.ps(f"pu{i}", [128, TB]) for i in range(2)],
            py=[C.ps(f"py{i}", [128, TB]) for i in range(4)],
        )
        S.dma("sync", gains[:], gains_d, "c0", writes=["consts"])
        S.op("vector", lambda e: e.memset(epsc[:], EPS), writes=["epsc"])
        S.op("vector", lambda e: e.memset(ones_bf[:], 1.0), writes=["ones"])
        load_hT(C, hT, h_in)
        emit_norm(C, hT, xnT, lambda kc: gains[:, kc:kc + 1], ones_bf, epsc, sq,
                  bufs["py"], rstd, lambda tb: ("py", tb))
        emit_ffn(C, hT, xnT, wg_d, wu_d, wd_d, bufs, "f")
        store_hT(C, hT, h_out)
        S.emit()
    return nc


def build_OA():
    nc = new_prog()
    h_in = nc.dram_tensor("h_in", [D, T], F32, kind="ExternalInput").ap()
    u_out = nc.dram_tensor("u_out", [D, T], BF16, kind="ExternalOutput").ap()
    gains_d = nc.dram_tensor("gains", [128, KC], F32, kind="ExternalInput").ap()
    with contextlib.ExitStack() as stack:
        C = Ctx(nc, stack)
        S = C.S
        hT = C.sb("hT", [128, KC, T], F32)
        xnT = C.sb("xnT", [128, KC, T], BF16)
        gains = C.sb("gains_sb", [128, KC], F32)
        ones_bf = C.sb("ones", [128, 128], BF16)
        epsc = C.sb("epsc", [128, 1], F32)
        rstd = C.sb("rstd", [128, T], F32)
        sq = [C.sb(f"sq{i}", [128, TB], BF16) for i in range(3)]
        py = [C.ps(f"py{i}", [128, TB]) for i in range(2)]
        S.dma("sync", gains[:], gains_d, "c0", writes=["consts"])
        S.op("vector", lambda e: e.memset(epsc[:], EPS), writes=["epsc"])
        S.op("vector", lambda e: e.memset(ones_bf[:], 1.0), writes=["ones"])
        load_hT(C, hT, h_in)
        emit_norm(C, hT, xnT, lambda kc: gains[:, kc:kc + 1], ones_bf, epsc, sq,
                  py, rstd, lambda tb: ("py", tb))
        store_hT(C, xnT, u_out, key="xn")
        S.emit()
    return nc


def build_OD():
    nc = new_prog()
    h_in = nc.dram_tensor("h_in", [D, T], F32, kind="ExternalInput").ap()
    m_in = nc.dram_tensor("m_in", [D, T], BF16, kind="ExternalInput").ap()
    h_out = nc.dram_tensor("h_out", [D, T], F32, kind="ExternalOutput").ap()
    wo_d = nc.dram_tensor("wo", [KC, 128, KC * 128], F32, kind="ExternalInput").ap()
    with contextlib.ExitStack() as stack:
        C = Ctx(nc, stack)
        S = C.S
        hT = C.sb("hT", [128, KC, T], F32)
        mT = C.sb("mT", [128, KC, T], BF16)
        wbuf = [C.sb(f"wo{i}", [128, KC * 128], BF16) for i in range(2)]
        py = [C.ps(f"py{i}", [128, TB]) for i in range(4)]
        load_hT(C, hT, h_in)
        mv = m_in.rearrange("(kc p) t -> p kc t", p=128)
        for kc in range(KC):
            S.dma("sync", mT[:, kc, :], mv[:, kc, :], "min",
                  writes=[("mx", kc, tb) for tb in range(NTB)])
        emit_proj_resid(C, hT, mT, wo_d, KC, wbuf, py, "mx")
        store_hT(C, hT, h_out)
        S.emit()
    return nc


TA = 8192
NB = TA // 128
NTBA = TA // TB
DH = 128


def build_OC(nheads=2, nblk=NB):
    nc = new_prog()
    u_in = nc.dram_tensor("u_all", [D, TA], BF16, kind="ExternalInput").ap()
    wqk_d = nc.dram_tensor("wqk", [2, 2, 128, KC * 128], F32, kind="ExternalInput").ap()
    wvo_d = nc.dram_tensor("wvo", [2, 2, 128, KC * 128], F32, kind="ExternalInput").ap()
    wf_d = nc.dram_tensor("wf", [2, 128, KC], F32, kind="ExternalInput").ap()
    sc_d = nc.dram_tensor("sc", [2, 128, 4], F32, kind="ExternalInput").ap()
    mask_d = nc.dram_tensor("mask", [128, 128], F32, kind="ExternalInput").ap()
    o_out = nc.dram_tensor("o_out", [TA, 2 * DH], F32, kind="ExternalOutput").ap()
    uv = u_in.rearrange("(kc p) t -> p kc t", p=128)
    ntb = (nblk * 128) // TB
    with contextlib.ExitStack() as stack:
        C = Ctx(nc, stack)
        S = C.S
        ub = [C.sb(f"ub{i}", [128, KC, TB], BF16) for i in range(2)]
        wq = C.sb("wq", [128, KC * 128], BF16)
        wk = C.sb("wk", [128, KC * 128], BF16)
        wv = C.sb("wv", [128, KC * 128], BF16)
        wo = C.sb("wog", [128, KC * 128], BF16)
        wf = C.sb("wf_sb", [128, KC], BF16)
        sc = C.sb("sc_sb", [128, 4], F32)
        qT = C.sb("qT", [128, TA], BF16)
        kT = C.sb("kT", [128, TA], BF16)
        Va = C.sb("Va", [128, NB, DH + 1], BF16)
        sog = C.sb("sog", [128, NB, DH], BF16)
        lf = C.sb("lf", [1, TA], F32)
        cum = C.sb("cum", [1, NB, 128], F32)
        ones_bf = C.sb("ones", [128, 128], BF16)
        onesf = C.sb("onesf", [1, 128], F32)
        epsc = C.sb("epsc", [128, 1], F32)
        onec = C.sb("onec", [128, 1], F32)
        mask = C.sb("mask_sb", [128, 128], BF16)
        sq = [C.sb(f"sq{i}", [128, TB], BF16) for i in range(2)]
        rr = [C.sb(f"rr{i}", [128, TB], F32) for i in range(2)]
        ef = C.sb("ef", [1, TB], F32)
        cK = C.sb("cK", [128, 128], F32)
        bias = [C.sb(f"bias{i}", [128, NB], F32) for i in range(2)]
        PT = [C.sb(f"PT{i}", [128, 256], BF16) for i in range(6)]
        rinv = [C.sb(f"rinv{i}", [128, 1], F32) for i in range(2)]
        ot = [C.sb(f"ot{i}", [128, DH], F32) for i in range(2)]
        pb = [C.ps(f"pb{i}", [128, TB]) for i in range(8)]

        S.op("vector", lambda e: e.memset(epsc[:], EPS), writes=["epsc"])
        S.op("vector", lambda e: e.memset(onec[:], 1.0), writes=["onec"])
        S.op("vector", lambda e: e.memset(ones_bf[:], 1.0), writes=["ones"])
        S.op("vector", lambda e: e.memset(onesf[:], 1.0), writes=["onesf"])
        S.op("vector", lambda e: e.memset(Va[:, :, DH:DH + 1], 1.0), writes=["Va1"])
        S.dma("gpsimd", mask[:], mask_d, "cmask", writes=["mask"])
        scale = float(DH) ** -0.5
        npt = 0
        nst = 0
        for h in range(nheads):
            S.dma("gpsimd", wq[:], wqk_d[h, 0], "wq", writes=["wq"])
            S.dma("gpsimd", wk[:], wqk_d[h, 1], "wk", writes=["wk"])
            S.dma("gpsimd", wv[:], wvo_d[h, 0], "wv", writes=["wv"])
            S.dma("gpsimd", wo[:], wvo_d[h, 1], "wog", writes=["wog"])
            S.dma("gpsimd", wf[:], wf_d[h], "wf", writes=["wf"])
            S.dma("sync", sc[:], sc_d[h], "sc", writes=["sc"])
            for tb in range(ntb):
                b = tb % 2
                ts = slice(tb * TB, (tb + 1) * TB)
                S.dma("sync", ub[b][:], uv[:, :, ts], f"ub{b}", writes=[("ub", b)])
                for which, (wt, wkey, dst, dkey, gcol) in enumerate(
                        ((wq, "wq", qT, "qT", 0), (wk, "wk", kT, "kT", 1))):
                    pbi = which
                    for kc in range(KC):
                        S.op("tensor", lambda e, pbi=pbi, wt=wt, kc=kc, b=b: e.matmul(
                            pb[pbi][:], wt[:, kc * 128:(kc + 1) * 128], ub[b][:, kc, :],
                            start=(kc == 0), stop=(kc == KC - 1)),
                            reads=[wkey, ("ub", b)], writes=[("pb", pbi)], inc=(kc == KC - 1))
                    S.op("scalar", lambda e, pbi=pbi: e.activation(
                        out=sq[pbi][:], in_=pb[pbi][:], func=AF.Square),
                        reads=[("pb", pbi)], writes=[("sq", pbi)])
                    S.op("tensor", lambda e, pbi=pbi: e.matmul(
                        pb[2][:], ones_bf[:], sq[pbi][:], start=True, stop=True),
                        reads=[("sq", pbi), "ones"], writes=[("pb", 2)])
                    S.op("scalar", lambda e, pbi=pbi: e.activation(
                        out=rr[pbi][:], in_=pb[2][:], func=AF.Sqrt, scale=1.0 / DH, bias=epsc[:]),
                        reads=[("pb", 2), "epsc"], writes=[("rr", pbi)])
                    S.op("vector", lambda e, pbi=pbi: e.reciprocal(out=rr[pbi][:], in_=rr[pbi][:]),
                         reads=[("rr", pbi)], writes=[("rr", pbi)])
                    S.op("vector", lambda e, pbi=pbi, dst=dst, ts=ts, gcol=gcol: e.scalar_tensor_tensor(
                        out=dst[:, ts], in0=pb[pbi][:], scalar=sc[:, gcol:gcol + 1], in1=rr[pbi][:],
                        op0=ALU.mult, op1=ALU.mult),
                        reads=[("pb", pbi), ("rr", pbi), "sc"], writes=[(dkey, tb)])
                for pbi, wt, wkey in ((3, wv, "wv"), (4, wo, "wog")):
                    for sbk in range(4):
                        cs = slice(sbk * 128, (sbk + 1) * 128)
                        for kc in range(KC):
                            S.op("tensor", lambda e, pbi=pbi, wt=wt, kc=kc, b=b, cs=cs: e.matmul(
                                pb[pbi][:, cs], ub[b][:, kc, cs], wt[:, kc * 128:(kc + 1) * 128],
                                start=(kc == 0), stop=(kc == KC - 1)),
                                reads=[wkey, ("ub", b)], writes=[("pb", pbi)],
                                inc=(kc == KC - 1))
                for sbk in range(4):
                    cs = slice(sbk * 128, (sbk + 1) * 128)
                    blk = tb * 4 + sbk
                    S.op("vector", lambda e, cs=cs, blk=blk: e.tensor_copy(
                        out=Va[:, blk, 0:DH], in_=pb[3][:, cs]),
                        reads=[("pb", 3)], writes=[("Va", blk)])
                    S.op("scalar", lambda e, cs=cs, blk=blk: e.activation(
                        out=sog[:, blk, :], in_=pb[4][:, cs], func=AF.Sigmoid),
                        reads=[("pb", 4)], writes=[("sog", blk)])
                for kc in range(KC):
                    S.op("tensor", lambda e, kc=kc, b=b: e.matmul(
                        pb[5][0:1, :], wf[:, kc:kc + 1], ub[b][:, kc, :],
                        start=(kc == 0), stop=(kc == KC - 1)),
                        reads=["wf", ("ub", b)], writes=[("pb", 5)], inc=(kc == KC - 1))
                S.op("scalar", lambda e: e.activation(
                    out=ef[:], in_=pb[5][0:1, :], func=AF.Exp, scale=-1.0, bias=sc[0:1, 2:3]),
                    reads=[("pb", 5), "sc"], writes=["ef"])
                S.op("scalar", lambda e: e.activation(
                    out=ef[:], in_=ef[:], func=AF.Ln, bias=onec[0:1, :]),
                    reads=["ef", "onec"], writes=["ef"])
                S.op("vector", lambda e, ts=ts: e.tensor_scalar(
                    out=lf[0:1, ts], in0=ef[:], scalar1=-1.0, scalar2=None, op0=ALU.mult),
                    reads=["ef"], writes=[("lf", tb)])
            for blk in range(nblk):
                init = 0.0 if blk == 0 else cum[0:1, blk - 1, 127:128]
                S.op("vector", lambda e, blk=blk, init=init: e.tensor_tensor_scan(
                    out=cum[0:1, blk, :], data0=onesf[0:1, :], data1=lf[0:1, blk * 128:(blk + 1) * 128],
                    initial=init, op0=ALU.mult, op1=ALU.add),
                    reads=[("lf", blk // 4), "onesf", "cum"], writes=["cum"])
            for j in range(nblk):
                S.op("tensor", lambda e, j=j: e.matmul(
                    pb[5][:, j:j + 1], cum[0:1, j, :], onesf[0:1, 0:1], start=True, stop=True),
                    reads=["cum", "onesf"], writes=[("pb", 5)], inc=(j == nblk - 1))
            S.op("tensor", lambda e: e.matmul(
                pb[5][:, 64:64 + nblk], onesf[0:1, :], cum[0:1, 0:nblk, 127], start=True, stop=True),
                reads=["cum", "onesf"], writes=[("pb", 5)])
            S.op("vector", lambda e: e.tensor_copy(out=cK[:, 0:nblk], in_=pb[5][:, 0:nblk]),
                 reads=[("pb", 5)], writes=["cK"])
            S.op("vector", lambda e: e.tensor_copy(out=cK[:, 64:64 + nblk], in_=pb[5][:, 64:64 + nblk]),
                 reads=[("pb", 5), "cK"], writes=["cK"])
            items = []
            for I in range(nblk // 2):
                i0, i1 = 2 * I, 2 * I + 1
                pos = (4 + 2 * (I % 2), 5 + 2 * (I % 2))
                for j in range(2 * I):
                    items.append(dict(I=I, wide=True, k=j, q0=i0, pvs=[(pos[0], 0, j == 0, False), (pos[1], 1, j == 0, False)],
                                      mask=False, first=(j == 0), lastq=[]))
                items.append(dict(I=I, wide=False, k=i0, q0=i0, pvs=[(pos[0], 0, I == 0, True)], mask=True,
                                  first=(I == 0), lastq=[(i0, pos[0])]))
                items.append(dict(I=I, wide=False, k=i0, q0=i1, pvs=[(pos[1], 0, I == 0, False)], mask=False,
                                  first=False, lastq=[]))
                items.append(dict(I=I, wide=False, k=i1, q0=i1, pvs=[(pos[1], 0, False, True)], mask=True,
                                  first=False, lastq=[(i1, pos[1])]))
            NPT = len(PT)
            DEPTH = 3

            def stage_a(n, it):
                I = it["I"]
                bi = I % 2
                if it["first"]:
                    i1 = 2 * I + 1
                    S.op("vector", lambda e, bi=bi, i1=i1: e.tensor_scalar(
                        out=bias[bi][:, 0:i1 + 1], in0=cK[:, 0:i1 + 1], scalar1=-1.0,
                        scalar2=cK[:, 64 + i1:65 + i1], op0=ALU.mult, op1=ALU.add),
                        reads=["cK"], writes=[("bias", bi)])
                w = 256 if it["wide"] else 128
                ks = slice(it["k"] * 128, (it["k"] + 1) * 128)
                qs_ = slice(it["q0"] * 128, it["q0"] * 128 + w)
                sb_ = n % 4
                p = n % NPT
                S.op("tensor", lambda e, sb_=sb_, ks=ks, qs_=qs_, w=w: e.matmul(
                    pb[sb_][:, 0:w], kT[:, ks], qT[:, qs_], start=True, stop=True),
                    reads=[("kT", it["k"] // 4), ("qT", it["q0"] // 4), ("qT", (it["q0"] + w // 128 - 1) // 4)],
                    writes=[("pb", sb_)])
                S.op("scalar", lambda e, sb_=sb_, p=p, bi=bi, k=it["k"], w=w: e.activation(
                    out=PT[p][:, 0:w], in_=pb[sb_][:, 0:w], func=AF.Exp, scale=scale,
                    bias=bias[bi][:, k:k + 1]),
                    reads=[("pb", sb_), ("bias", bi)], writes=[("PT", p)])
                if it["mask"]:
                    S.op("vector", lambda e, p=p: e.tensor_tensor(
                        out=PT[p][:, 0:128], in0=PT[p][:, 0:128], in1=mask[:], op=ALU.mult),
                        reads=[("PT", p), "mask"], writes=[("PT", p)])

            def stage_b(n, it):
                p = n % NPT
                for ii, (po, qq, st, sp) in enumerate(it["pvs"]):
                    S.op("tensor", lambda e, po=po, p=p, k=it["k"], qq=qq, st=st, sp=sp: e.matmul(
                        pb[po][:, 0:DH + 1], PT[p][:, qq * 128:(qq + 1) * 128], Va[:, k, :], start=st, stop=sp),
                        reads=[("PT", p), ("Va", it["k"]), "Va1"], writes=[("pb", po)])
                for (i, po) in it["lastq"]:
                    ob = i % 2
                    qs = slice(i * 128, (i + 1) * 128)
                    S.op("vector", lambda e, po=po, ob=ob: e.reciprocal(
                        out=rinv[ob][:], in_=pb[po][:, DH:DH + 1]),
                        reads=[("pb", po)], writes=[("rinv", ob)])
                    S.op("vector", lambda e, po=po, ob=ob, i=i: e.scalar_tensor_tensor(
                        out=ot[ob][:], in0=pb[po][:, 0:DH], scalar=rinv[ob][:, 0:1], in1=sog[:, i, :],
                        op0=ALU.mult, op1=ALU.mult),
                        reads=[("pb", po), ("rinv", ob), ("sog", i)], writes=[("ot", ob)])
                    S.dma("sync", o_out[qs, h * DH:(h + 1) * DH], ot[ob][:], f"oo{ob}",
                          reads=[("ot", ob)], writes=[("oout", ob)])

            for n in range(len(items) + DEPTH):
                if n < len(items):
                    stage_a(n, items[n])
                if n - DEPTH >= 0:
                    stage_b(n - DEPTH, items[n - DEPTH])
        S.wait_all("sync", [("oout", 0), ("oout", 1)])
        S.emit()
    return nc


NZ = 41


def build_EA():
    nc = new_prog()
    h_in = nc.dram_tensor("h_in", [D, T], F32, kind="ExternalInput").ap()
    z_out = nc.dram_tensor("z_out", [NZ * 128, T], F32, kind="ExternalOutput").ap()
    w_d = nc.dram_tensor("w", [NZ, 128, KC * 128], F32, kind="ExternalInput").ap()
    gains_d = nc.dram_tensor("gains", [128, KC], F32, kind="ExternalInput").ap()
    with contextlib.ExitStack() as stack:
        C = Ctx(nc, stack)
        S = C.S
        hT = C.sb("hT", [128, KC, T], F32)
        xnT = C.sb("xnT", [128, KC, T], BF16)
        gains = C.sb("gains_sb", [128, KC], F32)
        ones_bf = C.sb("ones", [128, 128], BF16)
        epsc = C.sb("epsc", [128, 1], F32)
        rstd = C.sb("rstd", [128, T], F32)
        sq = [C.sb(f"sq{i}", [128, TB], BF16) for i in range(3)]
        wbuf = [C.sb(f"w{i}", [128, KC * 128], BF16) for i in range(2)]
        stg = [C.sb(f"stg{i}", [128, TB], F32) for i in range(4)]
        py = [C.ps(f"py{i}", [128, TB]) for i in range(4)]
        S.dma("sync", gains[:], gains_d, "c0", writes=["consts"])
        S.op("vector", lambda e: e.memset(epsc[:], EPS), writes=["epsc"])
        S.op("vector", lambda e: e.memset(ones_bf[:], 1.0), writes=["ones"])
        load_hT(C, hT, h_in)
        emit_norm(C, hT, xnT, lambda kc: gains[:, kc:kc + 1], ones_bf, epsc, sq,
                  py, rstd, lambda tb: ("py", tb))
        n = 0
        for m in range(NZ):
            wb = m % 2
            S.dma("gpsimd", wbuf[wb][:], w_d[m], f"w{wb}", writes=[("w", wb)])
            for tb in range(NTB):
                ts = slice(tb * TB, (tb + 1) * TB)
                yb = n % 4
                n += 1
                for k in range(KC):
                    S.op("tensor", lambda e, yb=yb, wb=wb, k=k, ts=ts: e.matmul(
                        py[yb][:], wbuf[wb][:, k * 128:(k + 1) * 128], xnT[:, k, ts],
                        start=(k == 0), stop=(k == KC - 1)),
                        reads=[("w", wb), ("xn", k, tb)], writes=[("py", yb)], inc=(k == KC - 1))
                eng = "vector" if yb % 2 == 0 else "scalar"
                if eng == "vector":
                    S.op("vector", lambda e, yb=yb: e.tensor_copy(out=stg[yb][:], in_=py[yb][:]),
                         reads=[("py", yb)], writes=[("stg", yb)])
                else:
                    S.op("scalar", lambda e, yb=yb: e.copy(out=stg[yb][:], in_=py[yb][:]),
                         reads=[("py", yb)], writes=[("stg", yb)])
                S.dma("sync", z_out[m * 128:(m + 1) * 128, ts], stg[yb][:], f"zo{yb}",
                      reads=[("stg", yb)], writes=[("zo", yb)])
        S.wait_all("sync", [("zo", i) for i in range(4)])
        S.emit()
    return nc


GC = 64
NCH = T // GC
DK = 128
DV = 256


def build_EG(nsuper=TA // T):
    nc = new_prog()
    q_in = nc.dram_tensor("qT", [DK, TA], F32, kind="ExternalInput").ap()
    k_in = nc.dram_tensor("kT", [DK, TA], F32, kind="ExternalInput").ap()
    v_in = nc.dram_tensor("v", [TA, DV], F32, kind="ExternalInput").ap()
    gk_in = nc.dram_tensor("gkT", [16, TA], F32, kind="ExternalInput").ap()
    w2_d = nc.dram_tensor("w2", [16, DK], F32, kind="ExternalInput").ap()
    sc_d = nc.dram_tensor("sc", [128, 4], F32, kind="ExternalInput").ap()
    m64_d = nc.dram_tensor("m64", [GC, GC], F32, kind="ExternalInput").ap()
    id_d = nc.dram_tensor("ident", [128, 128], F32, kind="ExternalInput").ap()
    o_out = nc.dram_tensor("o_out", [DV, TA], F32, kind="ExternalOutput").ap()
    vv = v_in.rearrange("(n p) c -> p n c", p=GC)
    with contextlib.ExitStack() as stack:
        C = Ctx(nc, stack)
        S = C.S
        qf = C.sb("qf", [128, T], F32)
        kf = C.sb("kf", [128, T], F32)
        gk = C.sb("gk", [16, T], BF16)
        vs = C.sb("vs", [GC, NCH, DV], BF16)
        w2 = C.sb("w2_sb", [16, DK], BF16)
        sc = C.sb("sc_sb", [128, 4], F32)
        m64 = C.sb("m64_sb", [GC, GC], F32)
        ident = C.sb("ident_sb", [128, 128], BF16)
        ones_bf = C.sb("ones", [128, 128], BF16)
        epsc = C.sb("epsc", [128, 1], F32)
        onec = C.sb("onec", [128, 1], F32)
        la = C.sb("la", [128, T], F32)
        cum = C.sb("cum", [128, T], F32)
        nref = C.sb("nref", [128, NCH], F32)
        ee = [C.sb(f"ee{i}", [128, GC], F32) for i in range(4)]
        qt = C.sb("qt", [128, T], BF16)
        kt = C.sb("kt", [128, T], BF16)
        qi = C.sb("qi", [128, T], BF16)
        ko = C.sb("ko", [128, T], BF16)
        dec = C.sb("dec", [128, NCH], F32)
        kot = [C.sb(f"kot{i}", [GC, DK], BF16) for i in range(2)]
        ats = [C.sb(f"ats{i}", [GC, GC], BF16) for i in range(2)]
        St = C.sb("St", [128, DV], F32)
        Sb = [C.sb(f"Sb{i}", [128, DV], BF16) for i in range(2)]
        oT = C.sb("oT", [128, 2, T], F32)
        onT = C.sb("onT", [128, 2, T], F32)
        rstd = C.sb("rstd", [128, T], F32)
        sq = [C.sb(f"sq{i}", [128, TB], BF16) for i in range(3)]
        pb = [C.ps(f"pb{i}", [128, TB]) for i in range(8)]
        scale = float(DK) ** -0.5
        onesrow = C.sb("onesrow", [128, GC], F32)
        S.op("vector", lambda e: e.memset(onesrow[:], 1.0), writes=["onesrow"])

        S.op("vector", lambda e: e.memset(epsc[:], EPS), writes=["epsc"])
        S.op("vector", lambda e: e.memset(onec[:], 1.0), writes=["onec"])
        S.op("vector", lambda e: e.memset(ones_bf[:], 1.0), writes=["ones"])
        S.op("vector", lambda e: e.memset(St[:], 0.0), writes=["St"])
        S.op("vector", lambda e: e.memset(Sb[0][:], 0.0), writes=[("Sb", 0)])
        S.dma("gpsimd", w2[:], w2_d, "c1", writes=["w2"])
        S.dma("gpsimd", ident[:], id_d, "c2", writes=["ident"])
        S.dma("sync", sc[:], sc_d, "c3", writes=["sc", "consts"])
        S.dma("sync", m64[:], m64_d, "c4", writes=["m64"])
        nsb = 0
        for sp in range(nsuper):
            t0 = sp * T
            S.dma("sync", qf[:], q_in[:, t0:t0 + T], "qf", writes=["qf"])
            S.dma("sync", kf[:], k_in[:, t0:t0 + T], "kf", writes=["kf"])
            S.dma("gpsimd", gk[:], gk_in[:, t0:t0 + T], "gk", writes=["gk"])
            S.dma("gpsimd", vs[:], vv[:, sp * NCH:(sp + 1) * NCH, :], "vs", writes=["vs"])
            for tb in range(NTB):
                ts = slice(tb * TB, (tb + 1) * TB)
                S.op("tensor", lambda e, ts=ts: e.matmul(
                    pb[0][:], w2[:], gk[:, ts], start=True, stop=True),
                    reads=["w2", "gk"], writes=[("pb", 0)])
                S.op("scalar", lambda e, ts=ts: e.activation(
                    out=la[:, ts], in_=pb[0][:], func=AF.Exp, scale=-1.0, bias=sc[:, 0:1]),
                    reads=[("pb", 0), "sc"], writes=[("la", tb)])
                S.op("scalar", lambda e, ts=ts: e.activation(
                    out=la[:, ts], in_=la[:, ts], func=AF.Ln, bias=onec[:]),
                    reads=[("la", tb), "onec"], writes=[("la", tb)])
                S.op("vector", lambda e, ts=ts: e.tensor_scalar(
                    out=la[:, ts], in0=la[:, ts], scalar1=-1.0 / 16.0, scalar2=None, op0=ALU.mult),
                    reads=[("la", tb)], writes=[("la", tb)])
            for n in range(NCH):
                cs = slice(n * GC, (n + 1) * GC)
                S.op("vector", lambda e, cs=cs: e.tensor_tensor_scan(
                    out=cum[:, cs], data0=onesrow[:, 0:GC], data1=la[:, cs],
                    initial=0.0, op0=ALU.mult, op1=ALU.add),
                    reads=[("la", (n * GC) // TB), "onesrow"], writes=[("cum", n)])
            S.op("vector", lambda e: e.tensor_scalar(
                out=nref[:], in0=cum[:, GC // 2::GC], scalar1=-1.0, scalar2=None, op0=ALU.mult),
                reads=[("cum", n) for n in range(NCH)], writes=["nref"])
            S.op("scalar", lambda e: e.activation(
                out=dec[:], in_=cum[:, GC - 1::GC], func=AF.Exp),
                reads=[("cum", n) for n in range(NCH)], writes=["dec"])
            for n in range(NCH):
                cs = slice(n * GC, (n + 1) * GC)
                rcol = n * GC + GC // 2
                lcol = n * GC + GC - 1
                S.op("scalar", lambda e, cs=cs, n=n: e.activation(
                    out=ee[0][:], in_=cum[:, cs], func=AF.Exp, bias=nref[:, n:n + 1]),
                    reads=[("cum", n), "nref"], writes=[("ee", 0)])
                S.op("scalar", lambda e, cs=cs, rcol=rcol: e.activation(
                    out=ee[1][:], in_=cum[:, cs], func=AF.Exp, scale=-1.0, bias=cum[:, rcol:rcol + 1]),
                    reads=[("cum", n)], writes=[("ee", 1)])
                S.op("scalar", lambda e, cs=cs: e.activation(
                    out=ee[2][:], in_=cum[:, cs], func=AF.Exp),
                    reads=[("cum", n)], writes=[("ee", 2)])
                S.op("scalar", lambda e, cs=cs, lcol=lcol: e.activation(
                    out=ee[3][:], in_=cum[:, cs], func=AF.Exp, scale=-1.0, bias=cum[:, lcol:lcol + 1]),
                    reads=[("cum", n)], writes=[("ee", 3)])
                S.op("vector", lambda e, cs=cs: e.scalar_tensor_tensor(
                    out=qt[:, cs], in0=qf[:, cs], scalar=scale, in1=ee[0][:], op0=ALU.mult, op1=ALU.mult),
                    reads=["qf", ("ee", 0)], writes=[("qt", n)])
                S.op("vector", lambda e, cs=cs: e.tensor_tensor(
                    out=kt[:, cs], in0=kf[:, cs], in1=ee[1][:], op=ALU.mult),
                    reads=["kf", ("ee", 1)], writes=[("kt", n)])
                S.op("vector", lambda e, cs=cs: e.scalar_tensor_tensor(
                    out=qi[:, cs], in0=qf[:, cs], scalar=scale, in1=ee[2][:], op0=ALU.mult, op1=ALU.mult),
                    reads=["qf", ("ee", 2)], writes=[("qi", n)])
                S.op("vector", lambda e, cs=cs: e.tensor_tensor(
                    out=ko[:, cs], in0=kf[:, cs], in1=ee[3][:], op=ALU.mult),
                    reads=["kf", ("ee", 3)], writes=[("ko", n)])
            for n in range(NCH):
                cs = slice(n * GC, (n + 1) * GC)
                b2 = n % 2
                sbi = nsb % 2
                S.op("tensor", lambda e, cs=cs: e.matmul(
                    pb[1][0:GC, 0:GC], kt[:, cs], qt[:, cs], start=True, stop=True),
                    reads=[("kt", n), ("qt", n)], writes=[("pb", 1)])
                S.op("vector", lambda e, b2=b2: e.tensor_tensor(
                    out=ats[b2][:], in0=pb[1][0:GC, 0:GC], in1=m64[:], op=ALU.mult),
                    reads=[("pb", 1), "m64"], writes=[("ats", b2)])
                S.op("tensor", lambda e, cs=cs: e.matmul(
                    pb[2][0:GC, 0:DK], ko[:, cs], ident[:], start=True, stop=True),
                    reads=[("ko", n), "ident"], writes=[("pb", 2)])
                S.op("scalar", lambda e, b2=b2: e.copy(out=kot[b2][:], in_=pb[2][0:GC, 0:DK]),
                     reads=[("pb", 2)], writes=[("kot", b2)])
                for c in range(2):
                    ob = 3 if c == 0 else 7
                    S.op("tensor", lambda e, c=c, b2=b2, n=n, ob=ob: e.matmul(
                        pb[ob][:, 0:GC], vs[:, n, c * 128:(c + 1) * 128], ats[b2][:],
                        start=True, stop=False),
                        reads=["vs", ("ats", b2)], writes=[("pb", ob)], inc=False)
                    S.op("tensor", lambda e, c=c, sbi=sbi, cs=cs, ob=ob: e.matmul(
                        pb[ob][:, 0:GC], Sb[sbi][:, c * 128:(c + 1) * 128], qi[:, cs],
                        start=False, stop=True),
                        reads=[("Sb", sbi), ("qi", n)], writes=[("pb", ob)])
                    S.op("vector", lambda e, c=c, cs=cs, ob=ob: e.tensor_copy(
                        out=oT[:, c, cs], in_=pb[ob][:, 0:GC]),
                        reads=[("pb", ob)], writes=[("oT", c, (n * GC) // TB)])
                S.op("tensor", lambda e, b2=b2, n=n: e.matmul(
                    pb[4][:, 0:DV], kot[b2][:], vs[:, n, :], start=True, stop=True),
                    reads=[("kot", b2), "vs"], writes=[("pb", 4)])
                S.op("vector", lambda e, n=n: e.scalar_tensor_tensor(
                    out=St[:], in0=St[:], scalar=dec[:, n:n + 1], in1=pb[4][:, 0:DV],
                    op0=ALU.mult, op1=ALU.add),
                    reads=["St", "dec", ("pb", 4)], writes=["St"])
                nsb += 1
                S.op("scalar", lambda e, nb=nsb % 2: e.copy(out=Sb[nb][:], in_=St[:]),
                     reads=["St"], writes=[("Sb", nsb % 2)])
            emit_norm(C, oT, onT, lambda c: sc[:, 1 + c:2 + c], ones_bf, epsc, sq,
                      [pb[5], pb[6]], rstd, lambda tb: ("pb", 5 + tb), dim=DV, nkc=2, inkey="oT")
            ov = o_out.rearrange("(c p) t -> p c t", p=128)
            for c in range(2):
                S.dma("sync", ov[:, c, t0:t0 + T], onT[:, c, :], "oo",
                      reads=[("xn", c, tb) for tb in range(NTB)], writes=[("oo", c)])
        S.wait_all("sync", [("oo", 0), ("oo", 1)])
        S.emit()
    return nc


def ones_f_row(C):
    if not hasattr(C, "_onesrow"):
        C._onesrow = C.sb("onesrow", [128, GC], F32)
        C.S.op("vector", lambda e: e.memset(C._onesrow[:], 1.0), writes=["onesrow"])
    return C._onesrow


CW = 31
HALO = CW - 1
NCC = 8


def build_EB():
    nc = new_prog()
    h_in = nc.dram_tensor("h_in", [D, T], F32, kind="ExternalInput").ap()
    on_in = nc.dram_tensor("on_in", [1024, T], F32, kind="ExternalInput").ap()
    zg_in = nc.dram_tensor("zg_in", [1024, T], F32, kind="ExternalInput").ap()
    zv_in = nc.dram_tensor("zv_in", [1024, HALO + T], F32, kind="ExternalInput").ap()
    zs_in = nc.dram_tensor("zs_in", [1024, HALO + T], F32, kind="ExternalInput").ap()
    cs_d = nc.dram_tensor("cs", [128, NCC, 34], F32, kind="ExternalInput").ap()
    wo_d = nc.dram_tensor("wo", [KC, 128, KC * 128], F32, kind="ExternalInput").ap()
    h_out = nc.dram_tensor("h_out", [D, T], F32, kind="ExternalOutput").ap()
    with contextlib.ExitStack() as stack:
        C = Ctx(nc, stack)
        S = C.S
        hT = C.sb("hT", [128, KC, T], F32)
        mixT = C.sb("mixT", [128, KC, T], BF16)
        yT = C.sb("yT", [128, NCC, T], F32)
        cs = C.sb("cs_sb", [128, NCC, 34], F32)
        sa = [C.sb(f"sa{i}", [128, HALO + T], F32) for i in range(2)]
        sb_ = [C.sb(f"sb{i}", [128, HALO + T], F32) for i in range(2)]
        cc = [C.sb(f"cc{i}", [128, HALO + T], F32) for i in range(2)]
        ones_bf = C.sb("ones", [128, 128], BF16)
        epsc = C.sb("epsc", [128, 1], F32)
        ybf = [C.sb(f"ybf{i}", [128, TB], BF16) for i in range(3)]
        mean = C.sb("mean", [128, T], F32)
        var = C.sb("var", [128, T], F32)
        rstd = C.sb("rstd", [128, T], F32)
        tmp = [C.sb(f"tmp{i}", [128, T], F32) for i in range(2)]
        wbuf = [C.sb(f"wo{i}", [128, KC * 128], BF16) for i in range(2)]
        py = [C.ps(f"py{i}", [128, TB]) for i in range(4)]
        ps1 = [C.ps(f"ps1{i}", [128, TB]) for i in range(2)]
        ps2 = [C.ps(f"ps2{i}", [128, TB]) for i in range(2)]
        S.op("vector", lambda e: e.memset(epsc[:], EPS), writes=["epsc"])
        S.op("vector", lambda e: e.memset(ones_bf[:], 1.0), writes=["ones"])
        S.dma("sync", cs[:], cs_d, "c0", writes=["cs"])
        load_hT(C, hT, h_in)
        for c in range(NCC):
            b = c % 2
            rs = slice(c * 128, (c + 1) * 128)
            S.dma("sync", sa[b][:, 0:T], on_in[rs, :], f"sa{b}", writes=[("sa", b)])
            S.dma("sync", sb_[b][:, 0:T], zg_in[rs, :], f"sb{b}", writes=[("sb", b)])
            S.op("scalar", lambda e, b=b: e.activation(out=sb_[b][:, 0:T], in_=sb_[b][:, 0:T], func=AF.Silu),
                 reads=[("sb", b)], writes=[("sb", b)])
            S.op("vector", lambda e, b=b, c=c: e.tensor_tensor(
                out=mixT[:, c, :], in0=sa[b][:, 0:T], in1=sb_[b][:, 0:T], op=ALU.mult),
                reads=[("sa", b), ("sb", b)], writes=[("mx", c, 0), ("mx", c, 1)])
        for c in range(NCC):
            b = c % 2
            rs = slice(c * 128, (c + 1) * 128)
            S.dma("sync", sa[b][:], zv_in[rs, :], f"sa{b}", writes=[("sa", b)])
            S.dma("sync", sb_[b][:], zs_in[rs, :], f"sb{b}", writes=[("sb", b)])
            S.op("scalar", lambda e, b=b: e.activation(out=sb_[b][:], in_=sb_[b][:], func=AF.Sigmoid),
                 reads=[("sb", b)], writes=[("sb", b)])
            S.op("vector", lambda e, b=b: e.tensor_tensor(
                out=cc[b][:], in0=sa[b][:], in1=sb_[b][:], op=ALU.mult),
                reads=[("sa", b), ("sb", b)], writes=[("cc", b)])
            S.op("vector", lambda e, b=b, c=c: e.tensor_scalar(
                out=yT[:, c, :], in0=cc[b][:, 0:T], scalar1=cs[:, c, 0:1], scalar2=cs[:, c, 31:32],
                op0=ALU.mult, op1=ALU.add),
                reads=[("cc", b), "cs"], writes=[("y", c)])
            for j in range(1, CW):
                S.op("vector", lambda e, b=b, c=c, j=j: e.scalar_tensor_tensor(
                    out=yT[:, c, :], in0=cc[b][:, j:j + T], scalar=cs[:, c, j:j + 1], in1=yT[:, c, :],
                    op0=ALU.mult, op1=ALU.add),
                    reads=[("cc", b), "cs", ("y", c)], writes=[("y", c)])
        nb = 0
        for tb in range(NTB):
            ts = slice(tb * TB, (tb + 1) * TB)
            for c in range(NCC):
                i1 = nb % 3
                nb += 1
                S.op("scalar", lambda e, i1=i1, c=c, ts=ts: e.copy(out=ybf[i1][:], in_=yT[:, c, ts]),
                     reads=[("y", c)], writes=[("ybf", i1)])
                S.op("tensor", lambda e, i1=i1, tb=tb, c=c: e.matmul(
                    ps1[tb][:], ones_bf[:], ybf[i1][:], start=(c == 0), stop=(c == NCC - 1)),
                    reads=[("ybf", i1), "ones"], writes=[("ps1", tb)])
                i2 = nb % 3
                nb += 1
                S.op("scalar", lambda e, i2=i2, c=c, ts=ts: e.activation(
                    out=ybf[i2][:], in_=yT[:, c, ts], func=AF.Square),
                    reads=[("y", c)], writes=[("ybf", i2)])
                S.op("tensor", lambda e, i2=i2, tb=tb, c=c: e.matmul(
                    ps2[tb][:], ones_bf[:], ybf[i2][:], start=(c == 0), stop=(c == NCC - 1)),
                    reads=[("ybf", i2), "ones"], writes=[("ps2", tb)])
            S.op("vector", lambda e, tb=tb, ts=ts: e.tensor_scalar(
                out=mean[:, ts], in0=ps1[tb][:], scalar1=1.0 / 1024, scalar2=None, op0=ALU.mult),
                reads=[("ps1", tb)], writes=[("mean", tb)])
            S.op("vector", lambda e, ts=ts: e.tensor_tensor(
                out=var[:, ts], in0=mean[:, ts], in1=mean[:, ts], op=ALU.mult),
                reads=[("mean", tb)], writes=[("var", tb)])
            S.op("vector", lambda e, tb=tb, ts=ts: e.scalar_tensor_tensor(
                out=var[:, ts], in0=ps2[tb][:], scalar=1.0 / 1024, in1=var[:, ts],
                op0=ALU.mult, op1=ALU.subtract),
                reads=[("ps2", tb), ("var", tb)], writes=[("var", tb)])
            S.op("scalar", lambda e, ts=ts: e.activation(
                out=rstd[:, ts], in_=var[:, ts], func=AF.Sqrt, bias=epsc[:]),
                reads=[("var", tb), "epsc"], writes=[("rstd", tb)])
            S.op("vector", lambda e, ts=ts: e.reciprocal(out=rstd[:, ts], in_=rstd[:, ts]),
                 reads=[("rstd", tb)], writes=[("rstd", tb)])
        for c in range(NCC):
            b = c % 2
            S.op("vector", lambda e, b=b, c=c: e.tensor_tensor(
                out=tmp[b][:], in0=yT[:, c, :], in1=mean[:], op=ALU.subtract),
                reads=[("y", c), ("mean", 0), ("mean", 1)], writes=[("tmp", b)])
            S.op("vector", lambda e, b=b: e.tensor_tensor(
                out=tmp[b][:], in0=tmp[b][:], in1=rstd[:], op=ALU.mult),
                reads=[("tmp", b), ("rstd", 0), ("rstd", 1)], writes=[("tmp", b)])
            S.op("scalar", lambda e, b=b, c=c: e.activation(
                out=mixT[:, NCC + c, :], in_=tmp[b][:], func=AF.Silu,
                scale=cs[:, c, 32:33], bias=cs[:, c, 33:34]),
                reads=[("tmp", b), "cs"], writes=[("mx", NCC + c, 0), ("mx", NCC + c, 1)])
        emit_proj_resid(C, hT, mixT, wo_d, KC, wbuf, py, "mx")
        store_hT(C, hT, h_out)
        S.emit()
    return nc


class Arena:
    def __init__(self, C, nbytes):
        self.t = C.sb("arena", [128, nbytes // 4], F32)
        self.off = 0
        self.cap = nbytes

    def alloc(self, shape, dt):
        esz = 4 if dt == F32 else 2
        n = 1
        for s in shape[1:]:
            n *= s
        nb = (n * esz + 31) // 32 * 32
        assert self.off + nb <= self.cap, (self.off, nb, self.cap, shape)
        ap = self.t[0:shape[0], self.off // 4:(self.off + nb) // 4]
        if dt != F32:
            ap = ap.bitcast(dt)
        ap = ap[:, 0:n]
        if len(shape) == 3:
            ap = ap.rearrange("p (a b) -> p a b", a=shape[1], b=shape[2])
        self.off += nb
        return ap

    def mark(self):
        return self.off

    def release(self, m):
        self.off = m


def sched_barrier(S, new_sems=False):
    tot = dict(S.cnt)
    for eng in S.ENGS:
        wd = S.waited[eng]
        for k, v in tot.items():
            if v > 0 and wd.get(k, 0) < v:
                wd[k] = v
                h = S.semh[k]
                S.ops[eng].append(lambda e, h=h, v=v: e.wait_ge(h, v))
    S.res = {}
    if new_sems:
        S.gen = getattr(S, "gen", 0) + 1
        for e in S.ENGS:
            S.semh[e] = S.stack.enter_context(S.nc.semaphore(f"s_{e}_{S.gen}"))
            S.cnt[e] = 0
            for w in S.ENGS:
                S.waited[w].pop(e, None)


AR_STUB = False


def emit_ar(C, src, dst, reads, writes):
    S = C.S
    if AR_STUB:
        S.dma("sync", dst, src, "arstub", reads=reads, writes=writes)
        return
    S._deps("gpsimd", reads, writes)
    C.nar = getattr(C, "nar", 0) + 1
    key = f"ar{C.nar}"
    S.semh[key] = S.stack.enter_context(S.nc.semaphore(key))
    S.cnt[key] = 1
    h = S.semh[key]
    S.ops["gpsimd"].append(lambda e, h=h: e.collective_compute(
        "AllReduce", ALU.add, replica_groups=[list(range(NCORES))],
        ins=[src.opt()], outs=[dst.opt()]).then_inc(h))
    S._mark((key, 1), reads, writes)


class G_:
    pass


def pbk(i):
    return ("pb", i)


def seg_ffn(C, A, G, fi, gidx):
    S = C.S
    m0 = A.mark()
    xnT = A.alloc([128, KC, T], BF16)
    rstd = A.alloc([128, T], F32)
    sq = [A.alloc([128, TB], BF16) for _ in range(3)]
    pb = G.pb
    bufs = dict(
        wg=[A.alloc([128, KC * 128], BF16) for _ in range(2)],
        wu=[A.alloc([128, KC * 128], BF16) for _ in range(2)],
        wd=[A.alloc([128, FR, D], BF16) for _ in range(2)],
        h1=[A.alloc([128, FR, T], BF16) for _ in range(2)],
        sg=[A.alloc([128, TB], F32) for _ in range(2)],
        pg=pb[0:2], pu=pb[2:4], py=pb[4:8])
    emit_norm(C, G.hT, xnT, lambda kc: G.gains[:, gidx * KC + kc:gidx * KC + kc + 1], G.ones_bf, G.epsc,
              sq, pb[4:6], rstd, lambda tb: ("py", tb))
    emit_ffn(C, G.hT, xnT, G.wg_d[fi], G.wu_d[fi], G.wd_d[fi], bufs, "f")
    A.release(m0)
    sched_barrier(S)


ARENA_BYTES = 143360


def lay_w(w, nm):
    K = w.shape[0]
    return np.ascontiguousarray(w.reshape(K // 128, 128, nm, 128).transpose(2, 1, 0, 3)).reshape(nm, 128, K)


def lay_rhs(w):
    K, N = w.shape
    return np.ascontiguousarray(w.reshape(K // 128, 128, N).transpose(1, 0, 2)).reshape(128, (K // 128) * N)


_M64 = (np.arange(64)[None, :] >= np.arange(64)[:, None]).astype(np.float32)
_M128 = (np.arange(128)[None, :] >= np.arange(128)[:, None]).astype(np.float32)
_ID = np.eye(128, dtype=np.float32)

def seg_ea(C, A, G, gidx, w_d, z_out):
    S = C.S
    pb = G.pb
    m0 = A.mark()
    xnT = A.alloc([128, KC, T], BF16)
    rstd = A.alloc([128, T], F32)
    sq = [A.alloc([128, TB], BF16) for _ in range(3)]
    wbuf = [A.alloc([128, KC * 128], BF16) for _ in range(2)]
    stg = [A.alloc([128, TB], F32) for _ in range(4)]
    py = pb[4:8]
    emit_norm(C, G.hT, xnT, lambda kc: G.gains[:, gidx * KC + kc:gidx * KC + kc + 1], G.ones_bf, G.epsc,
              sq, pb[4:6], rstd, lambda tb: ("py", tb))
    n = 0
    for m in range(NZ):
        wb = m % 2
        S.dma("gpsimd", wbuf[wb], w_d[m], f"w{wb}", writes=[("w", wb)])
        for tb in range(NTB):
            ts = slice(tb * TB, (tb + 1) * TB)
            yb = n % 4
            n += 1
            for k in range(KC):
                S.op("tensor", lambda e, yb=yb, wb=wb, k=k, ts=ts: e.matmul(
                    py[yb][:], wbuf[wb][:, k * 128:(k + 1) * 128], xnT[:, k, ts],
                    start=(k == 0), stop=(k == KC - 1)),
                    reads=[("w", wb), ("xn", k, tb)], writes=[("py", yb)], inc=(k == KC - 1))
            if yb % 2 == 0:
                S.op("vector", lambda e, yb=yb: e.tensor_copy(out=stg[yb], in_=py[yb][:]),
                     reads=[("py", yb)], writes=[("stg", yb)])
            else:
                S.op("scalar", lambda e, yb=yb: e.copy(out=stg[yb], in_=py[yb][:]),
                     reads=[("py", yb)], writes=[("stg", yb)])
            S.dma("sync", z_out[m * 128:(m + 1) * 128, ts], stg[yb], f"zo{yb}",
                  reads=[("stg", yb)], writes=[("zo", yb)])
    A.release(m0)
    sched_barrier(S)


def seg_oa(C, A, G, gidx, u_out):
    S = C.S
    pb = G.pb
    m0 = A.mark()
    xnT = A.alloc([128, KC, T], BF16)
    rstd = A.alloc([128, T], F32)
    sq = [A.alloc([128, TB], BF16) for _ in range(3)]
    emit_norm(C, G.hT, xnT, lambda kc: G.gains[:, gidx * KC + kc:gidx * KC + kc + 1], G.ones_bf, G.epsc,
              sq, pb[4:6], rstd, lambda tb: ("py", tb))
    ov = u_out.rearrange("(kc p) t -> p kc t", p=128)
    for kc in range(KC):
        S.dma("sync", ov[:, kc, :], xnT[:, kc, :], "uo",
              reads=[("xn", kc, tb) for tb in range(NTB)], writes=[("uo", kc)])
    A.release(m0)
    sched_barrier(S)


def seg_od(C, A, G, m_in, wo_d):
    S = C.S
    pb = G.pb
    m0 = A.mark()
    mT = A.alloc([128, KC, T], BF16)
    wbuf = [A.alloc([128, KC * 128], BF16) for _ in range(2)]
    mv = m_in.rearrange("(kc p) t -> p kc t", p=128)
    for kc in range(KC):
        S.dma("gpsimd", mT[:, kc, :], mv[:, kc, :], "min",
              writes=[("mx", kc, tb) for tb in range(NTB)])
    emit_proj_resid(C, G.hT, mT, wo_d, KC, wbuf, pb[4:8], "mx")
    A.release(m0)
    sched_barrier(S)


def seg_eb(C, A, G, on_in, zg_in, zv_in, zs_in, cs_d, wo_d):
    S = C.S
    pb = G.pb
    hT = G.hT
    m0 = A.mark()
    mixT = A.alloc([128, KC, T], BF16)
    yT = A.alloc([128, NCC, T], F32)
    cs = A.alloc([128, NCC, 34], F32)
    sa = [A.alloc([128, HALO + T], F32) for _ in range(2)]
    sb_ = [A.alloc([128, HALO + T], F32) for _ in range(2)]
    cc = [A.alloc([128, HALO + T], F32) for _ in range(2)]
    ybf = [A.alloc([128, TB], BF16) for _ in range(3)]
    mean = A.alloc([128, T], F32)
    var = A.alloc([128, T], F32)
    rstd = A.alloc([128, T], F32)
    tmp = [A.alloc([128, T], F32) for _ in range(2)]
    wbuf = [A.alloc([128, KC * 128], BF16) for _ in range(2)]
    py = pb[4:8]
    ps1 = pb[0:2]
    ps2 = pb[2:4]
    ones_bf = G.ones_bf
    epsc = G.epsc
    S.dma("sync", cs, cs_d, "c0", writes=["cs"])
    for c in range(NCC):
        b = c % 2
        rs = slice(c * 128, (c + 1) * 128)
        S.dma("sync", sa[b][:, 0:T], on_in[rs, :], f"sa{b}", writes=[("sa", b)])
        S.dma("sync", sb_[b][:, 0:T], zg_in[rs, :], f"sb{b}", writes=[("sb", b)])
        S.op("scalar", lambda e, b=b: e.activation(out=sb_[b][:, 0:T], in_=sb_[b][:, 0:T], func=AF.Silu),
             reads=[("sb", b)], writes=[("sb", b)])
        S.op("vector", lambda e, b=b, c=c: e.tensor_tensor(
            out=mixT[:, c, :], in0=sa[b][:, 0:T], in1=sb_[b][:, 0:T], op=ALU.mult),
            reads=[("sa", b), ("sb", b)], writes=[("mx", c, 0), ("mx", c, 1)])
    for c in range(NCC):
        b = c % 2
        rs = slice(c * 128, (c + 1) * 128)
        S.dma("sync", sa[b][:], zv_in[rs, :], f"sa{b}", writes=[("sa", b)])
        S.dma("sync", sb_[b][:], zs_in[rs, :], f"sb{b}", writes=[("sb", b)])
        S.op("scalar", lambda e, b=b: e.activation(out=sb_[b][:], in_=sb_[b][:], func=AF.Sigmoid),
             reads=[("sb", b)], writes=[("sb", b)])
        S.op("vector", lambda e, b=b: e.tensor_tensor(
            out=cc[b][:], in0=sa[b][:], in1=sb_[b][:], op=ALU.mult),
            reads=[("sa", b), ("sb", b)], writes=[("cc", b)])
        S.op("vector", lambda e, b=b, c=c: e.tensor_scalar(
            out=yT[:, c, :], in0=cc[b][:, 0:T], scalar1=cs[:, c, 0:1], scalar2=cs[:, c, 31:32],
            op0=ALU.mult, op1=ALU.add),
            reads=[("cc", b), "cs"], writes=[("y", c)])
        for j in range(1, CW):
            S.op("vector", lambda e, b=b, c=c, j=j: e.scalar_tensor_tensor(
                out=yT[:, c, :], in0=cc[b][:, j:j + T], scalar=cs[:, c, j:j + 1], in1=yT[:, c, :],
                op0=ALU.mult, op1=ALU.add),
                reads=[("cc", b), "cs", ("y", c)], writes=[("y", c)])
    nb = 0
    for tb in range(NTB):
        ts = slice(tb * TB, (tb + 1) * TB)
        for c in range(NCC):
            i1 = nb % 3
            nb += 1
            S.op("scalar", lambda e, i1=i1, c=c, ts=ts: e.copy(out=ybf[i1][:], in_=yT[:, c, ts]),
                 reads=[("y", c)], writes=[("ybf", i1)])
            S.op("tensor", lambda e, i1=i1, tb=tb, c=c: e.matmul(
                ps1[tb][:], ones_bf[:], ybf[i1][:], start=(c == 0), stop=(c == NCC - 1)),
                reads=[("ybf", i1), "ones"], writes=[("ps1", tb)])
            i2 = nb % 3
            nb += 1
            S.op("scalar", lambda e, i2=i2, c=c, ts=ts: e.activation(
                out=ybf[i2][:], in_=yT[:, c, ts], func=AF.Square),
                reads=[("y", c)], writes=[("ybf", i2)])
            S.op("tensor", lambda e, i2=i2, tb=tb, c=c: e.matmul(
                ps2[tb][:], ones_bf[:], ybf[i2][:], start=(c == 0), stop=(c == NCC - 1)),
                reads=[("ybf", i2), "ones"], writes=[("ps2", tb)])
        S.op("vector", lambda e, tb=tb, ts=ts: e.tensor_scalar(
            out=mean[:, ts], in0=ps1[tb][:], scalar1=1.0 / 1024, scalar2=None, op0=ALU.mult),
            reads=[("ps1", tb)], writes=[("mean", tb)])
        S.op("vector", lambda e, ts=ts: e.tensor_tensor(
            out=var[:, ts], in0=mean[:, ts], in1=mean[:, ts], op=ALU.mult),
            reads=[("mean", tb)], writes=[("var", tb)])
        S.op("vector", lambda e, tb=tb, ts=ts: e.scalar_tensor_tensor(
            out=var[:, ts], in0=ps2[tb][:], scalar=1.0 / 1024, in1=var[:, ts],
            op0=ALU.mult, op1=ALU.subtract),
            reads=[("ps2", tb), ("var", tb)], writes=[("var", tb)])
        S.op("scalar", lambda e, ts=ts: e.activation(
            out=rstd[:, ts], in_=var[:, ts], func=AF.Sqrt, bias=epsc[:]),
            reads=[("var", tb), "epsc"], writes=[("rstd", tb)])
        S.op("vector", lambda e, ts=ts: e.reciprocal(out=rstd[:, ts], in_=rstd[:, ts]),
             reads=[("rstd", tb)], writes=[("rstd", tb)])
    for c in range(NCC):
        b = c % 2
        S.op("vector", lambda e, b=b, c=c: e.tensor_tensor(
            out=tmp[b][:], in0=yT[:, c, :], in1=mean[:], op=ALU.subtract),
            reads=[("y", c), ("mean", 0), ("mean", 1)], writes=[("tmp", b)])
        S.op("vector", lambda e, b=b: e.tensor_tensor(
            out=tmp[b][:], in0=tmp[b][:], in1=rstd[:], op=ALU.mult),
            reads=[("tmp", b), ("rstd", 0), ("rstd", 1)], writes=[("tmp", b)])
        S.op("scalar", lambda e, b=b, c=c: e.activation(
            out=mixT[:, NCC + c, :], in_=tmp[b][:], func=AF.Silu,
            scale=cs[:, c, 32:33], bias=cs[:, c, 33:34]),
            reads=[("tmp", b), "cs"], writes=[("mx", NCC + c, 0), ("mx", NCC + c, 1)])
    emit_proj_resid(C, hT, mixT, wo_d, KC, wbuf, py, "mx")
    A.release(m0)
    sched_barrier(S)


def build_comp(plan):
    nc = new_prog()
    G = G_()

    def din(name, shape, dt=F32):
        return nc.dram_tensor(name, list(shape), dt, kind="ExternalInput").ap()

    def dout(name, shape, dt=F32):
        return nc.dram_tensor(name, list(shape), dt, kind="ExternalOutput").ap()

    x_in = din("h_in", [D, T])
    h_out = dout("h_out", [D, T])
    gains_d = din("gains", [128, 4 * KC])
    nffn = sum(1 for p in plan if p == "ffn")
    if nffn:
        G.wg_d = din("wg", [nffn, NF, 128, KC * 128])
        G.wu_d = din("wu", [nffn, NF, 128, KC * 128])
        G.wd_d = din("wd", [nffn, NF, 128, D])
    if "ea" in plan:
        ea_w = din("ea_w", [NZ, 128, KC * 128])
        z_out = dout("z_out", [NZ * 128, T])
    if "eb" in plan:
        on_in = din("on_in", [1024, T])
        zg_in = din("zg_in", [1024, T])
        zv_in = din("zv_in", [1024, HALO + T])
        zs_in = din("zs_in", [1024, HALO + T])
        cs_d = din("cs", [128, NCC, 34])
        wo_d = din("wo", [KC, 128, KC * 128])
    if "od" in plan:
        m_in = din("m_in", [D, T])
        wo_d = din("wo", [KC, 128, KC * 128])
    if "oa" in plan:
        u_out = dout("u_out", [D, T], BF16)
    with contextlib.ExitStack() as stack:
        C = Ctx(nc, stack)
        S = C.S
        G.hT = C.sb("hT", [128, KC, T], F32)
        G.gains = C.sb("gains_sb", [128, 4 * KC], F32)
        G.ones_bf = C.sb("ones", [128, 128], BF16)
        G.epsc = C.sb("epsc", [128, 1], F32)
        G.pb = [C.ps(f"pb{i}", [128, TB]) for i in range(8)]
        A = Arena(C, ARENA_BYTES)
        S.dma("sync", G.gains[:], gains_d, "c0", writes=["consts"])
        S.op("vector", lambda e: e.memset(G.ones_bf[:], 1.0), writes=["ones"])
        S.op("vector", lambda e: e.memset(G.epsc[:], EPS), writes=["epsc"])
        load_hT(C, G.hT, x_in)
        sched_barrier(S)
        gi = 0
        fi = 0
        for p in plan:
            if p == "ffn":
                seg_ffn(C, A, G, fi, gi)
                fi += 1
                gi += 1
            elif p == "ea":
                seg_ea(C, A, G, gi, ea_w, z_out)
                gi += 1
            elif p == "oa":
                seg_oa(C, A, G, gi, u_out)
                gi += 1
            elif p == "eb":
                seg_eb(C, A, G, on_in, zg_in, zv_in, zs_in, cs_d, wo_d)
            elif p == "od":
                seg_od(C, A, G, m_in, wo_d)
        store_hT(C, G.hT, h_out)
        S.emit()
    return nc


_PROGS = {}


def _prog(key, fn):
    if key not in _PROGS:
        _PROGS[key] = fn()
    return _PROGS[key]


def _run(nc, in_maps):
    return run_bass_kernel_spmd(nc, in_maps, core_ids=list(range(len(in_maps)))).results


def _c(a):
    return np.ascontiguousarray(a)


def lay_g4(gs):
    g = np.zeros((128, 4 * KC), np.float32)
    for i, gv in enumerate(gs):
        g[:, i * KC:(i + 1) * KC] = gv.reshape(KC, 128).T
    return g


def _shards(aT):
    return [_c(aT[:, c * T:(c + 1) * T]) for c in range(NCORES)]


def ffn_stack(ws):
    return dict(wg=np.stack([lay_w(g, NF) for g, u, d in ws]),
                wu=np.stack([lay_w(u, NF) for g, u, d in ws]),
                wd=np.stack([_c(d.reshape(NF, 128, D)) for g, u, d in ws]))


def ea_weights(w_in):
    wpad = np.zeros((D, NZ * 128), np.float32)
    wpad[:, :w_in.shape[1]] = w_in
    return lay_w(wpad, NZ)


def eg_stage(zT, w_gk2, b_gk, out_g):
    nc = _prog("EG", build_EG)
    ims = []
    for h in range(4):
        sc = np.zeros((128, 4), np.float32)
        sc[:, 0] = -b_gk[h * 128:(h + 1) * 128]
        sc[:, 1] = out_g[:128]
        sc[:, 2] = out_g[128:]
        ims.append(dict(qT=_c(zT[h * 128:(h + 1) * 128]), kT=_c(zT[512 + h * 128:512 + (h + 1) * 128]),
                        v=_c(zT[1024 + h * 256:1024 + (h + 1) * 256].T), gkT=_c(zT[3072:3088]),
                        w2=_c(w_gk2[:, h * 128:(h + 1) * 128]), sc=sc, m64=_M64, ident=_ID))
    res = _run(nc, ims)
    return np.concatenate([r["o_out"] for r in res], axis=0)


def eb_inputs(zT, onT, conv_w, conv_b, ln_g, ln_b, w_out):
    zv = np.zeros((1024, HALO + TA), np.float32)
    zv[:, HALO:] = zT[3088:4112]
    zs = np.zeros((1024, HALO + TA), np.float32)
    zs[:, HALO:] = zT[4112:5136]
    cs = np.zeros((128, NCC, 34), np.float32)
    cs[:, :, 0:31] = conv_w.T.reshape(NCC, 128, 31).transpose(1, 0, 2)
    cs[:, :, 31] = conv_b.reshape(NCC, 128).T
    cs[:, :, 32] = ln_g.reshape(NCC, 128).T
    cs[:, :, 33] = ln_b.reshape(NCC, 128).T
    wo = lay_w(w_out, KC)
    out = []
    for c in range(NCORES):
        sl = slice(c * T, (c + 1) * T)
        out.append(dict(on_in=_c(onT[:, sl]), zg_in=_c(zT[2048:3072, sl]),
                        zv_in=_c(zv[:, c * T:c * T + HALO + T]), zs_in=_c(zs[:, c * T:c * T + HALO + T]),
                        cs=cs, wo=wo))
    return out


def oc_weights(w_in, b_f, q_g, k_g, heads):
    FD = 2048
    wqk = np.stack([np.stack([lay_rhs(w_in[:, 0 * FD + h * 128:0 * FD + (h + 1) * 128]),
                              lay_rhs(w_in[:, 1 * FD + h * 128:1 * FD + (h + 1) * 128])]) for h in heads])
    wvo = np.stack([np.stack([lay_rhs(w_in[:, 2 * FD + h * 128:2 * FD + (h + 1) * 128]),
                              lay_rhs(w_in[:, 3 * FD + h * 128:3 * FD + (h + 1) * 128])]) for h in heads])
    wf = np.stack([_c(w_in[:, 4 * FD + h].reshape(KC, 128).T) for h in heads])
    sc = np.zeros((2, 128, 4), np.float32)
    for i, h in enumerate(heads):
        sc[i, :, 0] = q_g
        sc[i, :, 1] = k_g
        sc[i, :, 2] = -b_f[h]
    return dict(wqk=_c(wqk.astype(np.float32)), wvo=_c(wvo.astype(np.float32)),
                wf=_c(wf.astype(np.float32)), sc=sc)


def oc_stage(u_all, w_in, b_f, q_g, k_g):
    nc = _prog("OC", build_OC)
    ims = [dict(u_all=u_all, mask=_M128, **oc_weights(w_in, b_f, q_g, k_g, [2 * c, 2 * c + 1]))
           for c in range(NCORES)]
    res = _run(nc, ims)
    return _c(np.concatenate([r["o_out"] for r in res], axis=1).T)


def kernel(x, ffn1_norm, ffn1_gate, ffn1_up, ffn1_down, mix_norm, ffn2_norm, ffn2_gate, ffn2_up,
           ffn2_down, ab_w_in, gla_w_gk2, gla_b_gk, gla_out_norm, conv_w, conv_b, conv_ln_g,
           conv_ln_b, ab_w_out, fox_w_in, fox_b_f, fox_q_norm, fox_k_norm, fox_w_out):
    f = lambda a: np.asarray(a, dtype=np.float32)
    n1, n2, nm = f(ffn1_norm), f(ffn2_norm), f(mix_norm)
    F1 = [(f(ffn1_gate)[l], f(ffn1_up)[l], f(ffn1_down)[l]) for l in range(4)]
    F2 = [(f(ffn2_gate)[l], f(ffn2_up)[l], f(ffn2_down)[l]) for l in range(4)]
    hT = _c(f(x)[0].T)
    nc = _prog(("ffn", "ea"), lambda: build_comp(["ffn", "ea"]))
    hs = _shards(hT)
    shared = dict(gains=lay_g4([n1[0], nm[0]]), ea_w=ea_weights(f(ab_w_in)[0]), **ffn_stack([F1[0]]))
    res = _run(nc, [dict(h_in=hs[c], **shared) for c in range(NCORES)])
    hT = np.concatenate([r["h_out"] for r in res], axis=1)
    zT = np.concatenate([r["z_out"] for r in res], axis=1)
    for e in range(2):
        l = 2 * e
        onT = eg_stage(zT, f(gla_w_gk2)[e], f(gla_b_gk)[e], f(gla_out_norm)[e])
        nc = _prog(("eb", "ffn", "ffn", "oa"), lambda: build_comp(["eb", "ffn", "ffn", "oa"]))
        hs = _shards(hT)
        ebi = eb_inputs(zT, onT, f(conv_w)[e], f(conv_b)[e], f(conv_ln_g)[e], f(conv_ln_b)[e], f(ab_w_out)[e])
        shared = dict(gains=lay_g4([n2[l], n1[l + 1], nm[l + 1]]), **ffn_stack([F2[l], F1[l + 1]]))
        res = _run(nc, [dict(h_in=hs[c], **ebi[c], **shared) for c in range(NCORES)])
        hT = np.concatenate([r["h_out"] for r in res], axis=1)
        u_all = _c(np.concatenate([r["u_out"] for r in res], axis=1))
        oT = oc_stage(u_all, f(fox_w_in)[e], f(fox_b_f)[e], f(fox_q_norm)[e], f(fox_k_norm)[e])
        hs = _shards(hT)
        ms = _shards(oT)
        wo = lay_w(f(fox_w_out)[e], KC)
        if e == 0:
            nc = _prog(("od", "ffn", "ffn", "ea"), lambda: build_comp(["od", "ffn", "ffn", "ea"]))
            shared = dict(gains=lay_g4([n2[1], n1[2], nm[2]]), ea_w=ea_weights(f(ab_w_in)[1]), wo=wo,
                          **ffn_stack([F2[1], F1[2]]))
            res = _run(nc, [dict(h_in=hs[c], m_in=ms[c], **shared) for c in range(NCORES)])
            hT = np.concatenate([r["h_out"] for r in res], axis=1)
            zT = np.concatenate([r["z_out"] for r in res], axis=1)
        else:
            nc = _prog(("od", "ffn"), lambda: build_comp(["od", "ffn"]))
            shared = dict(gains=lay_g4([n2[3]]), wo=wo, **ffn_stack([F2[3]]))
            res = _run(nc, [dict(h_in=hs[c], m_in=ms[c], **shared) for c in range(NCORES)])
            hT = np.concatenate([r["h_out"] for r in res], axis=1)
    return _c(hT.T)[None].astype(np.float32)
```
